# Optimizing a Trainium2 kernel written in Bass

```python
import jax
import jax.numpy as jnp
from jax import lax
import numpy as np

D_MODEL = 1024
BATCH = 8
SEQ = 2048
DEPTH = 4

CHUNK = 64
Q_BLOCK = 128
N_HEADS = 4
HEAD_DIM = 128
BRANCH_W = N_HEADS * HEAD_DIM
RET_DK = HEAD_DIM
RET_DV = HEAD_DIM
GLA_DK = HEAD_DIM // 2
GLA_DV = HEAD_DIM
GLA_LOWRANK = 16
GLA_TAU = 16.0
FOX_D = HEAD_DIM
N_BRANCH = 3
D_FF = 2816
CONV_W = 3
ROPE_BASE = 10000.0
EPS = 1e-6

IN_SPLITS = (
    N_HEADS * RET_DK, N_HEADS * RET_DK, N_HEADS * RET_DV, N_HEADS * RET_DV,
    N_HEADS * GLA_DK, N_HEADS * GLA_DK, N_HEADS * GLA_DV, GLA_LOWRANK, N_HEADS * GLA_DV,
    N_HEADS * FOX_D, N_HEADS * FOX_D, N_HEADS * FOX_D, N_HEADS,
)
IN_W = sum(IN_SPLITS)

kernel_name = 'hybrid_ret_gla_fox_adaln_convffn'

F32 = jnp.float32


def rms_norm(x, g):
    xf = x.astype(F32)
    y = xf * lax.rsqrt(jnp.mean(xf * xf, axis=-1, keepdims=True) + EPS)
    return (y * g).astype(x.dtype)


def head_group_norm(x, g):
    xf = x.astype(F32)
    mu = jnp.mean(xf, axis=-1, keepdims=True)
    xc = xf - mu
    return xc * lax.rsqrt(jnp.mean(xc * xc, axis=-1, keepdims=True) + EPS) * g


def modulate(h, shift, scale):
    return h * (1.0 + scale[:, None, :]) + shift[:, None, :]


def rotary(x, pos):
    half = x.shape[-1] // 2
    inv_freq = ROPE_BASE ** (-jnp.arange(half, dtype=F32) / half)
    ang = pos[:, None] * inv_freq[None, :]
    cos = jnp.cos(ang)[None, :, None, :]
    sin = jnp.sin(ang)[None, :, None, :]
    x1, x2 = x[..., :half], x[..., half:]
    return jnp.concatenate([x1 * cos - x2 * sin, x1 * sin + x2 * cos], axis=-1)


def retention(q, k, v):
    b, s, h, dk = q.shape
    dv = v.shape[-1]
    n = s // CHUNK
    pos = jnp.arange(s, dtype=F32)
    q = rotary(q.astype(F32), pos)
    k = rotary(k.astype(F32), pos) * dk ** -0.5
    v = v.astype(F32)
    log_g = jnp.log1p(-jnp.exp2(-5.0 - jnp.arange(h, dtype=F32)))
    idx = jnp.arange(CHUNK, dtype=F32)
    d_intra = jnp.exp(jnp.abs(idx[:, None] - idx[None, :])[None] * log_g[:, None, None])
    qc = q.reshape(b, n, CHUNK, h, dk)
    kc = k.reshape(b, n, CHUNK, h, dk)
    vc = v.reshape(b, n, CHUNK, h, dv)
    scores = jnp.einsum('bnihd,bnjhd->bnhij', qc, kc) * d_intra
    o_intra = jnp.einsum('bnhij,bnjhe->bnihe', scores, vc)
    k_w = jnp.exp((CHUNK - 1.0 - idx)[:, None] * log_g[None, :])
    kv = jnp.einsum('bnjhd,bnjhe->nbhde', kc * k_w[:, :, None], vc)
    g_chunk = jnp.exp(CHUNK * log_g)[None, :, None, None]

    def step(r, kv_n):
        return g_chunk * r + kv_n, r

    _, r_prev = lax.scan(step, jnp.zeros((b, h, dk, dv), F32), kv)
    q_w = jnp.exp((idx + 1.0)[:, None] * log_g[None, :])
    o_cross = jnp.einsum('bnihd,nbhde->bnihe', qc * q_w[:, :, None], r_prev)
    return (o_intra + o_cross).reshape(b, s, h, dv)


def gla(q, k, v, log_a):
    b, s, h, dk = q.shape
    dv = v.shape[-1]
    n = s // CHUNK
    qc = (q.astype(F32) * dk ** -0.5).reshape(b, n, CHUNK, h, dk)
    kc = k.astype(F32).reshape(b, n, CHUNK, h, dk)
    vc = v.astype(F32).reshape(b, n, CHUNK, h, dv)
    la = log_a.astype(F32).reshape(b, n, CHUNK, h, dk)
    b_cum = jnp.cumsum(la, axis=2)
    b_end = b_cum[:, :, -1:]
    kv = jnp.einsum('bnjhd,bnjhe->nbhde', kc * jnp.exp(b_end - b_cum), vc)
    a = jnp.exp(b_end[:, :, 0]).transpose(1, 0, 2, 3)

    def step(st, inp):
        a_n, kv_n = inp
        st = a_n[..., None] * st + kv_n
        return st, st

    _, s_all = lax.scan(step, jnp.zeros((b, h, dk, dv), F32), (a, kv))
    o = jnp.einsum('bnihd,nbhde->bnihe', qc, s_all)
    return o.reshape(b, s, h, dv)


def forgetting_attention(q, k, v, f_logit):
    b, s, h, d = q.shape
    log_f = jax.nn.log_sigmoid(f_logit.astype(F32))
    cum = jnp.cumsum(log_f, axis=1).transpose(0, 2, 1)
    outs = []
    for blk in range(s // Q_BLOCK):
        q0 = blk * Q_BLOCK
        q1 = q0 + Q_BLOCK
        logits = jnp.einsum('bihd,bjhd->bhij', q[:, q0:q1], k[:, :q1]).astype(F32) * d ** -0.5
        logits = logits + cum[:, :, q0:q1, None] - cum[:, :, None, :q1]
        mask = jnp.arange(q0, q1)[:, None] >= jnp.arange(q1)[None, :]
        p = jax.nn.softmax(jnp.where(mask, logits, -jnp.inf), axis=-1).astype(v.dtype)
        outs.append(jnp.einsum('bhij,bjhe->bihe', p, v[:, :q1]))
    return jnp.concatenate(outs, axis=1)


def causal_dwconv(u, w, bias):
    s = u.shape[1]
    up = jnp.pad(u, ((0, 0), (CONV_W - 1, 0), (0, 0)))
    out = bias
    for j in range(CONV_W):
        out = out + w[j] * up[:, j:j + s]
    return out


def hybrid_layer(x, c_act, norm1_g, norm2_g, w_ada, b_ada, w_in, w_gla_a2, b_gla_a, b_fox_f,
                 ret_norm_g, gla_norm_g, q_norm_g, k_norm_g, w_br, w_mg, b_mg, w_o,
                 w_up, w_conv, b_conv, w_down):
    b, s = x.shape[0], x.shape[1]
    mod = c_act @ w_ada + b_ada
    shift1, scale1, gate1, shift2, scale2, gate2 = jnp.split(mod, 6, axis=-1)

    h = modulate(rms_norm(x, norm1_g), shift1, scale1)
    split_points = np.cumsum(IN_SPLITS)[:-1].tolist()
    (rq, rk, rv, rg, gq, gk, gv, glr, gg, fq, fk, fv, ff) = jnp.split(h @ w_in, split_points, axis=-1)

    def heads(t, d):
        return t.reshape(b, s, N_HEADS, d)

    ret = retention(heads(rq, RET_DK), heads(rk, RET_DK), heads(rv, RET_DV))
    ret = head_group_norm(ret, ret_norm_g.reshape(N_HEADS, RET_DV)).reshape(b, s, BRANCH_W)
    ret = jax.nn.silu(rg) * ret

    log_a = jax.nn.log_sigmoid((glr @ w_gla_a2 + b_gla_a).astype(F32)) / GLA_TAU
    gla_o = gla(heads(gq, GLA_DK), heads(gk, GLA_DK), heads(gv, GLA_DV), heads(log_a, GLA_DK))
    gla_o = rms_norm(gla_o, gla_norm_g).reshape(b, s, BRANCH_W)
    gla_o = jax.nn.silu(gg) * gla_o

    fox_o = forgetting_attention(rms_norm(heads(fq, FOX_D), q_norm_g),
                                 rms_norm(heads(fk, FOX_D), k_norm_g),
                                 heads(fv, FOX_D), ff + b_fox_f).reshape(b, s, BRANCH_W)

    branches = jnp.stack([ret, gla_o, fox_o], axis=2).astype(h.dtype)
    y_br = jnp.einsum('bsnw,nwd->bsnd', branches, w_br)
    gates = jax.nn.sigmoid(h @ w_mg + b_mg).reshape(b, s, N_BRANCH, D_MODEL)
    mixed = jnp.sum(gates * y_br, axis=2) @ w_o
    x = x + (gate1[:, None, :] * mixed).astype(x.dtype)

    h2 = modulate(rms_norm(x, norm2_g), shift2, scale2)
    u, g = jnp.split(h2 @ w_up, 2, axis=-1)
    u = causal_dwconv(u, w_conv, b_conv)
    y = (jax.nn.silu(u) * g) @ w_down
    x = x + (gate2[:, None, :] * y).astype(x.dtype)
    return x


def setup_inputs(seed: int = 0) -> dict:
    key = jax.random.key(seed)
    ks = jax.random.split(key, 24)

    def nrm(k, shape, scale):
        return jax.random.normal(k, shape, F32) * scale

    L, D = DEPTH, D_MODEL
    return {
        'x': nrm(ks[0], (BATCH, SEQ, D), 1.0),
        'c': nrm(ks[1], (BATCH, D), 1.0),
        'norm1_g': 1.0 + nrm(ks[2], (L, D), 0.02),
        'norm2_g': 1.0 + nrm(ks[3], (L, D), 0.02),
        'w_ada': nrm(ks[4], (L, D, 6 * D), 0.5 * D ** -0.5),
        'b_ada': nrm(ks[5], (L, 6 * D), 0.02),
        'w_in': nrm(ks[6], (L, D, IN_W), D ** -0.5),
        'w_gla_a2': nrm(ks[7], (L, GLA_LOWRANK, N_HEADS * GLA_DK), GLA_LOWRANK ** -0.5),
        'b_gla_a': nrm(ks[8], (L, N_HEADS * GLA_DK), 0.1),
        'b_fox_f': 1.0 + nrm(ks[9], (L, N_HEADS), 0.1),
        'ret_norm_g': 1.0 + nrm(ks[10], (L, N_HEADS * RET_DV), 0.02),
        'gla_norm_g': 1.0 + nrm(ks[11], (L, GLA_DV), 0.02),
        'q_norm_g': 1.0 + nrm(ks[12], (L, FOX_D), 0.02),
        'k_norm_g': 1.0 + nrm(ks[13], (L, FOX_D), 0.02),
        'w_br': nrm(ks[14], (L, N_BRANCH, BRANCH_W, D), BRANCH_W ** -0.5),
        'w_mg': nrm(ks[15], (L, D, N_BRANCH * D), D ** -0.5),
        'b_mg': nrm(ks[16], (L, N_BRANCH * D), 0.02),
        'w_o': nrm(ks[17], (L, D, D), D ** -0.5),
        'w_up': nrm(ks[18], (L, D, 2 * D_FF), D ** -0.5),
        'w_conv': nrm(ks[19], (L, CONV_W, D_FF), CONV_W ** -0.5),
        'b_conv': nrm(ks[20], (L, D_FF), 0.02),
        'w_down': nrm(ks[21], (L, D_FF, D), D_FF ** -0.5),
    }


def reference(x, c, norm1_g, norm2_g, w_ada, b_ada, w_in, w_gla_a2, b_gla_a, b_fox_f,
              ret_norm_g, gla_norm_g, q_norm_g, k_norm_g, w_br, w_mg, b_mg, w_o,
              w_up, w_conv, b_conv, w_down):
    c_act = jax.nn.silu(c)
    for l in range(DEPTH):
        x = hybrid_layer(x, c_act, norm1_g[l], norm2_g[l], w_ada[l], b_ada[l], w_in[l],
                         w_gla_a2[l], b_gla_a[l], b_fox_f[l], ret_norm_g[l], gla_norm_g[l],
                         q_norm_g[l], k_norm_g[l], w_br[l], w_mg[l], b_mg[l], w_o[l],
                         w_up[l], w_conv[l], b_conv[l], w_down[l])
    return x
```

```python
import contextlib
import numpy as np
import concourse.bass as bass
import concourse.mybir as mybir
from concourse.bass_utils import run_bass_kernel_spmd

F32 = mybir.dt.float32
BF16 = mybir.dt.bfloat16
AF = mybir.ActivationFunctionType
ALU = mybir.AluOpType

PE, ACT, DVE, POOL, SP = "pe", "act", "dve", "pool", "sp"
ENGS = [PE, ACT, DVE, POOL, SP]

D = 1024
S = 2048
NB = 8
DEPTH = 4
KC = 8
TB = 512
NTB = S // TB
NT = S // 128
D_FF = 2816
NFC = D_FF // 128
IN_W = 5140
EPS = 1e-6
WBUF_ELEMS = 4096
N_WBUF = 4
N_F32 = 8
N_BF = 14
NSM = 184
SM_N1, SM_N2, SM_BADA, SM_BMG, SM_BCONV, SM_WCONV, SM_RETG, SM_GLAG, SM_QG, SM_KG = 0, 8, 16, 64, 88, 110, 176, 180, 181, 182
DBG_TB = 0
NBC = 260


class Res:
    __slots__ = ("name", "w", "rs", "dma_rs", "excl")

    def __init__(self, name, excl=False):
        self.name = name
        self.w = None
        self.rs = {}
        self.dma_rs = []
        self.excl = excl


class Op:
    __slots__ = ("eng", "fn", "deps", "pos", "signal", "sigval", "is_dma", "sem", "semval")

    def __init__(self, eng, fn, is_dma=False):
        self.eng = eng
        self.fn = fn
        self.deps = ()
        self.pos = 0
        self.signal = False
        self.sigval = 0
        self.is_dma = is_dma
        self.sem = None
        self.semval = 0


class Sched:
    def __init__(self, nc):
        self.nc = nc
        self.q = {e: [] for e in ENGS}
        self.dma_sems = {}

    def _track(self, op, reads, writes):
        deps = {}
        for r in reads:
            if r.w is not None:
                deps[id(r.w)] = r.w
            if r.excl:
                for e_, o in r.rs.items():
                    if e_ != op.eng:
                        deps[id(o)] = o
        for w in writes:
            if w.w is not None:
                deps[id(w.w)] = w.w
            for o in w.rs.values():
                deps[id(o)] = o
            for o in w.dma_rs:
                deps[id(o)] = o
        deps.pop(id(op), None)
        op.deps = tuple(deps.values())
        for r in reads:
            if op.is_dma:
                r.dma_rs.append(op)
            else:
                r.rs[op.eng] = op
        for w in writes:
            w.w = op
            w.rs = {}
            w.dma_rs = []

    muted = False

    def add(self, eng, fn, reads=(), writes=()):
        if self.muted:
            return None
        op = Op(eng, fn)
        op.pos = len(self.q[eng])
        self.q[eng].append(op)
        self._track(op, reads, writes)
        return op

    def dma(self, queue, semkey, fn, reads=(), writes=()):
        if self.muted:
            return None
        op = Op(queue, fn, is_dma=True)
        op.pos = len(self.q[queue])
        self.q[queue].append(op)
        st = self.dma_sems.setdefault(semkey, [0, queue])
        assert st[1] == queue
        st[0] += 16
        op.sem = semkey
        op.semval = st[0]
        self._track(op, reads, writes)
        return op

    def emit(self, final_wait_ops=()):
        nc = self.nc
        for e in ENGS:
            for op in self.q[e]:
                for d in op.deps:
                    if d.is_dma:
                        continue
                    if d.eng != op.eng:
                        d.signal = True
                    elif op.eng != PE:
                        d.signal = True
        for e in ENGS:
            c = 0
            for op in self.q[e]:
                if op.signal and not op.is_dma:
                    c += 1
                    op.sigval = c
        with contextlib.ExitStack() as st:
            esem = {e: st.enter_context(nc.semaphore("s_" + e)) for e in ENGS}
            dsem = {k: st.enter_context(nc.semaphore("d_%s" % (k,))) for k in self.dma_sems}
            block = st.enter_context(nc.Block())
            sched = self

            def run(e, eng):
                waited = {}
                for op in sched.q[e]:
                    for d in op.deps:
                        if d.is_dma:
                            key = ("d", d.sem)
                            val = d.semval
                            sem = dsem[d.sem]
                        else:
                            if d.eng == e and e == PE:
                                continue
                            key = ("e", d.eng)
                            val = d.sigval
                            sem = esem[d.eng]
                        if waited.get(key, 0) >= val:
                            continue
                        waited[key] = val
                        eng.wait_ge(sem, val)
                    ins = op.fn(eng)
                    if op.is_dma:
                        ins.then_inc(dsem[op.sem], 16)
                    elif op.signal:
                        ins.then_inc(esem[e], 1)
                if e == SP:
                    for d in final_wait_ops:
                        if d is not None:
                            eng.wait_ge(dsem[d.sem], d.semval)

            @block.tensor
            def _(eng):
                run(PE, eng)

            @block.scalar
            def _(eng):
                run(ACT, eng)

            @block.vector
            def _(eng):
                run(DVE, eng)

            @block.gpsimd
            def _(eng):
                run(POOL, eng)

            @block.sync
            def _(eng):
                run(SP, eng)


class Slot:
    __slots__ = ("t", "r", "idx")

    def __init__(self, t, r, idx):
        self.t = t
        self.r = r
        self.idx = idx


class Pool_:
    def __init__(self, slots, name):
        self.free_list = list(slots)
        self.name = name

    def alloc(self):
        assert self.free_list, "pool %s exhausted" % self.name
        return self.free_list.pop(0)

    def free(self, s):
        self.free_list.append(s)


def MM(out, lhsT, rhs, start=True, stop=True):
    return lambda e: e.matmul(out, lhsT, rhs, start=start, stop=stop)


def TR(out, in_, ident):
    return lambda e: e.transpose(out, in_, ident)


def ACTF(out, in_, func, bias=None, scale=None):
    kw = {}
    if bias is not None:
        kw["bias"] = bias
    if scale is not None:
        kw["scale"] = scale
    return lambda e: e.activation(out, in_, func, **kw)


def ACP(out, in_):
    return lambda e: e.copy(out, in_)


def TT(out, a, b, op):
    return lambda e: e.tensor_tensor(out, a, b, op)


def TS(out, a, s1, s2, op0, op1=None):
    if op1 is None:
        return lambda e: e.tensor_scalar(out, a, s1, s2, op0)
    return lambda e: e.tensor_scalar(out, a, s1, s2, op0, op1)


def STT(out, in0, scalar, in1, op0, op1):
    return lambda e: e.scalar_tensor_tensor(out, in0, scalar, in1, op0, op1)


def CP(out, in_):
    return lambda e: e.tensor_copy(out, in_)


def MEMSET(ap, v):
    return lambda e: e.memset(ap, v)


def DMA(out, in_):
    return lambda e: e.dma_start(out=out, in_=in_)


def weight_stream_layout():
    head = [("ada%d" % i, 4096) for i in range(12)]
    tb = [("ret%d" % h, 4096) for h in range(4)]
    tb += [("gla%d" % h, 8 * 384) for h in range(4)]
    tb += [("fox%d" % h, 8 * 384) for h in range(4)]
    for n in range(8):
        tb += [("mg%d" % n, 3072), ("br%d" % n, 1536)]
    tb += [("wo0", 4096), ("wo1", 4096)]
    for i in range(6):
        wdt = 512 if i < 5 else 256
        tb += [("up_u%d" % i, 8 * wdt), ("up_g%d" % i, 8 * wdt)]
    for ch in range(2):
        for g, nfc in enumerate((8, 8, 6)):
            tb.append(("dn%d_%d" % (ch, g), nfc * 512))
    return head, tb


def host_pack_weights(w_ada, w_in, w_br, w_mg, w_o, w_up, w_down):
    L = w_in.shape[0]
    head, tb = weight_stream_layout()
    names = [n for n, _ in head] + [n for n, _ in tb]
    sizes = dict(head + tb)
    offs = {}
    o = 0
    for n in names:
        offs[n] = o
        o += sizes[n]
    total = o
    out = np.empty((L, 128, total), np.float32)

    def kcp(w2d):
        n = w2d.shape[1]
        return w2d.reshape(8, 128, n).transpose(1, 0, 2)

    for l in range(L):
        dst = out[l]

        def put(name, arr):
            a = arr.reshape(128, -1)
            assert a.shape[1] == sizes[name], (name, a.shape, sizes[name])
            dst[:, offs[name]:offs[name] + sizes[name]] = a

        for i in range(12):
            put("ada%d" % i, kcp(w_ada[l][:, i * 512:(i + 1) * 512]))
        wi = w_in[l]
        for h in range(4):
            cols = np.concatenate([np.arange(0 + 128 * h, 128 * h + 128), np.arange(512 + 128 * h, 512 + 128 * h + 128),
                                   np.arange(1024 + 128 * h, 1024 + 128 * h + 128), np.arange(1536 + 128 * h, 1536 + 128 * h + 128)])
            put("ret%d" % h, kcp(wi[:, cols]))
            cols = np.concatenate([np.arange(2048 + 64 * h, 2048 + 64 * h + 64), np.arange(2304 + 64 * h, 2304 + 64 * h + 64),
                                   np.arange(2560 + 128 * h, 2560 + 128 * h + 128), np.arange(3088 + 128 * h, 3088 + 128 * h + 128)])
            put("gla%d" % h, kcp(wi[:, cols]))
            cols = np.concatenate([np.arange(3600 + 128 * h, 3600 + 128 * h + 128), np.arange(4112 + 128 * h, 4112 + 128 * h + 128),
                                   np.arange(4624 + 128 * h, 4624 + 128 * h + 128)])
            put("fox%d" % h, kcp(wi[:, cols]))
        mg = w_mg[l].reshape(8, 128, 3, 8, 128)
        br = w_br[l].reshape(3, 4, 128, 8, 128)
        for n in range(8):
            put("mg%d" % n, mg[:, :, :, n, :].transpose(1, 0, 2, 3))
            put("br%d" % n, br[:, :, :, n, :].transpose(2, 0, 1, 3))
        put("wo0", kcp(w_o[l][:, 0:512]))
        put("wo1", kcp(w_o[l][:, 512:1024]))
        for i in range(6):
            wdt = 512 if i < 5 else 256
            put("up_u%d" % i, kcp(w_up[l][:, i * 512:i * 512 + wdt]))
            put("up_g%d" % i, kcp(w_up[l][:, D_FF + i * 512:D_FF + i * 512 + wdt]))
        wd = w_down[l].reshape(NFC, 128, D)
        fc0 = 0
        for g, nfc in enumerate((8, 8, 6)):
            for ch in range(2):
                put("dn%d_%d" % (ch, g), wd[fc0:fc0 + nfc, :, ch * 512:(ch + 1) * 512].transpose(1, 0, 2))
            fc0 += nfc
    return out, offs, sizes, total


def host_consts():
    c = {}
    half = 64
    inv_freq = 10000.0 ** (-np.arange(half, dtype=np.float64) / half)
    pos = np.arange(S, dtype=np.float64)
    ang = inv_freq[:, None] * pos[None, :]
    cos = np.cos(ang)
    sin = np.sin(ang)
    c["rope_cos"] = np.concatenate([cos, cos], 0).astype(np.float32)
    c["rope_sin"] = np.concatenate([sin, -sin], 0).astype(np.float32)
    log_g = np.log1p(-np.exp2(-5.0 - np.arange(4, dtype=np.float64)))
    sc = 128.0 ** -0.5
    i = np.arange(128)
    wt = np.zeros((4, 128, 128), np.float64)
    for h in range(4):
        jj, ii = np.meshgrid(i, i, indexing="ij")
        same = (jj // 64) == (ii // 64)
        later = (ii // 64) > (jj // 64)
        w = np.where(same, np.exp(np.abs(ii - jj) * log_g[h]), np.where(later, np.exp((ii - jj) * log_g[h]), 0.0))
        wt[h] = w * sc
    c["ret_wt"] = wt.transpose(1, 0, 2).reshape(128, 512).astype(np.float32)
    qd = np.stack([np.exp((i + 1.0) * log_g[h]) * sc for h in range(4)], 0)
    c["ret_qdec"] = np.broadcast_to(qd.reshape(1, 512), (128, 512)).astype(np.float32).copy()
    kw = np.stack([np.exp((127.0 - i) * log_g[h]) for h in range(4)], 1)
    c["ret_kw"] = kw.astype(np.float32)
    c["ret_g128"] = [float(np.exp(128.0 * log_g[h])) for h in range(4)]
    ident = np.eye(128, dtype=np.float32)
    c["ident"] = ident
    jj, ii = np.meshgrid(i, i, indexing="ij")
    c["tri_incl"] = (jj <= ii).astype(np.float32)
    c["t3"] = ((jj > ii) & ((jj // 64) == (ii // 64))).astype(np.float32)
    c["chunk_ind"] = np.stack([(i < 64), (i >= 64)], 1).astype(np.float32)
    c["maskT"] = np.where(jj > ii, -30000.0, 0.0).astype(np.float32)
    return c


CONST_F32 = [("ident", 128), ("tri_incl", 128), ("t3", 128), ("chunk_ind", 2), ("ret_wt", 512),
             ("ret_qdec", 512), ("ret_kw", 4), ("maskT", 128)]


def build_program(n_layers, layer_elems, offs, sizes, dbg=None):
    nc = bass.Bass("TRN2", target_bir_lowering=False)
    consts = host_consts()
    g128 = consts["ret_g128"]

    def dram(name, shape, dt=F32, kind="ExternalInput"):
        return nc.dram_tensor(name, list(shape), dt, kind=kind).ap()

    x_d = dram("x", [S, D])
    out_d = dram("out", [S, D], kind="ExternalOutput")
    cT_d = dram("cT", [128, KC])
    wall_d = dram("wall", [n_layers, 128, layer_elems])
    wsm_d = dram("wsm", [n_layers, 128, KC * 32])
    wa2_d = dram("wa2", [n_layers, 16, 256])
    smT_d = dram("smT", [128, n_layers * NSM])
    bc_d = dram("bcp", [128, n_layers * NBC])
    cf_w = sum(w for _, w in CONST_F32)
    constf_d = dram("constf", [128, cf_w])
    rope_d = dram("rope", [128, 2, S])
    dbg_d = {}
    if dbg:
        for name, shape, dt in dbg:
            dbg_d[name] = dram("dbg_" + name, shape, dt, kind="ExternalOutput")

    with contextlib.ExitStack() as st:
        def sb(name, shape, dt):
            return st.enter_context(nc.sbuf_tensor("sb_" + name, list(shape), dt))

        Sx = Sched(nc)
        xT = sb("xT", [128, KC, S], F32)
        xT_r = [[Res("xT%d_%d" % (c, t)) for t in range(NTB)] for c in range(KC)]
        hT = sb("hT", [128, KC, TB], BF16)
        hT_r = [Res("hT%d" % c) for c in range(KC)]
        region = sb("region", [128, NFC, TB], BF16)
        reg_r = [Res("reg%d" % i) for i in range(NFC)]
        kcache = sb("kcache", [128, 4, S], BF16)
        kc_r = [[Res("kc%d_%d" % (h, t)) for t in range(NTB)] for h in range(4)]
        vcache = sb("vcache", [128, NT, 512], BF16)
        vc_r = [[Res("vc%d_%d" % (h, t)) for t in range(NTB)] for h in range(4)]
        ctok = sb("ctok", [128, NT, 4], F32)
        ctok_r = [Res("ctok%d" % t) for t in range(NTB)]
        carry = sb("carry", [128, 4], F32)
        carry_r = Res("carry")
        wbufs = [Slot(sb("wb%d" % i, [128, WBUF_ELEMS], BF16), Res("wb%d" % i), i) for i in range(N_WBUF)]
        wpool = Pool_(wbufs, "w")
        f32s = [Slot(sb("f%d" % i, [128, TB], F32), Res("f%d" % i), i) for i in range(N_F32)]
        fpool = Pool_(f32s, "f32")
        bfs = [Slot(sb("b%d" % i, [128, TB], BF16), Res("b%d" % i), i) for i in range(N_BF)]
        bpool = Pool_(bfs, "bf16")
        banks = [Slot(st.enter_context(nc.psum_tensor("ps%d" % i, [128, TB], F32)), Res("ps%d" % i, excl=True), i) for i in range(8)]
        ppool = Pool_(banks, "psum")
        constf = sb("constf", [128, cf_w], F32)
        constf_r = Res("constf")
        coff = {}
        o = 0
        for n_, w_ in CONST_F32:
            coff[n_] = (o, w_)
            o += w_

        def CF(name, lo=0, hi=None):
            o_, w_ = coff[name]
            hi = w_ if hi is None else hi
            return constf[:, o_ + lo:o_ + hi]

        ident_bf = sb("ident_bf", [128, 128], BF16)
        maskT_bf = sb("maskT_bf", [128, 128], BF16)
        ones_bf = sb("ones_bf", [128, 128], BF16)
        onesD_bf = sb("onesD_bf", [128, 128], BF16)
        ones_f = sb("ones_f", [128, 128], F32)
        onesrow0_bf = sb("onesrow0_bf", [128, 128], BF16)
        cbf_r = Res("cbf")
        smT2 = [sb("smT%d" % i, [128, NSM], F32) for i in range(2)]
        smT2_r = [Res("smT%d" % i) for i in range(2)]
        cT = sb("cT", [128, KC], F32)
        cact = sb("cact", [128, KC], BF16)
        cact_r = Res("cact")
        modT2 = [sb("modT%d" % i, [128, 48], F32) for i in range(2)]
        modT2_r = [Res("modT%d" % i) for i in range(2)]
        lay2 = [sb("lay%d" % i, [128, 24], F32) for i in range(2)]
        lay2_r = [Res("lay%d" % i) for i in range(2)]
        bcp = sb("bcp", [128, NBC], F32)
        bcp_r = Res("bcp")
        wsm = sb("wsm", [128, KC, 32], BF16)
        wsm_r = Res("wsm")
        wa2 = sb("wa2", [16, 256], BF16)
        wa2_r = Res("wa2")
        eps_t = sb("eps_t", [128, 1], F32)
        eps_ap = eps_t[:, 0:1]
        Rst = sb("Rst", [128, 4, 128], F32)
        Rst_r = [Res("R%d" % h) for h in range(4)]
        Sst = sb("Sst", [64, 4, 128], F32)
        Sst_r = [Res("S%d" % h) for h in range(4)]
        ucarry = sb("ucarry", [128, NFC, 2], F32)
        ucarry_r = [Res("uc%d" % f) for f in range(NFC)]
        glrT = sb("glrT", [16, TB], BF16)
        glrT_r = Res("glrT")
        lp = sb("lp", [128, 4, 64], F32)
        lp_r = Res("lp")
        kdec = sb("kdec", [128, 4, 64], BF16)
        kdec_r = Res("kdec")
        a_gla = sb("a_gla", [64, 8], F32)
        agla_r = Res("agla")
        lpf = sb("lpf", [128, 16], F32)
        lpf_r = Res("lpf")

        final_ops = []

        def wload(l, name):
            s = wpool.alloc()
            n = sizes[name]
            o_ = offs[name]
            Sx.dma(POOL, "w%d" % s.idx, DMA(s.t[:, 0:n], wall_d[l, :, o_:o_ + n]), writes=[s.r])
            return s

        def dump(name, ap, reads):
            if name in dbg_d:
                op = Sx.dma(SP, "dbg_" + name, DMA(dbg_d[name], ap), reads=reads)
                final_ops.append(op)

        Sx.dma(SP, "constf", DMA(constf[:], constf_d), writes=[constf_r])
        Sx.dma(SP, "cT", DMA(cT[:], cT_d), writes=[cact_r])
        Sx.add(ACT, ACTF(cact[:], cT[:], AF.Silu), reads=[cact_r], writes=[cact_r])
        Sx.add(DVE, CP(ident_bf[:], CF("ident")), reads=[constf_r], writes=[cbf_r])
        Sx.add(DVE, CP(maskT_bf[:], CF("maskT")), reads=[constf_r], writes=[cbf_r])
        Sx.add(DVE, MEMSET(ones_bf[:], 1.0), writes=[cbf_r])
        Sx.add(DVE, MEMSET(onesD_bf[:], 1.0 / 128.0), writes=[cbf_r])
        Sx.add(DVE, MEMSET(ones_f[:], 1.0), writes=[cbf_r])
        Sx.add(DVE, MEMSET(onesrow0_bf[:], 0.0), writes=[cbf_r])
        Sx.add(DVE, MEMSET(onesrow0_bf[0:1, :], 1.0), writes=[cbf_r])
        Sx.add(DVE, MEMSET(eps_t[:], EPS), writes=[cbf_r])
        identf = CF("ident")

        for t in range(NT):
            s0 = fpool.alloc()
            s1 = fpool.alloc()
            Sx.dma(SP, "f%d" % s0.idx, DMA(s0.t[:], x_d[t * 128:(t + 1) * 128, 0:512]), writes=[s0.r])
            Sx.dma(SP, "f%d" % s1.idx, DMA(s1.t[:], x_d[t * 128:(t + 1) * 128, 512:1024]), writes=[s1.r])
            for half, sl in enumerate((s0, s1)):
                pb = ppool.alloc()
                for j in range(4):
                    Sx.add(PE, TR(pb.t[:, j * 128:(j + 1) * 128], sl.t[:, j * 128:(j + 1) * 128], identf),
                           reads=[sl.r, constf_r], writes=[pb.r])
                dst = xT[:, half * 4:(half + 1) * 4, t * 128:(t + 1) * 128]
                src = pb.t[:].rearrange("p (c n) -> p c n", c=4)
                eng = ACT if half == 0 else DVE
                fn = ACP(dst, src) if eng == ACT else CP(dst, src)
                Sx.add(eng, fn, reads=[pb.r], writes=[xT_r[c][t // 4] for c in range(half * 4, half * 4 + 4)])
                ppool.free(pb)
            fpool.free(s0)
            fpool.free(s1)

        def mod_gen(l):
            smT, smT_r, modT, modT_r, lay, lay_r = smT2[l % 2], smT2_r[l % 2], modT2[l % 2], modT2_r[l % 2], lay2[l % 2], lay2_r[l % 2]
            sm0 = 0
            Sx.dma(SP, "smT", DMA(smT[:], smT_d[:, l * NSM:(l + 1) * NSM]), writes=[smT_r])
            pm = ppool.alloc()
            for cb in range(12):
                ws = wload(l, "ada%d" % cb)
                wv = ws.t[:, 0:4096].rearrange("p (k n) -> p k n", k=KC)
                pb = ppool.alloc()
                for kc in range(KC):
                    Sx.add(PE, MM(pb.t[0:1, :], cact[:, kc:kc + 1], wv[:, kc, :], start=(kc == 0), stop=(kc == KC - 1)),
                           reads=[cact_r, ws.r], writes=[pb.r])
                row = fpool.alloc()
                Sx.add(ACT, ACP(row.t[0:1, :], pb.t[0:1, :]), reads=[pb.r], writes=[row.r])
                ppool.free(pb)
                wpool.free(ws)
                for jj in range(4):
                    j = cb * 4 + jj
                    Sx.add(PE, MM(pm.t[:, j:j + 1], row.t[0:1, jj * 128:(jj + 1) * 128], ones_f[0:1, 0:1]),
                           reads=[row.r, cbf_r], writes=[pm.r])
                fpool.free(row)
                yield
            Sx.add(DVE, TT(modT[:], pm.t[:, 0:48], smT[:, sm0 + SM_BADA:sm0 + SM_BADA + 48], ALU.add),
                   reads=[pm.r, smT_r], writes=[modT_r])
            ppool.free(pm)
            Sx.add(DVE, STT(lay[:, 0:8], modT[:, 8:16], 1.0, smT[:, sm0 + SM_N1:sm0 + SM_N1 + 8], ALU.add, ALU.mult),
                   reads=[modT_r, smT_r], writes=[lay_r])
            Sx.add(DVE, STT(lay[:, 8:16], modT[:, 32:40], 1.0, smT[:, sm0 + SM_N2:sm0 + SM_N2 + 8], ALU.add, ALU.mult),
                   reads=[modT_r, smT_r], writes=[lay_r])
            Sx.add(DVE, TS(lay[:, 16:17], smT[:, sm0 + SM_QG:sm0 + SM_QG + 1], 128.0 ** -0.5, None, ALU.mult),
                   reads=[smT_r], writes=[lay_r])

        def emit_layer_small_loads(l):
            Sx.dma(SP, "bcp", DMA(bcp[:], bc_d[:, l * NBC:(l + 1) * NBC]), writes=[bcp_r])
            Sx.dma(POOL, "wsm", DMA(wsm[:].rearrange("p k n -> p (k n)"), wsm_d[l]), writes=[wsm_r])
            Sx.dma(POOL, "wa2", DMA(wa2[:], wa2_d[l]), writes=[wa2_r])

        def emit_mod(l):
            for _ in mod_gen(l):
                pass
            emit_layer_small_loads(l)

        def emit_norm(l, tb, which):
            smT, smT_r, modT, modT_r, lay, lay_r = smT2[l % 2], smT2_r[l % 2], modT2[l % 2], modT2_r[l % 2], lay2[l % 2], lay2_r[l % 2]
            a_off = 0 if which == 1 else 8
            b_off = 0 if which == 1 else 24
            pb = ppool.alloc()
            for c in range(KC):
                sq = bpool.alloc()
                xs = xT[:, c, tb * TB:(tb + 1) * TB]
                Sx.add(ACT if c % 2 == 0 else DVE, (ACTF(sq.t[:], xs, AF.Square) if c % 2 == 0 else TT(sq.t[:], xs, xs, ALU.mult)), reads=[xT_r[c][tb]], writes=[sq.r])
                Sx.add(PE, MM(pb.t[:], ones_bf[:], sq.t[:], start=(c == 0), stop=(c == KC - 1)),
                       reads=[sq.r, cbf_r], writes=[pb.r])
                bpool.free(sq)
            rstd = fpool.alloc()
            Sx.add(ACT, ACTF(rstd.t[:], pb.t[:], AF.Ln, bias=eps_ap, scale=1.0 / D), reads=[pb.r, cbf_r], writes=[rstd.r])
            ppool.free(pb)
            Sx.add(ACT, ACTF(rstd.t[:], rstd.t[:], AF.Exp, scale=-0.5), reads=[rstd.r], writes=[rstd.r])
            for c in range(KC):
                tmp = fpool.alloc()
                xs = xT[:, c, tb * TB:(tb + 1) * TB]
                Sx.add(DVE, STT(tmp.t[:], xs, lay[:, a_off + c:a_off + c + 1], rstd.t[:], ALU.mult, ALU.mult),
                       reads=[xT_r[c][tb], lay_r, rstd.r], writes=[tmp.r])
                Sx.add(ACT, ACTF(hT[:, c, :], tmp.t[:], AF.Identity, bias=modT[:, b_off + c:b_off + c + 1]),
                       reads=[tmp.r, modT_r], writes=[hT_r[c]])
                fpool.free(tmp)
            fpool.free(rstd)

        def proj_F(ws, ncols_blk, col0, m, pb):
            wv = ws.t[:, 0:8 * ncols_blk].rearrange("p (k n) -> p k n", k=KC)
            for kc in range(KC):
                Sx.add(PE, MM(pb.t[0:m, :], wv[:, kc, col0:col0 + m], hT[:, kc, :], start=(kc == 0), stop=(kc == KC - 1)),
                       reads=[ws.r, hT_r[kc]], writes=[pb.r])

        def proj_T4(ws, ncols_blk, col0, n, pb):
            wv = ws.t[:, 0:8 * ncols_blk].rearrange("p (k n) -> p k n", k=KC)
            for t in range(4):
                for kc in range(KC):
                    Sx.add(PE, MM(pb.t[:, t * n:(t + 1) * n], hT[:, kc, t * 128:(t + 1) * 128], wv[:, kc, col0:col0 + n],
                                  start=(kc == 0), stop=(kc == KC - 1)),
                           reads=[ws.r, hT_r[kc]], writes=[pb.r])

        def proj_V(ws, ncols_blk, col0):
            pvT = ppool.alloc()
            proj_F(ws, ncols_blk, col0, 128, pvT)
            vT = bpool.alloc()
            Sx.add(ACT, ACP(vT.t[:], pvT.t[:]), reads=[pvT.r], writes=[vT.r])
            ppool.free(pvT)
            pt_ = ppool.alloc()
            pt_bf = pt_.t[:].bitcast(BF16)
            for t in range(4):
                Sx.add(PE, TR(pt_bf[:, t * 128:(t + 1) * 128], vT.t[:, t * 128:(t + 1) * 128], ident_bf[:]),
                       reads=[vT.r, cbf_r], writes=[pt_.r])
            bpool.free(vT)
            return pt_, pt_bf[:, 0:512]

        def part_norm_stats(src_f, want_mean):
            sq = bpool.alloc()
            Sx.add(DVE, TT(sq.t[:], src_f.t[:], src_f.t[:], ALU.mult), reads=[src_f.r], writes=[sq.r])
            p2 = ppool.alloc()
            Sx.add(PE, MM(p2.t[:], onesD_bf[:], sq.t[:]), reads=[sq.r, cbf_r], writes=[p2.r])
            bpool.free(sq)
            if want_mean:
                ob = bpool.alloc()
                Sx.add(ACT, ACP(ob.t[:], src_f.t[:]), reads=[src_f.r], writes=[ob.r])
                p1 = ppool.alloc()
                Sx.add(PE, MM(p1.t[:], onesD_bf[:], ob.t[:]), reads=[ob.r, cbf_r], writes=[p1.r])
                bpool.free(ob)
                mean = fpool.alloc()
                Sx.add(ACT, ACP(mean.t[:], p1.t[:]), reads=[p1.r], writes=[mean.r])
                ppool.free(p1)
                var = fpool.alloc()
                Sx.add(ACT, ACTF(var.t[:], mean.t[:], AF.Square), reads=[mean.r], writes=[var.r])
                Sx.add(DVE, TT(var.t[:], p2.t[:], var.t[:], ALU.subtract), reads=[p2.r, var.r], writes=[var.r])
                ppool.free(p2)
                Sx.add(ACT, ACTF(var.t[:], var.t[:], AF.Ln, bias=eps_ap), reads=[var.r, cbf_r], writes=[var.r])
                Sx.add(ACT, ACTF(var.t[:], var.t[:], AF.Exp, scale=-0.5), reads=[var.r], writes=[var.r])
                return mean, var
            rstd = fpool.alloc()
            Sx.add(ACT, ACTF(rstd.t[:], p2.t[:], AF.Ln, bias=eps_ap), reads=[p2.r, cbf_r], writes=[rstd.r])
            ppool.free(p2)
            Sx.add(ACT, ACTF(rstd.t[:], rstd.t[:], AF.Exp, scale=-0.5), reads=[rstd.r], writes=[rstd.r])
            return None, rstd

        import os as _os

        def rotary(pb, dst, cos_s, sin_s):
            t1 = fpool.alloc()
            t2 = fpool.alloc()
            Sx.add(DVE, TT(t1.t[:], pb.t[:], cos_s.t[:], ALU.mult), reads=[pb.r, cos_s.r], writes=[t1.r])
            Sx.add(DVE, TT(t2.t[0:64, :], pb.t[64:128, :], sin_s.t[64:128, :], ALU.mult), reads=[pb.r, sin_s.r], writes=[t2.r])
            Sx.add(DVE, TT(t2.t[64:128, :], pb.t[0:64, :], sin_s.t[0:64, :], ALU.mult), reads=[pb.r, sin_s.r], writes=[t2.r])
            Sx.add(DVE, TT(dst.t[:], t1.t[:], t2.t[:], ALU.add), reads=[t1.r, t2.r], writes=[dst.r])
            fpool.free(t1)
            fpool.free(t2)

        def ret_head(l, tb, h):
            smT, smT_r, modT, modT_r, lay, lay_r = smT2[l % 2], smT2_r[l % 2], modT2[l % 2], modT2_r[l % 2], lay2[l % 2], lay2_r[l % 2]
            sm0 = 0
            ws = wload(l, "ret%d" % h)
            cos_s = fpool.alloc()
            sin_s = fpool.alloc()
            Sx.dma(SP, "f%d" % cos_s.idx, DMA(cos_s.t[:], rope_d[:, 0, tb * TB:(tb + 1) * TB]), writes=[cos_s.r])
            Sx.dma(SP, "f%d" % sin_s.idx, DMA(sin_s.t[:], rope_d[:, 1, tb * TB:(tb + 1) * TB]), writes=[sin_s.r])
            q_rot = bpool.alloc()
            k_rot = bpool.alloc()
            pb = ppool.alloc()
            proj_F(ws, 512, 0, 128, pb)
            rotary(pb, q_rot, cos_s, sin_s)
            ppool.free(pb)
            pb = ppool.alloc()
            proj_F(ws, 512, 128, 128, pb)
            rotary(pb, k_rot, cos_s, sin_s)
            ppool.free(pb)
            fpool.free(cos_s)
            fpool.free(sin_s)
            pb, pb_ap = proj_V(ws, 512, 256)
            v_t = bpool.alloc()
            Sx.add(DVE, CP(v_t.t[:], pb_ap), reads=[pb.r], writes=[v_t.r])
            ppool.free(pb)
            pb = ppool.alloc()
            proj_F(ws, 512, 384, 128, pb)
            wpool.free(ws)
            rgs = fpool.alloc()
            Sx.add(ACT, ACTF(rgs.t[:], pb.t[:], AF.Silu), reads=[pb.r], writes=[rgs.r])
            ppool.free(pb)
            yield
            pk = ppool.alloc()
            pk_bf = pk.t[:].bitcast(BF16)
            for t in range(4):
                Sx.add(PE, TR(pk_bf[:, t * 128:(t + 1) * 128], k_rot.t[:, t * 128:(t + 1) * 128], ident_bf[:]),
                       reads=[k_rot.r, cbf_r], writes=[pk.r])
            kwtok = bpool.alloc()
            Sx.add(DVE, TS(kwtok.t[:], pk_bf[:, 0:512], CF("ret_kw", h, h + 1), None, ALU.mult),
                   reads=[pk.r, constf_r], writes=[kwtok.r])
            ppool.free(pk)
            ps_s = ppool.alloc()
            for t in range(4):
                sl = slice(t * 128, (t + 1) * 128)
                Sx.add(PE, MM(ps_s.t[:, sl], k_rot.t[:, sl], q_rot.t[:, sl]),
                       reads=[k_rot.r, q_rot.r], writes=[ps_s.r])
            pt = bpool.alloc()
            for t in range(4):
                sl = slice(t * 128, (t + 1) * 128)
                Sx.add(DVE, TT(pt.t[:, sl], ps_s.t[:, sl], CF("ret_wt", h * 128, (h + 1) * 128), ALU.mult),
                       reads=[ps_s.r, constf_r], writes=[pt.r])
            ppool.free(ps_s)
            qw = bpool.alloc()
            for t in range(4):
                sl = slice(t * 128, (t + 1) * 128)
                Sx.add(DVE, TT(qw.t[:, sl], q_rot.t[:, sl], CF("ret_qdec", h * 128, (h + 1) * 128), ALU.mult),
                       reads=[q_rot.r, constf_r], writes=[qw.r])
            bpool.free(q_rot)
            bpool.free(k_rot)
            yield
            ps_kv = ppool.alloc()
            for t in range(4):
                sl = slice(t * 128, (t + 1) * 128)
                Sx.add(PE, MM(ps_kv.t[:, sl], kwtok.t[:, sl], v_t.t[:, sl]),
                       reads=[kwtok.r, v_t.r], writes=[ps_kv.r])
            bpool.free(kwtok)
            rb = bpool.alloc()
            for t in range(4):
                sl = slice(t * 128, (t + 1) * 128)
                gt = tb * 4 + t
                if gt > 0:
                    Sx.add(ACT, ACP(rb.t[:, sl], Rst[:, h, :]), reads=[Rst_r[h]], writes=[rb.r])
                if gt == 0:
                    Sx.add(DVE, CP(Rst[:, h, :], ps_kv.t[:, sl]), reads=[ps_kv.r], writes=[Rst_r[h]])
                else:
                    Sx.add(DVE, STT(Rst[:, h, :], Rst[:, h, :], g128[h], ps_kv.t[:, sl], ALU.mult, ALU.add),
                           reads=[ps_kv.r, Rst_r[h]], writes=[Rst_r[h]])
            ppool.free(ps_kv)
            yield
            ps_o = ppool.alloc()
            for t in range(4):
                sl = slice(t * 128, (t + 1) * 128)
                gt = tb * 4 + t
                Sx.add(PE, MM(ps_o.t[:, sl], v_t.t[:, sl], pt.t[:, sl], start=True, stop=(gt == 0)),
                       reads=[v_t.r, pt.r], writes=[ps_o.r])
                if gt > 0:
                    Sx.add(PE, MM(ps_o.t[:, sl], rb.t[:, sl], qw.t[:, sl], start=False, stop=True),
                           reads=[rb.r, qw.r], writes=[ps_o.r])
            for s_ in (pt, qw, v_t, rb):
                bpool.free(s_)
            o_f = fpool.alloc()
            Sx.add(ACT, ACP(o_f.t[:], ps_o.t[:]), reads=[ps_o.r], writes=[o_f.r])
            ppool.free(ps_o)
            ob = bpool.alloc()
            Sx.add(ACT, ACP(ob.t[:], o_f.t[:]), reads=[o_f.r], writes=[ob.r])
            yield
            p1 = ppool.alloc()
            Sx.add(PE, MM(p1.t[:], onesD_bf[:], ob.t[:]), reads=[ob.r, cbf_r], writes=[p1.r])
            bpool.free(ob)
            Sx.add(DVE, TT(o_f.t[:], o_f.t[:], p1.t[:], ALU.subtract), reads=[o_f.r, p1.r], writes=[o_f.r])
            ppool.free(p1)
            sq = bpool.alloc()
            Sx.add(ACT, ACTF(sq.t[:], o_f.t[:], AF.Square), reads=[o_f.r], writes=[sq.r])
            yield
            p2 = ppool.alloc()
            Sx.add(PE, MM(p2.t[:], onesD_bf[:], sq.t[:]), reads=[sq.r, cbf_r], writes=[p2.r])
            bpool.free(sq)
            rstd = fpool.alloc()
            Sx.add(ACT, ACTF(rstd.t[:], p2.t[:], AF.Ln, bias=eps_ap), reads=[p2.r, cbf_r], writes=[rstd.r])
            ppool.free(p2)
            Sx.add(ACT, ACTF(rstd.t[:], rstd.t[:], AF.Exp, scale=-0.5), reads=[rstd.r], writes=[rstd.r])
            Sx.add(DVE, TT(o_f.t[:], o_f.t[:], rstd.t[:], ALU.mult), reads=[o_f.r, rstd.r], writes=[o_f.r])
            Sx.add(DVE, STT(region[:, 8 + h, :], o_f.t[:], smT[:, sm0 + SM_RETG + h:sm0 + SM_RETG + h + 1], rgs.t[:], ALU.mult, ALU.mult),
                   reads=[o_f.r, smT_r, rgs.r], writes=[reg_r[8 + h]])
            for s_ in (rstd, o_f, rgs):
                fpool.free(s_)

        def rms_tail(o_f, gain_ap, extra_reads, post):
            sq = bpool.alloc()
            Sx.add(ACT, ACTF(sq.t[:], o_f.t[:], AF.Square), reads=[o_f.r], writes=[sq.r])
            yield
            p2 = ppool.alloc()
            Sx.add(PE, MM(p2.t[:], onesD_bf[:], sq.t[:]), reads=[sq.r, cbf_r], writes=[p2.r])
            bpool.free(sq)
            rstd = fpool.alloc()
            Sx.add(ACT, ACTF(rstd.t[:], p2.t[:], AF.Ln, bias=eps_ap), reads=[p2.r, cbf_r], writes=[rstd.r])
            ppool.free(p2)
            Sx.add(ACT, ACTF(rstd.t[:], rstd.t[:], AF.Exp, scale=-0.5), reads=[rstd.r], writes=[rstd.r])
            post(rstd)
            fpool.free(rstd)

        def gla_pre(l, tb):
            pb = ppool.alloc()
            for kc in range(KC):
                Sx.add(PE, MM(pb.t[0:16, :], wsm[:, kc, 0:16], hT[:, kc, :], start=(kc == 0), stop=(kc == KC - 1)),
                       reads=[wsm_r, hT_r[kc]], writes=[pb.r])
            Sx.add(ACT, ACP(glrT[:], pb.t[0:16, :]), reads=[pb.r], writes=[glrT_r])
            ppool.free(pb)

        def gla_head(l, tb, h):
            smT, smT_r, modT, modT_r, lay, lay_r = smT2[l % 2], smT2_r[l % 2], modT2[l % 2], modT2_r[l % 2], lay2[l % 2], lay2_r[l % 2]
            sm0 = 0
            ws = wload(l, "gla%d" % h)
            pz = ppool.alloc()
            for t in range(4):
                Sx.add(PE, MM(pz.t[:, t * 64:(t + 1) * 64], glrT[:, t * 128:(t + 1) * 128], wa2[:, h * 64:(h + 1) * 64]),
                       reads=[glrT_r, wa2_r], writes=[pz.r])
            for t in range(4):
                Sx.add(DVE, TT(lp[:, t, :], pz.t[:, t * 64:(t + 1) * 64], bcp[:, h * 64:(h + 1) * 64], ALU.add),
                       reads=[pz.r, bcp_r], writes=[lp_r])
            ppool.free(pz)
            lpv = lp[:].rearrange("p t d -> p (t d)")
            Sx.add(ACT, ACTF(lpv, lpv, AF.Exp, scale=-1.0), reads=[lp_r], writes=[lp_r])
            Sx.add(ACT, ACTF(lpv, lpv, AF.Ln, bias=1.0), reads=[lp_r], writes=[lp_r])
            pb = ppool.alloc()
            proj_F(ws, 384, 0, 64, pb)
            qg = bpool.alloc()
            Sx.add(ACT, ACTF(qg.t[0:64, :], pb.t[0:64, :], AF.Identity, scale=0.125), reads=[pb.r], writes=[qg.r])
            ppool.free(pb)
            pb, pb_ap = proj_V(ws, 384, 128)
            v_t = bpool.alloc()
            Sx.add(DVE, CP(v_t.t[:], pb_ap), reads=[pb.r], writes=[v_t.r])
            ppool.free(pb)
            pb = ppool.alloc()
            proj_F(ws, 384, 256, 128, pb)
            ggs = fpool.alloc()
            Sx.add(ACT, ACTF(ggs.t[:], pb.t[:], AF.Silu), reads=[pb.r], writes=[ggs.r])
            ppool.free(pb)
            pk = ppool.alloc()
            proj_T4(ws, 384, 64, 64, pk)
            wpool.free(ws)
            yield
            pe_ = ppool.alloc()
            for t in range(4):
                Sx.add(PE, MM(pe_.t[:, t * 64:(t + 1) * 64], CF("t3"), lp[:, t, :]), reads=[constf_r, lp_r], writes=[pe_.r])
            ef = fpool.alloc()
            Sx.add(ACT, ACTF(ef.t[:, 0:256], pe_.t[:, 0:256], AF.Exp, scale=-1.0 / 16.0), reads=[pe_.r], writes=[ef.r])
            ppool.free(pe_)
            pa = ppool.alloc()
            for t in range(4):
                Sx.add(PE, MM(pa.t[0:64, t * 2:t * 2 + 2], lp[:, t, :], CF("chunk_ind")),
                       reads=[lp_r, constf_r], writes=[pa.r])
            Sx.add(ACT, ACTF(a_gla[:], pa.t[0:64, 0:8], AF.Exp, scale=-1.0 / 16.0), reads=[pa.r], writes=[agla_r])
            ppool.free(pa)
            Sx.add(DVE, TT(kdec[:].rearrange("p t d -> p (t d)"), pk.t[:, 0:256], ef.t[:, 0:256], ALU.mult),
                   reads=[pk.r, ef.r], writes=[kdec_r])
            ppool.free(pk)
            fpool.free(ef)
            yield
            ps_kv = [ppool.alloc(), ppool.alloc()]
            for n in range(8):
                t, half = n // 2, n % 2
                rows = slice(half * 64, half * 64 + 64)
                kvb = ps_kv[n % 2]
                kvs = slice((n // 2) * 128, (n // 2) * 128 + 128)
                Sx.add(PE, MM(kvb.t[0:64, kvs], kdec[rows, t, :], v_t.t[rows, t * 128:(t + 1) * 128]),
                       reads=[kdec_r, v_t.r], writes=[kvb.r])
            bpool.free(v_t)
            sb8 = [bpool.alloc(), bpool.alloc()]
            for n in range(8):
                kvb = ps_kv[n % 2]
                kvs = slice((n // 2) * 128, (n // 2) * 128 + 128)
                if tb == 0 and n == 0:
                    Sx.add(DVE, CP(Sst[:, h, :], kvb.t[0:64, kvs]), reads=[kvb.r], writes=[Sst_r[h]])
                else:
                    Sx.add(DVE, STT(Sst[:, h, :], Sst[:, h, :], a_gla[:, n:n + 1], kvb.t[0:64, kvs], ALU.mult, ALU.add),
                           reads=[kvb.r, Sst_r[h], agla_r], writes=[Sst_r[h]])
                sbn = sb8[n // 4]
                Sx.add(ACT, ACP(sbn.t[0:64, (n % 4) * 128:(n % 4) * 128 + 128], Sst[:, h, :]), reads=[Sst_r[h]], writes=[sbn.r])
            ppool.free(ps_kv[0])
            ppool.free(ps_kv[1])
            yield
            ps_o = ppool.alloc()
            for n in range(8):
                sbn = sb8[n // 4]
                cs = slice(n * 64, n * 64 + 64)
                Sx.add(PE, MM(ps_o.t[:, cs], sbn.t[0:64, (n % 4) * 128:(n % 4) * 128 + 128], qg.t[0:64, cs]),
                       reads=[sbn.r, qg.r], writes=[ps_o.r])
            bpool.free(qg)
            bpool.free(sb8[0])
            bpool.free(sb8[1])
            o_f = fpool.alloc()
            Sx.add(ACT, ACP(o_f.t[:], ps_o.t[:]), reads=[ps_o.r], writes=[o_f.r])
            ppool.free(ps_o)

            def post(rstd):
                Sx.add(DVE, STT(o_f.t[:], o_f.t[:], smT[:, sm0 + SM_GLAG:sm0 + SM_GLAG + 1], rstd.t[:], ALU.mult, ALU.mult),
                       reads=[o_f.r, smT_r, rstd.r], writes=[o_f.r])
                Sx.add(DVE, TT(region[:, 12 + h, :], o_f.t[:], ggs.t[:], ALU.mult), reads=[o_f.r, ggs.r], writes=[reg_r[12 + h]])
            yield from rms_tail(o_f, None, None, post)
            fpool.free(o_f)
            fpool.free(ggs)

        def fox_pre(l, tb):
            pf = ppool.alloc()
            for t in range(4):
                for kc in range(KC):
                    Sx.add(PE, MM(pf.t[:, t * 4:t * 4 + 4], hT[:, kc, t * 128:(t + 1) * 128], wsm[:, kc, 16:20],
                                  start=(kc == 0), stop=(kc == KC - 1)),
                           reads=[wsm_r, hT_r[kc]], writes=[pf.r])
            for t in range(4):
                Sx.add(DVE, TT(lpf[:, t * 4:t * 4 + 4], pf.t[:, t * 4:t * 4 + 4], bcp[:, 256:260], ALU.add),
                       reads=[pf.r, bcp_r], writes=[lpf_r])
            ppool.free(pf)
            Sx.add(ACT, ACTF(lpf[:], lpf[:], AF.Exp, scale=-1.0), reads=[lpf_r], writes=[lpf_r])
            Sx.add(ACT, ACTF(lpf[:], lpf[:], AF.Ln, bias=1.0), reads=[lpf_r], writes=[lpf_r])
            yield
            pc = ppool.alloc()
            for t in range(4):
                for t2 in range(t + 1):
                    lhs = CF("tri_incl") if t2 == t else ones_f[:]
                    Sx.add(PE, MM(pc.t[:, t * 4:t * 4 + 4], lhs, lpf[:, t2 * 4:t2 * 4 + 4], start=(t2 == 0), stop=(t2 == t)),
                           reads=[constf_r, cbf_r, lpf_r], writes=[pc.r])
            if tb == 0:
                Sx.add(DVE, CP(ctok[:, 0:4, :].rearrange("p t h -> p (t h)"), pc.t[:, 0:16]), reads=[pc.r], writes=[ctok_r[tb]])
            else:
                for t in range(4):
                    Sx.add(DVE, TT(ctok[:, tb * 4 + t, :], pc.t[:, t * 4:t * 4 + 4], carry[:], ALU.add),
                           reads=[pc.r, carry_r], writes=[ctok_r[tb]])
            ppool.free(pc)
            pt_ = ppool.alloc()
            for t in range(4):
                Sx.add(PE, MM(pt_.t[:, 0:4], ones_f[:], lpf[:, t * 4:t * 4 + 4], start=(t == 0), stop=(t == 3)),
                       reads=[cbf_r, lpf_r], writes=[pt_.r])
            if tb == 0:
                Sx.add(DVE, CP(carry[:], pt_.t[:, 0:4]), reads=[pt_.r], writes=[carry_r])
            else:
                Sx.add(DVE, TT(carry[:], carry[:], pt_.t[:, 0:4], ALU.add), reads=[pt_.r, carry_r], writes=[carry_r])
            ppool.free(pt_)

        def fox_head(l, tb, h):
            smT, smT_r, modT, modT_r, lay, lay_r = smT2[l % 2], smT2_r[l % 2], modT2[l % 2], modT2_r[l % 2], lay2[l % 2], lay2_r[l % 2]
            sm0 = 0
            ws = wload(l, "fox%d" % h)
            pq = ppool.alloc()
            proj_F(ws, 384, 0, 128, pq)
            qf = fpool.alloc()
            Sx.add(ACT, ACP(qf.t[:], pq.t[:]), reads=[pq.r], writes=[qf.r])
            ppool.free(pq)
            sqq = bpool.alloc()
            Sx.add(DVE, TT(sqq.t[:], qf.t[:], qf.t[:], ALU.mult), reads=[qf.r], writes=[sqq.r])
            pk = ppool.alloc()
            proj_F(ws, 384, 128, 128, pk)
            kf = fpool.alloc()
            Sx.add(ACT, ACP(kf.t[:], pk.t[:]), reads=[pk.r], writes=[kf.r])
            ppool.free(pk)
            sqk = bpool.alloc()
            Sx.add(DVE, TT(sqk.t[:], kf.t[:], kf.t[:], ALU.mult), reads=[kf.r], writes=[sqk.r])
            pb, pb_ap = proj_V(ws, 384, 256)
            wpool.free(ws)
            Sx.add(DVE, CP(vcache[:, tb * 4:(tb + 1) * 4, h * 128:(h + 1) * 128], pb_ap.rearrange("p (t e) -> p t e", t=4)),
                   reads=[pb.r], writes=[vc_r[h][tb]])
            ppool.free(pb)
            yield
            qh = bpool.alloc()
            for src, sq_, gain_ap, dst_ap, dst_res in ((qf, sqq, lay[:, 16:17], qh.t[:], qh.r),
                                                       (kf, sqk, smT[:, sm0 + SM_KG:sm0 + SM_KG + 1], kcache[:, h, tb * TB:(tb + 1) * TB], kc_r[h][tb])):
                p2 = ppool.alloc()
                Sx.add(PE, MM(p2.t[:], onesD_bf[:], sq_.t[:]), reads=[sq_.r, cbf_r], writes=[p2.r])
                bpool.free(sq_)
                rstd = fpool.alloc()
                Sx.add(ACT, ACTF(rstd.t[:], p2.t[:], AF.Ln, bias=eps_ap), reads=[p2.r, cbf_r], writes=[rstd.r])
                ppool.free(p2)
                Sx.add(ACT, ACTF(rstd.t[:], rstd.t[:], AF.Exp, scale=-0.5), reads=[rstd.r], writes=[rstd.r])
                Sx.add(DVE, STT(dst_ap, src.t[:], gain_ap, rstd.t[:], ALU.mult, ALU.mult),
                       reads=[src.r, rstd.r, lay_r, smT_r], writes=[dst_res])
                fpool.free(rstd)
                fpool.free(src)
            pa = ppool.alloc()
            for t in range(4):
                gt = tb * 4 + t
                Sx.add(PE, MM(pa.t[0:1, t * 128:(t + 1) * 128], ctok[:, gt, h:h + 1], identf),
                       reads=[ctok_r[tb], constf_r], writes=[pa.r])
            augs = bpool.alloc()
            Sx.add(DVE, MEMSET(augs.t[:], 0.0), writes=[augs.r])
            Sx.add(ACT, ACTF(augs.t[0:1, :], pa.t[0:1, :], AF.Identity, scale=-1.0), reads=[pa.r, augs.r], writes=[augs.r])
            ppool.free(pa)
            yield
            ps_o = ppool.alloc()
            ps_n = ppool.alloc()
            njb = tb * 4 + 4

            def scores(jb):
                tau = jb - tb * 4
                c0 = 0 if tau < 0 else tau * 128
                cs = slice(c0, TB)
                ps_s = ppool.alloc()
                Sx.add(PE, MM(ps_s.t[:, cs], kcache[:, h, jb * 128:(jb + 1) * 128], qh.t[:, cs], start=True, stop=False),
                       reads=[kc_r[h][jb // 4], qh.r], writes=[ps_s.r])
                Sx.add(PE, MM(ps_s.t[:, cs], onesrow0_bf[:], augs.t[:, cs], start=False, stop=(tau < 0)),
                       reads=[cbf_r, augs.r], writes=[ps_s.r])
                if tau >= 0:
                    Sx.add(PE, MM(ps_s.t[:, c0:c0 + 128], ident_bf[:], maskT_bf[:], start=False, stop=True),
                           reads=[cbf_r], writes=[ps_s.r])
                pt = bpool.alloc()
                Sx.add(ACT, ACTF(pt.t[:, cs], ps_s.t[:, cs], AF.Exp, bias=ctok[:, jb, h:h + 1]),
                       reads=[ps_s.r, ctok_r[jb // 4]], writes=[pt.r])
                ppool.free(ps_s)
                return pt, cs

            LOOK = int(_os.environ.get("MK_LOOK", "3"))
            pend = []
            nissued = 0
            while nissued < min(LOOK, njb):
                pend.append(scores(nissued))
                nissued += 1
            for jb in range(njb):
                pt, cs = pend.pop(0)
                if nissued < njb:
                    pend.append(scores(nissued))
                    nissued += 1
                Sx.add(PE, MM(ps_o.t[:, cs], vcache[:, jb, h * 128:(h + 1) * 128], pt.t[:, cs], start=(jb == 0), stop=(jb == njb - 1)),
                       reads=[vc_r[h][jb // 4], pt.r], writes=[ps_o.r])
                Sx.add(PE, MM(ps_n.t[:, cs], ones_bf[:], pt.t[:, cs], start=(jb == 0), stop=(jb == njb - 1)),
                       reads=[cbf_r, pt.r], writes=[ps_n.r])
                bpool.free(pt)
                if jb % 2 == 1 and jb + 1 < njb:
                    yield
            bpool.free(qh)
            bpool.free(augs)
            rs = fpool.alloc()
            Sx.add(DVE, (lambda e, o_=rs.t[:], i_=ps_n.t[:]: e.reciprocal(o_, i_)), reads=[ps_n.r], writes=[rs.r])
            ppool.free(ps_n)
            Sx.add(DVE, TT(region[:, 16 + h, :], ps_o.t[:], rs.t[:], ALU.mult), reads=[ps_o.r, rs.r], writes=[reg_r[16 + h]])
            ppool.free(ps_o)
            fpool.free(rs)

        def run_interleaved(items, width, background=None):
            items = list(items)
            live = []
            while items or live or background is not None:
                if background is not None:
                    try:
                        next(background)
                    except StopIteration:
                        background = None
                while items and len(live) < width:
                    k = items[0][0]
                    if k in ("gla", "foxpre", "fox") and any(k == lk for lk, _ in live):
                        break
                    if k == "fox" and any(lk == "foxpre" for lk, _ in live):
                        break
                    live.append(items.pop(0))
                for it in list(live):
                    try:
                        next(it[1])
                    except StopIteration:
                        live.remove(it)

        def emit_mixers(l, tb, background=None):
            gla_pre(l, tb)
            items = [("foxpre", fox_pre(l, tb))]
            for h in range(4):
                items += [("ret", ret_head(l, tb, h)), ("gla", gla_head(l, tb, h)), ("fox", fox_head(l, tb, h))]
            run_interleaved(items, int(_os.environ.get("MK_WIDTH", "3")), background)

        def emit_gate(l, tb):
            smT, smT_r, modT, modT_r, lay, lay_r = smT2[l % 2], smT2_r[l % 2], modT2[l % 2], modT2_r[l % 2], lay2[l % 2], lay2_r[l % 2]
            sm0 = 0
            for n in range(8):
                wm = wload(l, "mg%d" % n)
                wb = wload(l, "br%d" % n)
                mgv = wm.t[:, 0:3072].rearrange("p (k b j) -> p k b j", k=KC, b=3)
                brv = wb.t[:, 0:1536].rearrange("p (b k j) -> p b k j", b=3, k=4)
                prods = []
                for b in range(3):
                    pg = ppool.alloc()
                    for kc in range(KC):
                        Sx.add(PE, MM(pg.t[:], mgv[:, kc, b, :], hT[:, kc, :], start=(kc == 0), stop=(kc == KC - 1)),
                               reads=[wm.r, hT_r[kc]], writes=[pg.r])
                    py = ppool.alloc()
                    for k4 in range(4):
                        ch = 8 + b * 4 + k4
                        Sx.add(PE, MM(py.t[:], brv[:, b, k4, :], region[:, ch, :], start=(k4 == 0), stop=(k4 == 3)),
                               reads=[wb.r, reg_r[ch]], writes=[py.r])
                    sg = fpool.alloc()
                    Sx.add(ACT, ACTF(sg.t[:], pg.t[:], AF.Sigmoid, bias=smT[:, sm0 + SM_BMG + b * 8 + n:sm0 + SM_BMG + b * 8 + n + 1]),
                           reads=[pg.r, smT_r], writes=[sg.r])
                    ppool.free(pg)
                    Sx.add(DVE, TT(sg.t[:], py.t[:], sg.t[:], ALU.mult), reads=[py.r, sg.r], writes=[sg.r])
                    ppool.free(py)
                    prods.append(sg)
                wpool.free(wm)
                wpool.free(wb)
                Sx.add(DVE, TT(prods[0].t[:], prods[0].t[:], prods[1].t[:], ALU.add), reads=[prods[0].r, prods[1].r], writes=[prods[0].r])
                Sx.add(DVE, TT(region[:, n, :], prods[0].t[:], prods[2].t[:], ALU.add), reads=[prods[0].r, prods[2].r], writes=[reg_r[n]])
                for p_ in prods:
                    fpool.free(p_)
            for ch in range(2):
                ws = wload(l, "wo%d" % ch)
                wv = ws.t[:, 0:4096].rearrange("p (k n) -> p k n", k=KC)
                for dl in range(4):
                    dc = ch * 4 + dl
                    pb = ppool.alloc()
                    for n in range(8):
                        Sx.add(PE, MM(pb.t[:], wv[:, n, dl * 128:(dl + 1) * 128], region[:, n, :], start=(n == 0), stop=(n == 7)),
                               reads=[ws.r, reg_r[n]], writes=[pb.r])
                    xs = xT[:, dc, tb * TB:(tb + 1) * TB]
                    Sx.add(DVE, STT(xs, pb.t[:], modT[:, 16 + dc:17 + dc], xs, ALU.mult, ALU.add),
                           reads=[pb.r, modT_r, xT_r[dc][tb]], writes=[xT_r[dc][tb]])
                    ppool.free(pb)
                wpool.free(ws)

        def emit_ffn(l, tb, between=None):
            smT, smT_r, modT, modT_r, lay, lay_r = smT2[l % 2], smT2_r[l % 2], modT2[l % 2], modT2_r[l % 2], lay2[l % 2], lay2_r[l % 2]
            sm0 = 0
            S3 = int(_os.environ.get("MK_SUB3", "99"))

            def c3(k):
                if k > S3:
                    Sx.muted = True
            for i in range(6):
                nf = 4 if i < 5 else 2
                wdt = 128 * nf
                wu = wload(l, "up_u%d" % i)
                wg = wload(l, "up_g%d" % i)
                wuv = wu.t[:, 0:8 * wdt].rearrange("p (k n) -> p k n", k=KC)
                wgv = wg.t[:, 0:8 * wdt].rearrange("p (k n) -> p k n", k=KC)
                for j in range(nf):
                    f = i * 4 + j
                    pu = ppool.alloc()
                    pg = ppool.alloc()
                    for kc in range(KC):
                        Sx.add(PE, MM(pu.t[:], wuv[:, kc, j * 128:(j + 1) * 128], hT[:, kc, :], start=(kc == 0), stop=(kc == KC - 1)),
                               reads=[wu.r, hT_r[kc]], writes=[pu.r])
                    for kc in range(KC):
                        Sx.add(PE, MM(pg.t[:], wgv[:, kc, j * 128:(j + 1) * 128], hT[:, kc, :], start=(kc == 0), stop=(kc == KC - 1)),
                               reads=[wg.r, hT_r[kc]], writes=[pg.r])
                    u_sb = fpool.alloc()
                    Sx.add(ACT, ACP(u_sb.t[:], pu.t[:]), reads=[pu.r], writes=[u_sb.r])
                    ppool.free(pu)
                    c0 = fpool.alloc()
                    wc = sm0 + SM_WCONV
                    w2 = smT[:, wc + 2 * NFC + f:wc + 2 * NFC + f + 1]
                    w1 = smT[:, wc + NFC + f:wc + NFC + f + 1]
                    w0 = smT[:, wc + f:wc + f + 1]
                    Sx.add(DVE, TS(c0.t[:], u_sb.t[:], w2, smT[:, sm0 + SM_BCONV + f:sm0 + SM_BCONV + f + 1], ALU.mult, ALU.add),
                           reads=[u_sb.r, smT_r], writes=[c0.r])
                    Sx.add(DVE, STT(c0.t[:, 1:TB], u_sb.t[:, 0:TB - 1], w1, c0.t[:, 1:TB], ALU.mult, ALU.add),
                           reads=[u_sb.r, smT_r, c0.r], writes=[c0.r])
                    Sx.add(DVE, STT(c0.t[:, 2:TB], u_sb.t[:, 0:TB - 2], w0, c0.t[:, 2:TB], ALU.mult, ALU.add),
                           reads=[u_sb.r, smT_r, c0.r], writes=[c0.r])
                    if tb > 0:
                        Sx.add(DVE, STT(c0.t[:, 0:1], ucarry[:, f, 1:2], w1, c0.t[:, 0:1], ALU.mult, ALU.add),
                               reads=[ucarry_r[f], smT_r, c0.r], writes=[c0.r])
                        Sx.add(DVE, STT(c0.t[:, 0:2], ucarry[:, f, 0:2], w0, c0.t[:, 0:2], ALU.mult, ALU.add),
                               reads=[ucarry_r[f], smT_r, c0.r], writes=[c0.r])
                    Sx.add(DVE, CP(ucarry[:, f, :], u_sb.t[:, TB - 2:TB]), reads=[u_sb.r], writes=[ucarry_r[f]])
                    fpool.free(u_sb)
                    Sx.add(ACT, ACTF(c0.t[:], c0.t[:], AF.Silu), reads=[c0.r], writes=[c0.r])
                    c3(9)
                    Sx.add(DVE, TT(region[:, f, :], pg.t[:], c0.t[:], ALU.mult), reads=[pg.r, c0.r], writes=[reg_r[f]])
                    ppool.free(pg)
                    fpool.free(c0)
                wpool.free(wu)
                wpool.free(wg)
            for ch in range(2):
                if ch == 1 and between is not None:
                    between()
                accs = [ppool.alloc() for _ in range(4)]
                fc0 = 0
                for g, nfc in enumerate((8, 8, 6)):
                    ws = wload(l, "dn%d_%d" % (ch, g))
                    wv = ws.t[:, 0:nfc * 512].rearrange("p (k n) -> p k n", k=nfc)
                    for k in range(nfc):
                        fc = fc0 + k
                        for dl in range(4):
                            Sx.add(PE, MM(accs[dl].t[:], wv[:, k, dl * 128:(dl + 1) * 128], region[:, fc, :],
                                          start=(fc == 0), stop=(fc == NFC - 1)),
                                   reads=[ws.r, reg_r[fc]], writes=[accs[dl].r])
                    wpool.free(ws)
                    fc0 += nfc
                for dl in range(4):
                    dc = ch * 4 + dl
                    xs = xT[:, dc, tb * TB:(tb + 1) * TB]
                    Sx.add(DVE, STT(xs, accs[dl].t[:], modT[:, 40 + dc:41 + dc], xs, ALU.mult, ALU.add),
                           reads=[accs[dl].r, modT_r, xT_r[dc][tb]], writes=[xT_r[dc][tb]])
                    ppool.free(accs[dl])
            Sx.muted = False

        def emit_out(tb):
            for t in range(tb * 4, tb * 4 + 4):
                for half in range(2):
                    pb = ppool.alloc()
                    for j in range(4):
                        c = half * 4 + j
                        Sx.add(PE, TR(pb.t[:, j * 128:(j + 1) * 128], xT[:, c, t * 128:(t + 1) * 128], identf),
                               reads=[xT_r[c][t // 4], constf_r], writes=[pb.r])
                    so = fpool.alloc()
                    eng = ACT if half == 0 else DVE
                    fn_ = ACP(so.t[:], pb.t[:]) if eng == ACT else CP(so.t[:], pb.t[:])
                    Sx.add(eng, fn_, reads=[pb.r], writes=[so.r])
                    ppool.free(pb)
                    op = Sx.dma(SP, "f%d" % so.idx, DMA(out_d[t * 128:(t + 1) * 128, half * 512:(half + 1) * 512], so.t[:]), reads=[so.r])
                    final_ops.append(op)
                    fpool.free(so)

        import os
        STOP = int(os.environ.get("MK_STOP", "99"))
        NTB_RUN = int(os.environ.get("MK_NTB", str(NTB)))
        emit_mod(0)
        emit_norm(0, 0, 1)
        for l in range(n_layers):
            for tb in range(NTB):
                last = (tb == NTB - 1)
                nxt_mod = mod_gen(l + 1) if (last and l + 1 < n_layers) else None
                emit_mixers(l, tb, background=nxt_mod)
                if l == n_layers - 1 and tb > 0:
                    emit_out(tb - 1)
                if last and l + 1 < n_layers:
                    emit_layer_small_loads(l + 1)
                emit_gate(l, tb)
                emit_norm(l, tb, 2)
                if not last:
                    nxt = (lambda l_=l, t_=tb + 1: emit_norm(l_, t_, 1))
                elif l + 1 < n_layers:
                    nxt = (lambda l_=l + 1: emit_norm(l_, 0, 1))
                else:
                    nxt = None
                emit_ffn(l, tb, between=nxt)

        emit_out(NTB - 1)
        print("sbuf bytes remaining:", nc.sbuf_bytes_remaining, "ops:", {e: len(Sx.q[e]) for e in ENGS})
        Sx.emit(final_wait_ops=final_ops)
    return nc


_CACHE = {}


def _small_params(norm1_g, norm2_g, b_ada, b_mg, b_conv, w_conv, ret_norm_g, gla_norm_g, q_norm_g, k_norm_g, layers):
    L = len(layers)
    sm = np.zeros((128, L * NSM), np.float32)
    for i, l in enumerate(layers):
        o = i * NSM
        sm[:, o + SM_N1:o + SM_N1 + 8] = norm1_g[l].reshape(8, 128).T
        sm[:, o + SM_N2:o + SM_N2 + 8] = norm2_g[l].reshape(8, 128).T
        sm[:, o + SM_BADA:o + SM_BADA + 48] = b_ada[l].reshape(48, 128).T
        sm[:, o + SM_BMG:o + SM_BMG + 24] = b_mg[l].reshape(24, 128).T
        sm[:, o + SM_BCONV:o + SM_BCONV + 22] = b_conv[l].reshape(22, 128).T
        sm[:, o + SM_WCONV:o + SM_WCONV + 66] = w_conv[l].reshape(66, 128).T
        sm[:, o + SM_RETG:o + SM_RETG + 4] = ret_norm_g[l].reshape(4, 128).T
        sm[:, o + SM_GLAG] = gla_norm_g[l]
        sm[:, o + SM_QG] = q_norm_g[l]
        sm[:, o + SM_KG] = k_norm_g[l]
    return sm


def _prep_common(inputs, layers):
    L = len(layers)
    idx = list(layers)
    wall, offs, sizes, total = host_pack_weights(inputs["w_ada"][idx], inputs["w_in"][idx], inputs["w_br"][idx],
                                                 inputs["w_mg"][idx], inputs["w_o"][idx], inputs["w_up"][idx], inputs["w_down"][idx])
    wsm = np.zeros((L, 128, KC, 32), np.float32)
    for i, l in enumerate(layers):
        wi = inputs["w_in"][l]
        wsm[i, :, :, 0:16] = wi[:, 3072:3088].reshape(8, 128, 16).transpose(1, 0, 2)
        wsm[i, :, :, 16:20] = wi[:, 5136:5140].reshape(8, 128, 4).transpose(1, 0, 2)
    wsm = wsm.reshape(L, 128, KC * 32)
    wa2 = np.ascontiguousarray(inputs["w_gla_a2"][idx])
    sm = _small_params(inputs["norm1_g"], inputs["norm2_g"], inputs["b_ada"], inputs["b_mg"], inputs["b_conv"], inputs["w_conv"],
                       inputs["ret_norm_g"], inputs["gla_norm_g"], inputs["q_norm_g"], inputs["k_norm_g"], layers)
    bc = np.zeros((128, L * NBC), np.float32)
    for i, l in enumerate(layers):
        bc[:, i * NBC:i * NBC + 256] = inputs["b_gla_a"][l][None, :]
        bc[:, i * NBC + 256:i * NBC + 260] = inputs["b_fox_f"][l][None, :]
    c = host_consts()
    constf = np.concatenate([c[n].reshape(128, -1) for n, _ in CONST_F32], axis=1).astype(np.float32)
    rope = np.stack([c["rope_cos"], c["rope_sin"]], axis=1).astype(np.float32)
    common = {"wall": wall, "wsm": wsm, "wa2": wa2, "smT": sm, "bcp": bc, "constf": constf, "rope": rope}
    return common, offs, sizes, total


def run_layers(x, c, inputs, layers, dbg=None, ncores=NB, trace=False):
    common, offs, sizes, total = _prep_common(inputs, layers)
    import os
    key = (len(layers), total, str(dbg), os.environ.get("MK_STOP"), os.environ.get("MK_NTB"))
    if key not in _CACHE:
        _CACHE[key] = build_program(len(layers), total, offs, sizes, dbg=dbg)
    nc = _CACHE[key]
    in_maps = []
    for b in range(ncores):
        m = dict(common)
        m["x"] = np.ascontiguousarray(x[b])
        m["cT"] = np.ascontiguousarray(c[b].reshape(8, 128).T)
        in_maps.append(m)
    res = run_bass_kernel_spmd(nc, in_maps, core_ids=list(range(ncores)), **({"trace": True} if trace else {}))
    return res


def kernel(**inputs):
    inputs = {k: np.asarray(v) for k, v in inputs.items()}
    x = np.ascontiguousarray(inputs["x"], dtype=np.float32)
    c = np.ascontiguousarray(inputs["c"], dtype=np.float32)
    res = run_layers(x, c, inputs, list(range(DEPTH)))
    out = np.stack([np.asarray(r["out"]) for r in res.results], axis=0)
    return out.astype(np.float32)
```

```python
import contextlib
import numpy as np
import concourse.bass as bass
import concourse.mybir as mybir
from concourse.bass_utils import run_bass_kernel_spmd

F32 = mybir.dt.float32
BF16 = mybir.dt.bfloat16
AF = mybir.ActivationFunctionType
ALU = mybir.AluOpType

PE, ACT, DVE, POOL, SP = "pe", "act", "dve", "pool", "sp"
ENGS = [PE, ACT, DVE, POOL, SP]

D = 1024
S = 2048
NB = 8
DEPTH = 4
KC = 8
TB = 512
NTB = S // TB
NT = S // 128
D_FF = 2816
NFC = D_FF // 128
IN_W = 5140
EPS = 1e-6
WBUF_ELEMS = 4096
N_WBUF = 4
N_F32 = 8
N_BF = 14
NSM = 184
SM_N1, SM_N2, SM_BADA, SM_BMG, SM_BCONV, SM_WCONV, SM_RETG, SM_GLAG, SM_QG, SM_KG = 0, 8, 16, 64, 88, 110, 176, 180, 181, 182
DBG_TB = 0
NBC = 260


class Res:
    __slots__ = ("name", "w", "rs", "dma_rs", "excl")

    def __init__(self, name, excl=False):
        self.name = name
        self.w = None
        self.rs = {}
        self.dma_rs = []
        self.excl = excl


class Op:
    __slots__ = ("eng", "fn", "deps", "pos", "signal", "sigval", "is_dma", "sem", "semval")

    def __init__(self, eng, fn, is_dma=False):
        self.eng = eng
        self.fn = fn
        self.deps = ()
        self.pos = 0
        self.signal = False
        self.sigval = 0
        self.is_dma = is_dma
        self.sem = None
        self.semval = 0


class Sched:
    def __init__(self, nc):
        self.nc = nc
        self.q = {e: [] for e in ENGS}
        self.dma_sems = {}

    def _track(self, op, reads, writes):
        deps = {}
        for r in reads:
            if r.w is not None:
                deps[id(r.w)] = r.w
            if r.excl:
                for e_, o in r.rs.items():
                    if e_ != op.eng:
                        deps[id(o)] = o
        for w in writes:
            if w.w is not None:
                deps[id(w.w)] = w.w
            for o in w.rs.values():
                deps[id(o)] = o
            for o in w.dma_rs:
                deps[id(o)] = o
        deps.pop(id(op), None)
        op.deps = tuple(deps.values())
        for r in reads:
            if op.is_dma:
                r.dma_rs.append(op)
            else:
                r.rs[op.eng] = op
        for w in writes:
            w.w = op
            w.rs = {}
            w.dma_rs = []

    muted = False

    def add(self, eng, fn, reads=(), writes=()):
        if self.muted:
            return None
        op = Op(eng, fn)
        op.pos = len(self.q[eng])
        self.q[eng].append(op)
        self._track(op, reads, writes)
        return op

    def dma(self, queue, semkey, fn, reads=(), writes=()):
        if self.muted:
            return None
        op = Op(queue, fn, is_dma=True)
        op.pos = len(self.q[queue])
        self.q[queue].append(op)
        st = self.dma_sems.setdefault(semkey, [0, queue])
        assert st[1] == queue
        st[0] += 16
        op.sem = semkey
        op.semval = st[0]
        self._track(op, reads, writes)
        return op

    def emit(self, final_wait_ops=()):
        nc = self.nc
        for e in ENGS:
            for op in self.q[e]:
                for d in op.deps:
                    if d.is_dma:
                        continue
                    if d.eng != op.eng:
                        d.signal = True
                    elif op.eng != PE:
                        d.signal = True
        for e in ENGS:
            c = 0
            for op in self.q[e]:
                if op.signal and not op.is_dma:
                    c += 1
                    op.sigval = c
        with contextlib.ExitStack() as st:
            esem = {e: st.enter_context(nc.semaphore("s_" + e)) for e in ENGS}
            dsem = {k: st.enter_context(nc.semaphore("d_%s" % (k,))) for k in self.dma_sems}
            block = st.enter_context(nc.Block())
            sched = self

            def run(e, eng):
                waited = {}
                for op in sched.q[e]:
                    for d in op.deps:
                        if d.is_dma:
                            key = ("d", d.sem)
                            val = d.semval
                            sem = dsem[d.sem]
                        else:
                            if d.eng == e and e == PE:
                                continue
                            key = ("e", d.eng)
                            val = d.sigval
                            sem = esem[d.eng]
                        if waited.get(key, 0) >= val:
                            continue
                        waited[key] = val
                        eng.wait_ge(sem, val)
                    ins = op.fn(eng)
                    if op.is_dma:
                        ins.then_inc(dsem[op.sem], 16)
                    elif op.signal:
                        ins.then_inc(esem[e], 1)
                if e == SP:
                    for d in final_wait_ops:
                        if d is not None:
                            eng.wait_ge(dsem[d.sem], d.semval)

            @block.tensor
            def _(eng):
                run(PE, eng)

            @block.scalar
            def _(eng):
                run(ACT, eng)

            @block.vector
            def _(eng):
                run(DVE, eng)

            @block.gpsimd
            def _(eng):
                run(POOL, eng)

            @block.sync
            def _(eng):
                run(SP, eng)


class Slot:
    __slots__ = ("t", "r", "idx")

    def __init__(self, t, r, idx):
        self.t = t
        self.r = r
        self.idx = idx


class Pool_:
    def __init__(self, slots, name):
        self.free_list = list(slots)
        self.name = name

    def alloc(self):
        assert self.free_list, "pool %s exhausted" % self.name
        return self.free_list.pop(0)

    def free(self, s):
        self.free_list.append(s)


def MM(out, lhsT, rhs, start=True, stop=True):
    return lambda e: e.matmul(out, lhsT, rhs, start=start, stop=stop)


def TR(out, in_, ident):
    return lambda e: e.transpose(out, in_, ident)


def ACTF(out, in_, func, bias=None, scale=None):
    kw = {}
    if bias is not None:
        kw["bias"] = bias
    if scale is not None:
        kw["scale"] = scale
    return lambda e: e.activation(out, in_, func, **kw)


def ACP(out, in_):
    return lambda e: e.copy(out, in_)


def TT(out, a, b, op):
    return lambda e: e.tensor_tensor(out, a, b, op)


def TS(out, a, s1, s2, op0, op1=None):
    if op1 is None:
        return lambda e: e.tensor_scalar(out, a, s1, s2, op0)
    return lambda e: e.tensor_scalar(out, a, s1, s2, op0, op1)


def STT(out, in0, scalar, in1, op0, op1):
    return lambda e: e.scalar_tensor_tensor(out, in0, scalar, in1, op0, op1)


def CP(out, in_):
    return lambda e: e.tensor_copy(out, in_)


def MEMSET(ap, v):
    return lambda e: e.memset(ap, v)


def DMA(out, in_):
    return lambda e: e.dma_start(out=out, in_=in_)


def weight_stream_layout():
    head = [("ada%d" % i, 4096) for i in range(12)]
    tb = [("ret%d" % h, 4096) for h in range(4)]
    tb += [("gla%d" % h, 8 * 384) for h in range(4)]
    tb += [("fox%d" % h, 8 * 384) for h in range(4)]
    for n in range(8):
        tb += [("mg%d" % n, 3072), ("br%d" % n, 1536)]
    tb += [("wo0", 4096), ("wo1", 4096)]
    for i in range(6):
        wdt = 512 if i < 5 else 256
        tb += [("up_u%d" % i, 8 * wdt), ("up_g%d" % i, 8 * wdt)]
    for ch in range(2):
        for g, nfc in enumerate((8, 8, 6)):
            tb.append(("dn%d_%d" % (ch, g), nfc * 512))
    return head, tb


def host_pack_weights(w_ada, w_in, w_br, w_mg, w_o, w_up, w_down):
    L = w_in.shape[0]
    head, tb = weight_stream_layout()
    names = [n for n, _ in head] + [n for n, _ in tb]
    sizes = dict(head + tb)
    offs = {}
    o = 0
    for n in names:
        offs[n] = o
        o += sizes[n]
    total = o
    out = np.empty((L, 128, total), np.float32)

    def kcp(w2d):
        n = w2d.shape[1]
        return w2d.reshape(8, 128, n).transpose(1, 0, 2)

    for l in range(L):
        dst = out[l]

        def put(name, arr):
            a = arr.reshape(128, -1)
            assert a.shape[1] == sizes[name], (name, a.shape, sizes[name])
            dst[:, offs[name]:offs[name] + sizes[name]] = a

        for i in range(12):
            put("ada%d" % i, kcp(w_ada[l][:, i * 512:(i + 1) * 512]))
        wi = w_in[l]
        for h in range(4):
            cols = np.concatenate([np.arange(0 + 128 * h, 128 * h + 128), np.arange(512 + 128 * h, 512 + 128 * h + 128),
                                   np.arange(1024 + 128 * h, 1024 + 128 * h + 128), np.arange(1536 + 128 * h, 1536 + 128 * h + 128)])
            put("ret%d" % h, kcp(wi[:, cols]))
            cols = np.concatenate([np.arange(2048 + 64 * h, 2048 + 64 * h + 64), np.arange(2304 + 64 * h, 2304 + 64 * h + 64),
                                   np.arange(2560 + 128 * h, 2560 + 128 * h + 128), np.arange(3088 + 128 * h, 3088 + 128 * h + 128)])
            put("gla%d" % h, kcp(wi[:, cols]))
            cols = np.concatenate([np.arange(3600 + 128 * h, 3600 + 128 * h + 128), np.arange(4112 + 128 * h, 4112 + 128 * h + 128),
                                   np.arange(4624 + 128 * h, 4624 + 128 * h + 128)])
            put("fox%d" % h, kcp(wi[:, cols]))
        mg = w_mg[l].reshape(8, 128, 3, 8, 128)
        br = w_br[l].reshape(3, 4, 128, 8, 128)
        for n in range(8):
            put("mg%d" % n, mg[:, :, :, n, :].transpose(1, 0, 2, 3))
            put("br%d" % n, br[:, :, :, n, :].transpose(2, 0, 1, 3))
        put("wo0", kcp(w_o[l][:, 0:512]))
        put("wo1", kcp(w_o[l][:, 512:1024]))
        for i in range(6):
            wdt = 512 if i < 5 else 256
            put("up_u%d" % i, kcp(w_up[l][:, i * 512:i * 512 + wdt]))
            put("up_g%d" % i, kcp(w_up[l][:, D_FF + i * 512:D_FF + i * 512 + wdt]))
        wd = w_down[l].reshape(NFC, 128, D)
        fc0 = 0
        for g, nfc in enumerate((8, 8, 6)):
            for ch in range(2):
                put("dn%d_%d" % (ch, g), wd[fc0:fc0 + nfc, :, ch * 512:(ch + 1) * 512].transpose(1, 0, 2))
            fc0 += nfc
    return out, offs, sizes, total


def host_consts():
    c = {}
    half = 64
    inv_freq = 10000.0 ** (-np.arange(half, dtype=np.float64) / half)
    pos = np.arange(S, dtype=np.float64)
    ang = inv_freq[:, None] * pos[None, :]
    cos = np.cos(ang)
    sin = np.sin(ang)
    c["rope_cos"] = np.concatenate([cos, cos], 0).astype(np.float32)
    c["rope_sin"] = np.concatenate([sin, -sin], 0).astype(np.float32)
    log_g = np.log1p(-np.exp2(-5.0 - np.arange(4, dtype=np.float64)))
    sc = 128.0 ** -0.5
    i = np.arange(128)
    wt = np.zeros((4, 128, 128), np.float64)
    for h in range(4):
        jj, ii = np.meshgrid(i, i, indexing="ij")
        same = (jj // 64) == (ii // 64)
        later = (ii // 64) > (jj // 64)
        w = np.where(same, np.exp(np.abs(ii - jj) * log_g[h]), np.where(later, np.exp((ii - jj) * log_g[h]), 0.0))
        wt[h] = w * sc
    c["ret_wt"] = wt.transpose(1, 0, 2).reshape(128, 512).astype(np.float32)
    qd = np.stack([np.exp((i + 1.0) * log_g[h]) * sc for h in range(4)], 0)
    c["ret_qdec"] = np.broadcast_to(qd.reshape(1, 512), (128, 512)).astype(np.float32).copy()
    kw = np.stack([np.exp((127.0 - i) * log_g[h]) for h in range(4)], 1)
    c["ret_kw"] = kw.astype(np.float32)
    c["ret_g128"] = [float(np.exp(128.0 * log_g[h])) for h in range(4)]
    ident = np.eye(128, dtype=np.float32)
    c["ident"] = ident
    jj, ii = np.meshgrid(i, i, indexing="ij")
    c["tri_incl"] = (jj <= ii).astype(np.float32)
    c["t3"] = ((jj > ii) & ((jj // 64) == (ii // 64))).astype(np.float32)
    c["chunk_ind"] = np.stack([(i < 64), (i >= 64)], 1).astype(np.float32)
    c["maskT"] = np.where(jj > ii, -30000.0, 0.0).astype(np.float32)
    return c


CONST_F32 = [("ident", 128), ("tri_incl", 128), ("t3", 128), ("chunk_ind", 2), ("ret_wt", 512),
             ("ret_qdec", 512), ("ret_kw", 4), ("maskT", 128)]


def build_program(n_layers, layer_elems, offs, sizes, dbg=None):
    nc = bass.Bass("TRN2", target_bir_lowering=False)
    consts = host_consts()
    g128 = consts["ret_g128"]

    def dram(name, shape, dt=F32, kind="ExternalInput"):
        return nc.dram_tensor(name, list(shape), dt, kind=kind).ap()

    x_d = dram("x", [S, D])
    out_d = dram("out", [S, D], kind="ExternalOutput")
    cT_d = dram("cT", [128, KC])
    wall_d = dram("wall", [n_layers, 128, layer_elems])
    wsm_d = dram("wsm", [n_layers, 128, KC * 32])
    wa2_d = dram("wa2", [n_layers, 16, 256])
    smT_d = dram("smT", [128, n_layers * NSM])
    bc_d = dram("bcp", [128, n_layers * NBC])
    cf_w = sum(w for _, w in CONST_F32)
    constf_d = dram("constf", [128, cf_w])
    rope_d = dram("rope", [128, 2, S])
    dbg_d = {}
    if dbg:
        for name, shape, dt in dbg:
            dbg_d[name] = dram("dbg_" + name, shape, dt, kind="ExternalOutput")

    with contextlib.ExitStack() as st:
        def sb(name, shape, dt):
            return st.enter_context(nc.sbuf_tensor("sb_" + name, list(shape), dt))

        Sx = Sched(nc)
        xT = sb("xT", [128, KC, S], F32)
        xT_r = [[Res("xT%d_%d" % (c, t)) for t in range(NTB)] for c in range(KC)]
        hT = sb("hT", [128, KC, TB], BF16)
        hT_r = [Res("hT%d" % c) for c in range(KC)]
        region = sb("region", [128, NFC, TB], BF16)
        reg_r = [Res("reg%d" % i) for i in range(NFC)]
        kcache = sb("kcache", [128, 4, S], BF16)
        kc_r = [[Res("kc%d_%d" % (h, t)) for t in range(NTB)] for h in range(4)]
        vcache = sb("vcache", [128, NT, 512], BF16)
        vc_r = [[Res("vc%d_%d" % (h, t)) for t in range(NTB)] for h in range(4)]
        ctok = sb("ctok", [128, NT, 4], F32)
        ctok_r = [Res("ctok%d" % t) for t in range(NTB)]
        carry = sb("carry", [128, 4], F32)
        carry_r = Res("carry")
        wbufs = [Slot(sb("wb%d" % i, [128, WBUF_ELEMS], BF16), Res("wb%d" % i), i) for i in range(N_WBUF)]
        wpool = Pool_(wbufs, "w")
        f32s = [Slot(sb("f%d" % i, [128, TB], F32), Res("f%d" % i), i) for i in range(N_F32)]
        fpool = Pool_(f32s, "f32")
        bfs = [Slot(sb("b%d" % i, [128, TB], BF16), Res("b%d" % i), i) for i in range(N_BF)]
        bpool = Pool_(bfs, "bf16")
        banks = [Slot(st.enter_context(nc.psum_tensor("ps%d" % i, [128, TB], F32)), Res("ps%d" % i, excl=True), i) for i in range(8)]
        ppool = Pool_(banks, "psum")
        constf = sb("constf", [128, cf_w], F32)
        constf_r = Res("constf")
        coff = {}
        o = 0
        for n_, w_ in CONST_F32:
            coff[n_] = (o, w_)
            o += w_

        def CF(name, lo=0, hi=None):
            o_, w_ = coff[name]
            hi = w_ if hi is None else hi
            return constf[:, o_ + lo:o_ + hi]

        ident_bf = sb("ident_bf", [128, 128], BF16)
        maskT_bf = sb("maskT_bf", [128, 128], BF16)
        ones_bf = sb("ones_bf", [128, 128], BF16)
        onesD_bf = sb("onesD_bf", [128, 128], BF16)
        ones_f = sb("ones_f", [128, 128], F32)
        onesrow0_bf = sb("onesrow0_bf", [128, 128], BF16)
        cbf_r = Res("cbf")
        smT2 = [sb("smT%d" % i, [128, NSM], F32) for i in range(2)]
        smT2_r = [Res("smT%d" % i) for i in range(2)]
        cT = sb("cT", [128, KC], F32)
        cact = sb("cact", [128, KC], BF16)
        cact_r = Res("cact")
        modT2 = [sb("modT%d" % i, [128, 48], F32) for i in range(2)]
        modT2_r = [Res("modT%d" % i) for i in range(2)]
        lay2 = [sb("lay%d" % i, [128, 24], F32) for i in range(2)]
        lay2_r = [Res("lay%d" % i) for i in range(2)]
        bcp = sb("bcp", [128, NBC], F32)
        bcp_r = Res("bcp")
        wsm = sb("wsm", [128, KC, 32], BF16)
        wsm_r = Res("wsm")
        wa2 = sb("wa2", [16, 256], BF16)
        wa2_r = Res("wa2")
        eps_t = sb("eps_t", [128, 1], F32)
        eps_ap = eps_t[:, 0:1]
        Rst = sb("Rst", [128, 4, 128], F32)
        Rst_r = [Res("R%d" % h) for h in range(4)]
        Sst = sb("Sst", [64, 4, 128], F32)
        Sst_r = [Res("S%d" % h) for h in range(4)]
        ucarry = sb("ucarry", [128, NFC, 2], F32)
        ucarry_r = [Res("uc%d" % f) for f in range(NFC)]
        glrT = sb("glrT", [16, TB], BF16)
        glrT_r = Res("glrT")
        lp = sb("lp", [128, 4, 64], F32)
        lp_r = Res("lp")
        kdec = sb("kdec", [128, 4, 64], BF16)
        kdec_r = Res("kdec")
        a_gla = sb("a_gla", [64, 8], F32)
        agla_r = Res("agla")
        lpf = sb("lpf", [128, 16], F32)
        lpf_r = Res("lpf")

        final_ops = []

        def wload(l, name):
            s = wpool.alloc()
            n = sizes[name]
            o_ = offs[name]
            Sx.dma(POOL, "w%d" % s.idx, DMA(s.t[:, 0:n], wall_d[l, :, o_:o_ + n]), writes=[s.r])
            return s

        def dump(name, ap, reads):
            if name in dbg_d:
                op = Sx.dma(SP, "dbg_" + name, DMA(dbg_d[name], ap), reads=reads)
                final_ops.append(op)

        Sx.dma(SP, "constf", DMA(constf[:], constf_d), writes=[constf_r])
        Sx.dma(SP, "cT", DMA(cT[:], cT_d), writes=[cact_r])
        Sx.add(ACT, ACTF(cact[:], cT[:], AF.Silu), reads=[cact_r], writes=[cact_r])
        Sx.add(DVE, CP(ident_bf[:], CF("ident")), reads=[constf_r], writes=[cbf_r])
        Sx.add(DVE, CP(maskT_bf[:], CF("maskT")), reads=[constf_r], writes=[cbf_r])
        Sx.add(DVE, MEMSET(ones_bf[:], 1.0), writes=[cbf_r])
        Sx.add(DVE, MEMSET(onesD_bf[:], 1.0 / 128.0), writes=[cbf_r])
        Sx.add(DVE, MEMSET(ones_f[:], 1.0), writes=[cbf_r])
        Sx.add(DVE, MEMSET(onesrow0_bf[:], 0.0), writes=[cbf_r])
        Sx.add(DVE, MEMSET(onesrow0_bf[0:1, :], 1.0), writes=[cbf_r])
        Sx.add(DVE, MEMSET(eps_t[:], EPS), writes=[cbf_r])
        identf = CF("ident")

        for t in range(NT):
            s0 = fpool.alloc()
            s1 = fpool.alloc()
            Sx.dma(SP, "f%d" % s0.idx, DMA(s0.t[:], x_d[t * 128:(t + 1) * 128, 0:512]), writes=[s0.r])
            Sx.dma(SP, "f%d" % s1.idx, DMA(s1.t[:], x_d[t * 128:(t + 1) * 128, 512:1024]), writes=[s1.r])
            for half, sl in enumerate((s0, s1)):
                pb = ppool.alloc()
                for j in range(4):
                    Sx.add(PE, TR(pb.t[:, j * 128:(j + 1) * 128], sl.t[:, j * 128:(j + 1) * 128], identf),
                           reads=[sl.r, constf_r], writes=[pb.r])
                dst = xT[:, half * 4:(half + 1) * 4, t * 128:(t + 1) * 128]
                src = pb.t[:].rearrange("p (c n) -> p c n", c=4)
                eng = ACT if half == 0 else DVE
                fn = ACP(dst, src) if eng == ACT else CP(dst, src)
                Sx.add(eng, fn, reads=[pb.r], writes=[xT_r[c][t // 4] for c in range(half * 4, half * 4 + 4)])
                ppool.free(pb)
            fpool.free(s0)
            fpool.free(s1)

        def mod_gen(l):
            smT, smT_r, modT, modT_r, lay, lay_r = smT2[l % 2], smT2_r[l % 2], modT2[l % 2], modT2_r[l % 2], lay2[l % 2], lay2_r[l % 2]
            sm0 = 0
            Sx.dma(SP, "smT", DMA(smT[:], smT_d[:, l * NSM:(l + 1) * NSM]), writes=[smT_r])
            pm = ppool.alloc()
            for cb in range(12):
                ws = wload(l, "ada%d" % cb)
                wv = ws.t[:, 0:4096].rearrange("p (k n) -> p k n", k=KC)
                pb = ppool.alloc()
                for kc in range(KC):
                    Sx.add(PE, MM(pb.t[0:1, :], cact[:, kc:kc + 1], wv[:, kc, :], start=(kc == 0), stop=(kc == KC - 1)),
                           reads=[cact_r, ws.r], writes=[pb.r])
                row = fpool.alloc()
                Sx.add(ACT, ACP(row.t[0:1, :], pb.t[0:1, :]), reads=[pb.r], writes=[row.r])
                ppool.free(pb)
                wpool.free(ws)
                for jj in range(4):
                    j = cb * 4 + jj
                    Sx.add(PE, MM(pm.t[:, j:j + 1], row.t[0:1, jj * 128:(jj + 1) * 128], ones_f[0:1, 0:1]),
                           reads=[row.r, cbf_r], writes=[pm.r])
                fpool.free(row)
                yield
            Sx.add(DVE, TT(modT[:], pm.t[:, 0:48], smT[:, sm0 + SM_BADA:sm0 + SM_BADA + 48], ALU.add),
                   reads=[pm.r, smT_r], writes=[modT_r])
            ppool.free(pm)
            Sx.add(DVE, STT(lay[:, 0:8], modT[:, 8:16], 1.0, smT[:, sm0 + SM_N1:sm0 + SM_N1 + 8], ALU.add, ALU.mult),
                   reads=[modT_r, smT_r], writes=[lay_r])
            Sx.add(DVE, STT(lay[:, 8:16], modT[:, 32:40], 1.0, smT[:, sm0 + SM_N2:sm0 + SM_N2 + 8], ALU.add, ALU.mult),
                   reads=[modT_r, smT_r], writes=[lay_r])
            Sx.add(DVE, TS(lay[:, 16:17], smT[:, sm0 + SM_QG:sm0 + SM_QG + 1], 128.0 ** -0.5, None, ALU.mult),
                   reads=[smT_r], writes=[lay_r])

        def emit_layer_small_loads(l):
            Sx.dma(SP, "bcp", DMA(bcp[:], bc_d[:, l * NBC:(l + 1) * NBC]), writes=[bcp_r])
            Sx.dma(POOL, "wsm", DMA(wsm[:].rearrange("p k n -> p (k n)"), wsm_d[l]), writes=[wsm_r])
            Sx.dma(POOL, "wa2", DMA(wa2[:], wa2_d[l]), writes=[wa2_r])

        def emit_mod(l):
            for _ in mod_gen(l):
                pass
            emit_layer_small_loads(l)

        def emit_norm(l, tb, which):
            smT, smT_r, modT, modT_r, lay, lay_r = smT2[l % 2], smT2_r[l % 2], modT2[l % 2], modT2_r[l % 2], lay2[l % 2], lay2_r[l % 2]
            a_off = 0 if which == 1 else 8
            b_off = 0 if which == 1 else 24
            pb = ppool.alloc()
            for c in range(KC):
                sq = bpool.alloc()
                xs = xT[:, c, tb * TB:(tb + 1) * TB]
                Sx.add(ACT if c % 2 == 0 else DVE, (ACTF(sq.t[:], xs, AF.Square) if c % 2 == 0 else TT(sq.t[:], xs, xs, ALU.mult)), reads=[xT_r[c][tb]], writes=[sq.r])
                Sx.add(PE, MM(pb.t[:], ones_bf[:], sq.t[:], start=(c == 0), stop=(c == KC - 1)),
                       reads=[sq.r, cbf_r], writes=[pb.r])
                bpool.free(sq)
            rstd = fpool.alloc()
            Sx.add(ACT, ACTF(rstd.t[:], pb.t[:], AF.Ln, bias=eps_ap, scale=1.0 / D), reads=[pb.r, cbf_r], writes=[rstd.r])
            ppool.free(pb)
            Sx.add(ACT, ACTF(rstd.t[:], rstd.t[:], AF.Exp, scale=-0.5), reads=[rstd.r], writes=[rstd.r])
            for c in range(KC):
                tmp = fpool.alloc()
                xs = xT[:, c, tb * TB:(tb + 1) * TB]
                Sx.add(DVE, STT(tmp.t[:], xs, lay[:, a_off + c:a_off + c + 1], rstd.t[:], ALU.mult, ALU.mult),
                       reads=[xT_r[c][tb], lay_r, rstd.r], writes=[tmp.r])
                Sx.add(ACT, ACTF(hT[:, c, :], tmp.t[:], AF.Identity, bias=modT[:, b_off + c:b_off + c + 1]),
                       reads=[tmp.r, modT_r], writes=[hT_r[c]])
                fpool.free(tmp)
            fpool.free(rstd)

        def proj_F(ws, ncols_blk, col0, m, pb):
            wv = ws.t[:, 0:8 * ncols_blk].rearrange("p (k n) -> p k n", k=KC)
            for kc in range(KC):
                Sx.add(PE, MM(pb.t[0:m, :], wv[:, kc, col0:col0 + m], hT[:, kc, :], start=(kc == 0), stop=(kc == KC - 1)),
                       reads=[ws.r, hT_r[kc]], writes=[pb.r])

        def proj_T4(ws, ncols_blk, col0, n, pb):
            wv = ws.t[:, 0:8 * ncols_blk].rearrange("p (k n) -> p k n", k=KC)
            for t in range(4):
                for kc in range(KC):
                    Sx.add(PE, MM(pb.t[:, t * n:(t + 1) * n], hT[:, kc, t * 128:(t + 1) * 128], wv[:, kc, col0:col0 + n],
                                  start=(kc == 0), stop=(kc == KC - 1)),
                           reads=[ws.r, hT_r[kc]], writes=[pb.r])

        def part_norm_stats(src_f, want_mean):
            sq = bpool.alloc()
            Sx.add(DVE, TT(sq.t[:], src_f.t[:], src_f.t[:], ALU.mult), reads=[src_f.r], writes=[sq.r])
            p2 = ppool.alloc()
            Sx.add(PE, MM(p2.t[:], onesD_bf[:], sq.t[:]), reads=[sq.r, cbf_r], writes=[p2.r])
            bpool.free(sq)
            if want_mean:
                ob = bpool.alloc()
                Sx.add(ACT, ACP(ob.t[:], src_f.t[:]), reads=[src_f.r], writes=[ob.r])
                p1 = ppool.alloc()
                Sx.add(PE, MM(p1.t[:], onesD_bf[:], ob.t[:]), reads=[ob.r, cbf_r], writes=[p1.r])
                bpool.free(ob)
                mean = fpool.alloc()
                Sx.add(ACT, ACP(mean.t[:], p1.t[:]), reads=[p1.r], writes=[mean.r])
                ppool.free(p1)
                var = fpool.alloc()
                Sx.add(ACT, ACTF(var.t[:], mean.t[:], AF.Square), reads=[mean.r], writes=[var.r])
                Sx.add(DVE, TT(var.t[:], p2.t[:], var.t[:], ALU.subtract), reads=[p2.r, var.r], writes=[var.r])
                ppool.free(p2)
                Sx.add(ACT, ACTF(var.t[:], var.t[:], AF.Ln, bias=eps_ap), reads=[var.r, cbf_r], writes=[var.r])
                Sx.add(ACT, ACTF(var.t[:], var.t[:], AF.Exp, scale=-0.5), reads=[var.r], writes=[var.r])
                return mean, var
            rstd = fpool.alloc()
            Sx.add(ACT, ACTF(rstd.t[:], p2.t[:], AF.Ln, bias=eps_ap), reads=[p2.r, cbf_r], writes=[rstd.r])
            ppool.free(p2)
            Sx.add(ACT, ACTF(rstd.t[:], rstd.t[:], AF.Exp, scale=-0.5), reads=[rstd.r], writes=[rstd.r])
            return None, rstd

        import os as _os

        def rotary(pb, dst, cos_s, sin_s):
            t1 = fpool.alloc()
            t2 = fpool.alloc()
            Sx.add(DVE, TT(t1.t[:], pb.t[:], cos_s.t[:], ALU.mult), reads=[pb.r, cos_s.r], writes=[t1.r])
            Sx.add(DVE, TT(t2.t[0:64, :], pb.t[64:128, :], sin_s.t[64:128, :], ALU.mult), reads=[pb.r, sin_s.r], writes=[t2.r])
            Sx.add(DVE, TT(t2.t[64:128, :], pb.t[0:64, :], sin_s.t[0:64, :], ALU.mult), reads=[pb.r, sin_s.r], writes=[t2.r])
            Sx.add(DVE, TT(dst.t[:], t1.t[:], t2.t[:], ALU.add), reads=[t1.r, t2.r], writes=[dst.r])
            fpool.free(t1)
            fpool.free(t2)

        def ret_head(l, tb, h):
            smT, smT_r, modT, modT_r, lay, lay_r = smT2[l % 2], smT2_r[l % 2], modT2[l % 2], modT2_r[l % 2], lay2[l % 2], lay2_r[l % 2]
            sm0 = 0
            ws = wload(l, "ret%d" % h)
            cos_s = fpool.alloc()
            sin_s = fpool.alloc()
            Sx.dma(SP, "f%d" % cos_s.idx, DMA(cos_s.t[:], rope_d[:, 0, tb * TB:(tb + 1) * TB]), writes=[cos_s.r])
            Sx.dma(SP, "f%d" % sin_s.idx, DMA(sin_s.t[:], rope_d[:, 1, tb * TB:(tb + 1) * TB]), writes=[sin_s.r])
            q_rot = bpool.alloc()
            k_rot = bpool.alloc()
            pb = ppool.alloc()
            proj_F(ws, 512, 0, 128, pb)
            rotary(pb, q_rot, cos_s, sin_s)
            ppool.free(pb)
            pb = ppool.alloc()
            proj_F(ws, 512, 128, 128, pb)
            rotary(pb, k_rot, cos_s, sin_s)
            ppool.free(pb)
            fpool.free(cos_s)
            fpool.free(sin_s)
            pb = ppool.alloc()
            proj_T4(ws, 512, 256, 128, pb)
            v_t = bpool.alloc()
            Sx.add(ACT, ACP(v_t.t[:], pb.t[:]), reads=[pb.r], writes=[v_t.r])
            ppool.free(pb)
            pb = ppool.alloc()
            proj_F(ws, 512, 384, 128, pb)
            wpool.free(ws)
            rgs = fpool.alloc()
            Sx.add(ACT, ACTF(rgs.t[:], pb.t[:], AF.Silu), reads=[pb.r], writes=[rgs.r])
            ppool.free(pb)
            yield
            pk = ppool.alloc()
            pk_bf = pk.t[:].bitcast(BF16)
            for t in range(4):
                Sx.add(PE, TR(pk_bf[:, t * 128:(t + 1) * 128], k_rot.t[:, t * 128:(t + 1) * 128], ident_bf[:]),
                       reads=[k_rot.r, cbf_r], writes=[pk.r])
            kwtok = bpool.alloc()
            Sx.add(DVE, TS(kwtok.t[:], pk_bf[:, 0:512], CF("ret_kw", h, h + 1), None, ALU.mult),
                   reads=[pk.r, constf_r], writes=[kwtok.r])
            ppool.free(pk)
            ps_s = ppool.alloc()
            for t in range(4):
                sl = slice(t * 128, (t + 1) * 128)
                Sx.add(PE, MM(ps_s.t[:, sl], k_rot.t[:, sl], q_rot.t[:, sl]),
                       reads=[k_rot.r, q_rot.r], writes=[ps_s.r])
            pt = bpool.alloc()
            for t in range(4):
                sl = slice(t * 128, (t + 1) * 128)
                Sx.add(DVE, TT(pt.t[:, sl], ps_s.t[:, sl], CF("ret_wt", h * 128, (h + 1) * 128), ALU.mult),
                       reads=[ps_s.r, constf_r], writes=[pt.r])
            ppool.free(ps_s)
            qw = bpool.alloc()
            for t in range(4):
                sl = slice(t * 128, (t + 1) * 128)
                Sx.add(DVE, TT(qw.t[:, sl], q_rot.t[:, sl], CF("ret_qdec", h * 128, (h + 1) * 128), ALU.mult),
                       reads=[q_rot.r, constf_r], writes=[qw.r])
            bpool.free(q_rot)
            bpool.free(k_rot)
            yield
            ps_kv = ppool.alloc()
            for t in range(4):
                sl = slice(t * 128, (t + 1) * 128)
                Sx.add(PE, MM(ps_kv.t[:, sl], kwtok.t[:, sl], v_t.t[:, sl]),
                       reads=[kwtok.r, v_t.r], writes=[ps_kv.r])
            bpool.free(kwtok)
            rb = bpool.alloc()
            for t in range(4):
                sl = slice(t * 128, (t + 1) * 128)
                gt = tb * 4 + t
                if gt > 0:
                    Sx.add(ACT, ACP(rb.t[:, sl], Rst[:, h, :]), reads=[Rst_r[h]], writes=[rb.r])
                if gt == 0:
                    Sx.add(DVE, CP(Rst[:, h, :], ps_kv.t[:, sl]), reads=[ps_kv.r], writes=[Rst_r[h]])
                else:
                    Sx.add(DVE, STT(Rst[:, h, :], Rst[:, h, :], g128[h], ps_kv.t[:, sl], ALU.mult, ALU.add),
                           reads=[ps_kv.r, Rst_r[h]], writes=[Rst_r[h]])
            ppool.free(ps_kv)
            yield
            ps_o = ppool.alloc()
            for t in range(4):
                sl = slice(t * 128, (t + 1) * 128)
                gt = tb * 4 + t
                Sx.add(PE, MM(ps_o.t[:, sl], v_t.t[:, sl], pt.t[:, sl], start=True, stop=(gt == 0)),
                       reads=[v_t.r, pt.r], writes=[ps_o.r])
                if gt > 0:
                    Sx.add(PE, MM(ps_o.t[:, sl], rb.t[:, sl], qw.t[:, sl], start=False, stop=True),
                           reads=[rb.r, qw.r], writes=[ps_o.r])
            for s_ in (pt, qw, v_t, rb):
                bpool.free(s_)
            o_f = fpool.alloc()
            Sx.add(ACT, ACP(o_f.t[:], ps_o.t[:]), reads=[ps_o.r], writes=[o_f.r])
            ppool.free(ps_o)
            ob = bpool.alloc()
            Sx.add(ACT, ACP(ob.t[:], o_f.t[:]), reads=[o_f.r], writes=[ob.r])
            yield
            p1 = ppool.alloc()
            Sx.add(PE, MM(p1.t[:], onesD_bf[:], ob.t[:]), reads=[ob.r, cbf_r], writes=[p1.r])
            bpool.free(ob)
            Sx.add(DVE, TT(o_f.t[:], o_f.t[:], p1.t[:], ALU.subtract), reads=[o_f.r, p1.r], writes=[o_f.r])
            ppool.free(p1)
            sq = bpool.alloc()
            Sx.add(ACT, ACTF(sq.t[:], o_f.t[:], AF.Square), reads=[o_f.r], writes=[sq.r])
            yield
            p2 = ppool.alloc()
            Sx.add(PE, MM(p2.t[:], onesD_bf[:], sq.t[:]), reads=[sq.r, cbf_r], writes=[p2.r])
            bpool.free(sq)
            rstd = fpool.alloc()
            Sx.add(ACT, ACTF(rstd.t[:], p2.t[:], AF.Ln, bias=eps_ap), reads=[p2.r, cbf_r], writes=[rstd.r])
            ppool.free(p2)
            Sx.add(ACT, ACTF(rstd.t[:], rstd.t[:], AF.Exp, scale=-0.5), reads=[rstd.r], writes=[rstd.r])
            Sx.add(DVE, TT(o_f.t[:], o_f.t[:], rstd.t[:], ALU.mult), reads=[o_f.r, rstd.r], writes=[o_f.r])
            Sx.add(DVE, STT(region[:, 8 + h, :], o_f.t[:], smT[:, sm0 + SM_RETG + h:sm0 + SM_RETG + h + 1], rgs.t[:], ALU.mult, ALU.mult),
                   reads=[o_f.r, smT_r, rgs.r], writes=[reg_r[8 + h]])
            for s_ in (rstd, o_f, rgs):
                fpool.free(s_)

        def rms_tail(o_f, gain_ap, extra_reads, post):
            sq = bpool.alloc()
            Sx.add(ACT, ACTF(sq.t[:], o_f.t[:], AF.Square), reads=[o_f.r], writes=[sq.r])
            yield
            p2 = ppool.alloc()
            Sx.add(PE, MM(p2.t[:], onesD_bf[:], sq.t[:]), reads=[sq.r, cbf_r], writes=[p2.r])
            bpool.free(sq)
            rstd = fpool.alloc()
            Sx.add(ACT, ACTF(rstd.t[:], p2.t[:], AF.Ln, bias=eps_ap), reads=[p2.r, cbf_r], writes=[rstd.r])
            ppool.free(p2)
            Sx.add(ACT, ACTF(rstd.t[:], rstd.t[:], AF.Exp, scale=-0.5), reads=[rstd.r], writes=[rstd.r])
            post(rstd)
            fpool.free(rstd)

        def gla_pre(l, tb):
            pb = ppool.alloc()
            for kc in range(KC):
                Sx.add(PE, MM(pb.t[0:16, :], wsm[:, kc, 0:16], hT[:, kc, :], start=(kc == 0), stop=(kc == KC - 1)),
                       reads=[wsm_r, hT_r[kc]], writes=[pb.r])
            Sx.add(ACT, ACP(glrT[:], pb.t[0:16, :]), reads=[pb.r], writes=[glrT_r])
            ppool.free(pb)

        def gla_head(l, tb, h):
            smT, smT_r, modT, modT_r, lay, lay_r = smT2[l % 2], smT2_r[l % 2], modT2[l % 2], modT2_r[l % 2], lay2[l % 2], lay2_r[l % 2]
            sm0 = 0
            ws = wload(l, "gla%d" % h)
            pz = ppool.alloc()
            for t in range(4):
                Sx.add(PE, MM(pz.t[:, t * 64:(t + 1) * 64], glrT[:, t * 128:(t + 1) * 128], wa2[:, h * 64:(h + 1) * 64]),
                       reads=[glrT_r, wa2_r], writes=[pz.r])
            for t in range(4):
                Sx.add(DVE, TT(lp[:, t, :], pz.t[:, t * 64:(t + 1) * 64], bcp[:, h * 64:(h + 1) * 64], ALU.add),
                       reads=[pz.r, bcp_r], writes=[lp_r])
            ppool.free(pz)
            lpv = lp[:].rearrange("p t d -> p (t d)")
            Sx.add(ACT, ACTF(lpv, lpv, AF.Exp, scale=-1.0), reads=[lp_r], writes=[lp_r])
            Sx.add(ACT, ACTF(lpv, lpv, AF.Ln, bias=1.0), reads=[lp_r], writes=[lp_r])
            pb = ppool.alloc()
            proj_F(ws, 384, 0, 128, pb)
            qg = bpool.alloc()
            Sx.add(ACT, ACTF(qg.t[0:64, :], pb.t[0:64, :], AF.Identity, scale=0.125), reads=[pb.r], writes=[qg.r])
            ppool.free(pb)
            pb = ppool.alloc()
            proj_T4(ws, 384, 128, 128, pb)
            v_t = bpool.alloc()
            Sx.add(ACT, ACP(v_t.t[:], pb.t[:]), reads=[pb.r], writes=[v_t.r])
            ppool.free(pb)
            pb = ppool.alloc()
            proj_F(ws, 384, 256, 128, pb)
            ggs = fpool.alloc()
            Sx.add(ACT, ACTF(ggs.t[:], pb.t[:], AF.Silu), reads=[pb.r], writes=[ggs.r])
            ppool.free(pb)
            pk = ppool.alloc()
            proj_T4(ws, 384, 64, 64, pk)
            wpool.free(ws)
            yield
            pe_ = ppool.alloc()
            for t in range(4):
                Sx.add(PE, MM(pe_.t[:, t * 64:(t + 1) * 64], CF("t3"), lp[:, t, :]), reads=[constf_r, lp_r], writes=[pe_.r])
            ef = fpool.alloc()
            Sx.add(ACT, ACTF(ef.t[:, 0:256], pe_.t[:, 0:256], AF.Exp, scale=-1.0 / 16.0), reads=[pe_.r], writes=[ef.r])
            ppool.free(pe_)
            pa = ppool.alloc()
            for t in range(4):
                Sx.add(PE, MM(pa.t[0:64, t * 2:t * 2 + 2], lp[:, t, :], CF("chunk_ind")),
                       reads=[lp_r, constf_r], writes=[pa.r])
            Sx.add(ACT, ACTF(a_gla[:], pa.t[0:64, 0:8], AF.Exp, scale=-1.0 / 16.0), reads=[pa.r], writes=[agla_r])
            ppool.free(pa)
            Sx.add(DVE, TT(kdec[:].rearrange("p t d -> p (t d)"), pk.t[:, 0:256], ef.t[:, 0:256], ALU.mult),
                   reads=[pk.r, ef.r], writes=[kdec_r])
            ppool.free(pk)
            fpool.free(ef)
            yield
            ps_kv = [ppool.alloc(), ppool.alloc()]
            for n in range(8):
                t, half = n // 2, n % 2
                rows = slice(half * 64, half * 64 + 64)
                kvb = ps_kv[n % 2]
                kvs = slice((n // 2) * 128, (n // 2) * 128 + 128)
                Sx.add(PE, MM(kvb.t[0:64, kvs], kdec[rows, t, :], v_t.t[rows, t * 128:(t + 1) * 128]),
                       reads=[kdec_r, v_t.r], writes=[kvb.r])
            bpool.free(v_t)
            sb8 = [bpool.alloc(), bpool.alloc()]
            for n in range(8):
                kvb = ps_kv[n % 2]
                kvs = slice((n // 2) * 128, (n // 2) * 128 + 128)
                if tb == 0 and n == 0:
                    Sx.add(DVE, CP(Sst[:, h, :], kvb.t[0:64, kvs]), reads=[kvb.r], writes=[Sst_r[h]])
                else:
                    Sx.add(DVE, STT(Sst[:, h, :], Sst[:, h, :], a_gla[:, n:n + 1], kvb.t[0:64, kvs], ALU.mult, ALU.add),
                           reads=[kvb.r, Sst_r[h], agla_r], writes=[Sst_r[h]])
                sbn = sb8[n // 4]
                Sx.add(ACT, ACP(sbn.t[0:64, (n % 4) * 128:(n % 4) * 128 + 128], Sst[:, h, :]), reads=[Sst_r[h]], writes=[sbn.r])
            ppool.free(ps_kv[0])
            ppool.free(ps_kv[1])
            yield
            ps_o = ppool.alloc()
            for n in range(8):
                sbn = sb8[n // 4]
                cs = slice(n * 64, n * 64 + 64)
                Sx.add(PE, MM(ps_o.t[:, cs], sbn.t[0:64, (n % 4) * 128:(n % 4) * 128 + 128], qg.t[0:64, cs]),
                       reads=[sbn.r, qg.r], writes=[ps_o.r])
            bpool.free(qg)
            bpool.free(sb8[0])
            bpool.free(sb8[1])
            o_f = fpool.alloc()
            Sx.add(ACT, ACP(o_f.t[:], ps_o.t[:]), reads=[ps_o.r], writes=[o_f.r])
            ppool.free(ps_o)

            def post(rstd):
                Sx.add(DVE, STT(o_f.t[:], o_f.t[:], smT[:, sm0 + SM_GLAG:sm0 + SM_GLAG + 1], rstd.t[:], ALU.mult, ALU.mult),
                       reads=[o_f.r, smT_r, rstd.r], writes=[o_f.r])
                Sx.add(DVE, TT(region[:, 12 + h, :], o_f.t[:], ggs.t[:], ALU.mult), reads=[o_f.r, ggs.r], writes=[reg_r[12 + h]])
            yield from rms_tail(o_f, None, None, post)
            fpool.free(o_f)
            fpool.free(ggs)

        def fox_pre(l, tb):
            pf = ppool.alloc()
            for t in range(4):
                for kc in range(KC):
                    Sx.add(PE, MM(pf.t[:, t * 4:t * 4 + 4], hT[:, kc, t * 128:(t + 1) * 128], wsm[:, kc, 16:20],
                                  start=(kc == 0), stop=(kc == KC - 1)),
                           reads=[wsm_r, hT_r[kc]], writes=[pf.r])
            for t in range(4):
                Sx.add(DVE, TT(lpf[:, t * 4:t * 4 + 4], pf.t[:, t * 4:t * 4 + 4], bcp[:, 256:260], ALU.add),
                       reads=[pf.r, bcp_r], writes=[lpf_r])
            ppool.free(pf)
            Sx.add(ACT, ACTF(lpf[:], lpf[:], AF.Exp, scale=-1.0), reads=[lpf_r], writes=[lpf_r])
            Sx.add(ACT, ACTF(lpf[:], lpf[:], AF.Ln, bias=1.0), reads=[lpf_r], writes=[lpf_r])
            yield
            pc = ppool.alloc()
            for t in range(4):
                for t2 in range(t + 1):
                    lhs = CF("tri_incl") if t2 == t else ones_f[:]
                    Sx.add(PE, MM(pc.t[:, t * 4:t * 4 + 4], lhs, lpf[:, t2 * 4:t2 * 4 + 4], start=(t2 == 0), stop=(t2 == t)),
                           reads=[constf_r, cbf_r, lpf_r], writes=[pc.r])
            if tb == 0:
                Sx.add(DVE, CP(ctok[:, 0:4, :].rearrange("p t h -> p (t h)"), pc.t[:, 0:16]), reads=[pc.r], writes=[ctok_r[tb]])
            else:
                for t in range(4):
                    Sx.add(DVE, TT(ctok[:, tb * 4 + t, :], pc.t[:, t * 4:t * 4 + 4], carry[:], ALU.add),
                           reads=[pc.r, carry_r], writes=[ctok_r[tb]])
            ppool.free(pc)
            pt_ = ppool.alloc()
            for t in range(4):
                Sx.add(PE, MM(pt_.t[:, 0:4], ones_f[:], lpf[:, t * 4:t * 4 + 4], start=(t == 0), stop=(t == 3)),
                       reads=[cbf_r, lpf_r], writes=[pt_.r])
            if tb == 0:
                Sx.add(DVE, CP(carry[:], pt_.t[:, 0:4]), reads=[pt_.r], writes=[carry_r])
            else:
                Sx.add(DVE, TT(carry[:], carry[:], pt_.t[:, 0:4], ALU.add), reads=[pt_.r, carry_r], writes=[carry_r])
            ppool.free(pt_)

        def fox_head(l, tb, h):
            smT, smT_r, modT, modT_r, lay, lay_r = smT2[l % 2], smT2_r[l % 2], modT2[l % 2], modT2_r[l % 2], lay2[l % 2], lay2_r[l % 2]
            sm0 = 0
            ws = wload(l, "fox%d" % h)
            pq = ppool.alloc()
            proj_F(ws, 384, 0, 128, pq)
            qf = fpool.alloc()
            Sx.add(ACT, ACP(qf.t[:], pq.t[:]), reads=[pq.r], writes=[qf.r])
            ppool.free(pq)
            sqq = bpool.alloc()
            Sx.add(DVE, TT(sqq.t[:], qf.t[:], qf.t[:], ALU.mult), reads=[qf.r], writes=[sqq.r])
            pk = ppool.alloc()
            proj_F(ws, 384, 128, 128, pk)
            kf = fpool.alloc()
            Sx.add(ACT, ACP(kf.t[:], pk.t[:]), reads=[pk.r], writes=[kf.r])
            ppool.free(pk)
            sqk = bpool.alloc()
            Sx.add(DVE, TT(sqk.t[:], kf.t[:], kf.t[:], ALU.mult), reads=[kf.r], writes=[sqk.r])
            pb = ppool.alloc()
            proj_T4(ws, 384, 256, 128, pb)
            wpool.free(ws)
            Sx.add(ACT, ACP(vcache[:, tb * 4:(tb + 1) * 4, h * 128:(h + 1) * 128], pb.t[:].rearrange("p (t e) -> p t e", t=4)),
                   reads=[pb.r], writes=[vc_r[h][tb]])
            ppool.free(pb)
            yield
            qh = bpool.alloc()
            for src, sq_, gain_ap, dst_ap, dst_res in ((qf, sqq, lay[:, 16:17], qh.t[:], qh.r),
                                                       (kf, sqk, smT[:, sm0 + SM_KG:sm0 + SM_KG + 1], kcache[:, h, tb * TB:(tb + 1) * TB], kc_r[h][tb])):
                p2 = ppool.alloc()
                Sx.add(PE, MM(p2.t[:], onesD_bf[:], sq_.t[:]), reads=[sq_.r, cbf_r], writes=[p2.r])
                bpool.free(sq_)
                rstd = fpool.alloc()
                Sx.add(ACT, ACTF(rstd.t[:], p2.t[:], AF.Ln, bias=eps_ap), reads=[p2.r, cbf_r], writes=[rstd.r])
                ppool.free(p2)
                Sx.add(ACT, ACTF(rstd.t[:], rstd.t[:], AF.Exp, scale=-0.5), reads=[rstd.r], writes=[rstd.r])
                Sx.add(DVE, STT(dst_ap, src.t[:], gain_ap, rstd.t[:], ALU.mult, ALU.mult),
                       reads=[src.r, rstd.r, lay_r, smT_r], writes=[dst_res])
                fpool.free(rstd)
                fpool.free(src)
            pa = ppool.alloc()
            for t in range(4):
                gt = tb * 4 + t
                Sx.add(PE, MM(pa.t[0:1, t * 128:(t + 1) * 128], ctok[:, gt, h:h + 1], identf),
                       reads=[ctok_r[tb], constf_r], writes=[pa.r])
            augs = bpool.alloc()
            Sx.add(DVE, MEMSET(augs.t[:], 0.0), writes=[augs.r])
            Sx.add(ACT, ACTF(augs.t[0:1, :], pa.t[0:1, :], AF.Identity, scale=-1.0), reads=[pa.r, augs.r], writes=[augs.r])
            ppool.free(pa)
            yield
            ps_o = ppool.alloc()
            ps_n = ppool.alloc()
            njb = tb * 4 + 4

            def scores(jb):
                tau = jb - tb * 4
                c0 = 0 if tau < 0 else tau * 128
                cs = slice(c0, TB)
                ps_s = ppool.alloc()
                Sx.add(PE, MM(ps_s.t[:, cs], kcache[:, h, jb * 128:(jb + 1) * 128], qh.t[:, cs], start=True, stop=False),
                       reads=[kc_r[h][jb // 4], qh.r], writes=[ps_s.r])
                Sx.add(PE, MM(ps_s.t[:, cs], onesrow0_bf[:], augs.t[:, cs], start=False, stop=(tau < 0)),
                       reads=[cbf_r, augs.r], writes=[ps_s.r])
                if tau >= 0:
                    Sx.add(PE, MM(ps_s.t[:, c0:c0 + 128], ident_bf[:], maskT_bf[:], start=False, stop=True),
                           reads=[cbf_r], writes=[ps_s.r])
                pt = bpool.alloc()
                Sx.add(ACT, ACTF(pt.t[:, cs], ps_s.t[:, cs], AF.Exp, bias=ctok[:, jb, h:h + 1]),
                       reads=[ps_s.r, ctok_r[jb // 4]], writes=[pt.r])
                ppool.free(ps_s)
                return pt, cs

            LOOK = int(_os.environ.get("MK_LOOK", "3"))
            pend = []
            nissued = 0
            while nissued < min(LOOK, njb):
                pend.append(scores(nissued))
                nissued += 1
            for jb in range(njb):
                pt, cs = pend.pop(0)
                if nissued < njb:
                    pend.append(scores(nissued))
                    nissued += 1
                Sx.add(PE, MM(ps_o.t[:, cs], vcache[:, jb, h * 128:(h + 1) * 128], pt.t[:, cs], start=(jb == 0), stop=(jb == njb - 1)),
                       reads=[vc_r[h][jb // 4], pt.r], writes=[ps_o.r])
                Sx.add(PE, MM(ps_n.t[:, cs], ones_bf[:], pt.t[:, cs], start=(jb == 0), stop=(jb == njb - 1)),
                       reads=[cbf_r, pt.r], writes=[ps_n.r])
                bpool.free(pt)
                if jb % 2 == 1 and jb + 1 < njb:
                    yield
            bpool.free(qh)
            bpool.free(augs)
            rs = fpool.alloc()
            Sx.add(DVE, (lambda e, o_=rs.t[:], i_=ps_n.t[:]: e.reciprocal(o_, i_)), reads=[ps_n.r], writes=[rs.r])
            ppool.free(ps_n)
            Sx.add(DVE, TT(region[:, 16 + h, :], ps_o.t[:], rs.t[:], ALU.mult), reads=[ps_o.r, rs.r], writes=[reg_r[16 + h]])
            ppool.free(ps_o)
            fpool.free(rs)

        def run_interleaved(items, width, background=None):
            items = list(items)
            live = []
            while items or live or background is not None:
                if background is not None:
                    try:
                        next(background)
                    except StopIteration:
                        background = None
                while items and len(live) < width:
                    k = items[0][0]
                    if k in ("gla", "foxpre", "fox") and any(k == lk for lk, _ in live):
                        break
                    if k == "fox" and any(lk == "foxpre" for lk, _ in live):
                        break
                    live.append(items.pop(0))
                for it in list(live):
                    try:
                        next(it[1])
                    except StopIteration:
                        live.remove(it)

        def emit_mixers(l, tb, background=None):
            gla_pre(l, tb)
            items = [("foxpre", fox_pre(l, tb))]
            for h in range(4):
                items += [("ret", ret_head(l, tb, h)), ("gla", gla_head(l, tb, h)), ("fox", fox_head(l, tb, h))]
            run_interleaved(items, int(_os.environ.get("MK_WIDTH", "3")), background)

        def emit_gate(l, tb):
            smT, smT_r, modT, modT_r, lay, lay_r = smT2[l % 2], smT2_r[l % 2], modT2[l % 2], modT2_r[l % 2], lay2[l % 2], lay2_r[l % 2]
            sm0 = 0
            for n in range(8):
                wm = wload(l, "mg%d" % n)
                wb = wload(l, "br%d" % n)
                mgv = wm.t[:, 0:3072].rearrange("p (k b j) -> p k b j", k=KC, b=3)
                brv = wb.t[:, 0:1536].rearrange("p (b k j) -> p b k j", b=3, k=4)
                prods = []
                for b in range(3):
                    pg = ppool.alloc()
                    for kc in range(KC):
                        Sx.add(PE, MM(pg.t[:], mgv[:, kc, b, :], hT[:, kc, :], start=(kc == 0), stop=(kc == KC - 1)),
                               reads=[wm.r, hT_r[kc]], writes=[pg.r])
                    py = ppool.alloc()
                    for k4 in range(4):
                        ch = 8 + b * 4 + k4
                        Sx.add(PE, MM(py.t[:], brv[:, b, k4, :], region[:, ch, :], start=(k4 == 0), stop=(k4 == 3)),
                               reads=[wb.r, reg_r[ch]], writes=[py.r])
                    sg = fpool.alloc()
                    Sx.add(ACT, ACTF(sg.t[:], pg.t[:], AF.Sigmoid, bias=smT[:, sm0 + SM_BMG + b * 8 + n:sm0 + SM_BMG + b * 8 + n + 1]),
                           reads=[pg.r, smT_r], writes=[sg.r])
                    ppool.free(pg)
                    Sx.add(DVE, TT(sg.t[:], py.t[:], sg.t[:], ALU.mult), reads=[py.r, sg.r], writes=[sg.r])
                    ppool.free(py)
                    prods.append(sg)
                wpool.free(wm)
                wpool.free(wb)
                Sx.add(DVE, TT(prods[0].t[:], prods[0].t[:], prods[1].t[:], ALU.add), reads=[prods[0].r, prods[1].r], writes=[prods[0].r])
                Sx.add(DVE, TT(region[:, n, :], prods[0].t[:], prods[2].t[:], ALU.add), reads=[prods[0].r, prods[2].r], writes=[reg_r[n]])
                for p_ in prods:
                    fpool.free(p_)
            for ch in range(2):
                ws = wload(l, "wo%d" % ch)
                wv = ws.t[:, 0:4096].rearrange("p (k n) -> p k n", k=KC)
                for dl in range(4):
                    dc = ch * 4 + dl
                    pb = ppool.alloc()
                    for n in range(8):
                        Sx.add(PE, MM(pb.t[:], wv[:, n, dl * 128:(dl + 1) * 128], region[:, n, :], start=(n == 0), stop=(n == 7)),
                               reads=[ws.r, reg_r[n]], writes=[pb.r])
                    xs = xT[:, dc, tb * TB:(tb + 1) * TB]
                    Sx.add(DVE, STT(xs, pb.t[:], modT[:, 16 + dc:17 + dc], xs, ALU.mult, ALU.add),
                           reads=[pb.r, modT_r, xT_r[dc][tb]], writes=[xT_r[dc][tb]])
                    ppool.free(pb)
                wpool.free(ws)

        def emit_ffn(l, tb, between=None):
            smT, smT_r, modT, modT_r, lay, lay_r = smT2[l % 2], smT2_r[l % 2], modT2[l % 2], modT2_r[l % 2], lay2[l % 2], lay2_r[l % 2]
            sm0 = 0
            S3 = int(_os.environ.get("MK_SUB3", "99"))

            def c3(k):
                if k > S3:
                    Sx.muted = True
            for i in range(6):
                nf = 4 if i < 5 else 2
                wdt = 128 * nf
                wu = wload(l, "up_u%d" % i)
                wg = wload(l, "up_g%d" % i)
                wuv = wu.t[:, 0:8 * wdt].rearrange("p (k n) -> p k n", k=KC)
                wgv = wg.t[:, 0:8 * wdt].rearrange("p (k n) -> p k n", k=KC)
                for j in range(nf):
                    f = i * 4 + j
                    pu = ppool.alloc()
                    pg = ppool.alloc()
                    for kc in range(KC):
                        Sx.add(PE, MM(pu.t[:], wuv[:, kc, j * 128:(j + 1) * 128], hT[:, kc, :], start=(kc == 0), stop=(kc == KC - 1)),
                               reads=[wu.r, hT_r[kc]], writes=[pu.r])
                    for kc in range(KC):
                        Sx.add(PE, MM(pg.t[:], wgv[:, kc, j * 128:(j + 1) * 128], hT[:, kc, :], start=(kc == 0), stop=(kc == KC - 1)),
                               reads=[wg.r, hT_r[kc]], writes=[pg.r])
                    u_sb = fpool.alloc()
                    Sx.add(ACT, ACP(u_sb.t[:], pu.t[:]), reads=[pu.r], writes=[u_sb.r])
                    ppool.free(pu)
                    c0 = fpool.alloc()
                    wc = sm0 + SM_WCONV
                    w2 = smT[:, wc + 2 * NFC + f:wc + 2 * NFC + f + 1]
                    w1 = smT[:, wc + NFC + f:wc + NFC + f + 1]
                    w0 = smT[:, wc + f:wc + f + 1]
                    Sx.add(DVE, TS(c0.t[:], u_sb.t[:], w2, smT[:, sm0 + SM_BCONV + f:sm0 + SM_BCONV + f + 1], ALU.mult, ALU.add),
                           reads=[u_sb.r, smT_r], writes=[c0.r])
                    Sx.add(DVE, STT(c0.t[:, 1:TB], u_sb.t[:, 0:TB - 1], w1, c0.t[:, 1:TB], ALU.mult, ALU.add),
                           reads=[u_sb.r, smT_r, c0.r], writes=[c0.r])
                    Sx.add(DVE, STT(c0.t[:, 2:TB], u_sb.t[:, 0:TB - 2], w0, c0.t[:, 2:TB], ALU.mult, ALU.add),
                           reads=[u_sb.r, smT_r, c0.r], writes=[c0.r])
                    if tb > 0:
                        Sx.add(DVE, STT(c0.t[:, 0:1], ucarry[:, f, 1:2], w1, c0.t[:, 0:1], ALU.mult, ALU.add),
                               reads=[ucarry_r[f], smT_r, c0.r], writes=[c0.r])
                        Sx.add(DVE, STT(c0.t[:, 0:2], ucarry[:, f, 0:2], w0, c0.t[:, 0:2], ALU.mult, ALU.add),
                               reads=[ucarry_r[f], smT_r, c0.r], writes=[c0.r])
                    Sx.add(DVE, CP(ucarry[:, f, :], u_sb.t[:, TB - 2:TB]), reads=[u_sb.r], writes=[ucarry_r[f]])
                    fpool.free(u_sb)
                    Sx.add(ACT, ACTF(c0.t[:], c0.t[:], AF.Silu), reads=[c0.r], writes=[c0.r])
                    c3(9)
                    Sx.add(DVE, TT(region[:, f, :], pg.t[:], c0.t[:], ALU.mult), reads=[pg.r, c0.r], writes=[reg_r[f]])
                    ppool.free(pg)
                    fpool.free(c0)
                wpool.free(wu)
                wpool.free(wg)
            for ch in range(2):
                if ch == 1 and between is not None:
                    between()
                accs = [ppool.alloc() for _ in range(4)]
                fc0 = 0
                for g, nfc in enumerate((8, 8, 6)):
                    ws = wload(l, "dn%d_%d" % (ch, g))
                    wv = ws.t[:, 0:nfc * 512].rearrange("p (k n) -> p k n", k=nfc)
                    for k in range(nfc):
                        fc = fc0 + k
                        for dl in range(4):
                            Sx.add(PE, MM(accs[dl].t[:], wv[:, k, dl * 128:(dl + 1) * 128], region[:, fc, :],
                                          start=(fc == 0), stop=(fc == NFC - 1)),
                                   reads=[ws.r, reg_r[fc]], writes=[accs[dl].r])
                    wpool.free(ws)
                    fc0 += nfc
                for dl in range(4):
                    dc = ch * 4 + dl
                    xs = xT[:, dc, tb * TB:(tb + 1) * TB]
                    Sx.add(DVE, STT(xs, accs[dl].t[:], modT[:, 40 + dc:41 + dc], xs, ALU.mult, ALU.add),
                           reads=[accs[dl].r, modT_r, xT_r[dc][tb]], writes=[xT_r[dc][tb]])
                    ppool.free(accs[dl])
            Sx.muted = False

        def emit_out(tb):
            for t in range(tb * 4, tb * 4 + 4):
                for half in range(2):
                    pb = ppool.alloc()
                    for j in range(4):
                        c = half * 4 + j
                        Sx.add(PE, TR(pb.t[:, j * 128:(j + 1) * 128], xT[:, c, t * 128:(t + 1) * 128], identf),
                               reads=[xT_r[c][t // 4], constf_r], writes=[pb.r])
                    so = fpool.alloc()
                    eng = ACT if half == 0 else DVE
                    fn_ = ACP(so.t[:], pb.t[:]) if eng == ACT else CP(so.t[:], pb.t[:])
                    Sx.add(eng, fn_, reads=[pb.r], writes=[so.r])
                    ppool.free(pb)
                    op = Sx.dma(SP, "f%d" % so.idx, DMA(out_d[t * 128:(t + 1) * 128, half * 512:(half + 1) * 512], so.t[:]), reads=[so.r])
                    final_ops.append(op)
                    fpool.free(so)

        import os
        STOP = int(os.environ.get("MK_STOP", "99"))
        NTB_RUN = int(os.environ.get("MK_NTB", str(NTB)))
        emit_mod(0)
        emit_norm(0, 0, 1)
        for l in range(n_layers):
            for tb in range(NTB):
                last = (tb == NTB - 1)
                nxt_mod = mod_gen(l + 1) if (last and l + 1 < n_layers) else None
                emit_mixers(l, tb, background=nxt_mod)
                if l == n_layers - 1 and tb > 0:
                    emit_out(tb - 1)
                if last and l + 1 < n_layers:
                    emit_layer_small_loads(l + 1)
                emit_gate(l, tb)
                emit_norm(l, tb, 2)
                if not last:
                    nxt = (lambda l_=l, t_=tb + 1: emit_norm(l_, t_, 1))
                elif l + 1 < n_layers:
                    nxt = (lambda l_=l + 1: emit_norm(l_, 0, 1))
                else:
                    nxt = None
                emit_ffn(l, tb, between=nxt)

        emit_out(NTB - 1)
        print("sbuf bytes remaining:", nc.sbuf_bytes_remaining, "ops:", {e: len(Sx.q[e]) for e in ENGS})
        Sx.emit(final_wait_ops=final_ops)
    return nc


_CACHE = {}


def _small_params(norm1_g, norm2_g, b_ada, b_mg, b_conv, w_conv, ret_norm_g, gla_norm_g, q_norm_g, k_norm_g, layers):
    L = len(layers)
    sm = np.zeros((128, L * NSM), np.float32)
    for i, l in enumerate(layers):
        o = i * NSM
        sm[:, o + SM_N1:o + SM_N1 + 8] = norm1_g[l].reshape(8, 128).T
        sm[:, o + SM_N2:o + SM_N2 + 8] = norm2_g[l].reshape(8, 128).T
        sm[:, o + SM_BADA:o + SM_BADA + 48] = b_ada[l].reshape(48, 128).T
        sm[:, o + SM_BMG:o + SM_BMG + 24] = b_mg[l].reshape(24, 128).T
        sm[:, o + SM_BCONV:o + SM_BCONV + 22] = b_conv[l].reshape(22, 128).T
        sm[:, o + SM_WCONV:o + SM_WCONV + 66] = w_conv[l].reshape(66, 128).T
        sm[:, o + SM_RETG:o + SM_RETG + 4] = ret_norm_g[l].reshape(4, 128).T
        sm[:, o + SM_GLAG] = gla_norm_g[l]
        sm[:, o + SM_QG] = q_norm_g[l]
        sm[:, o + SM_KG] = k_norm_g[l]
    return sm


def _prep_common(inputs, layers):
    L = len(layers)
    idx = list(layers)
    wall, offs, sizes, total = host_pack_weights(inputs["w_ada"][idx], inputs["w_in"][idx], inputs["w_br"][idx],
                                                 inputs["w_mg"][idx], inputs["w_o"][idx], inputs["w_up"][idx], inputs["w_down"][idx])
    wsm = np.zeros((L, 128, KC, 32), np.float32)
    for i, l in enumerate(layers):
        wi = inputs["w_in"][l]
        wsm[i, :, :, 0:16] = wi[:, 3072:3088].reshape(8, 128, 16).transpose(1, 0, 2)
        wsm[i, :, :, 16:20] = wi[:, 5136:5140].reshape(8, 128, 4).transpose(1, 0, 2)
    wsm = wsm.reshape(L, 128, KC * 32)
    wa2 = np.ascontiguousarray(inputs["w_gla_a2"][idx])
    sm = _small_params(inputs["norm1_g"], inputs["norm2_g"], inputs["b_ada"], inputs["b_mg"], inputs["b_conv"], inputs["w_conv"],
                       inputs["ret_norm_g"], inputs["gla_norm_g"], inputs["q_norm_g"], inputs["k_norm_g"], layers)
    bc = np.zeros((128, L * NBC), np.float32)
    for i, l in enumerate(layers):
        bc[:, i * NBC:i * NBC + 256] = inputs["b_gla_a"][l][None, :]
        bc[:, i * NBC + 256:i * NBC + 260] = inputs["b_fox_f"][l][None, :]
    c = host_consts()
    constf = np.concatenate([c[n].reshape(128, -1) for n, _ in CONST_F32], axis=1).astype(np.float32)
    rope = np.stack([c["rope_cos"], c["rope_sin"]], axis=1).astype(np.float32)
    common = {"wall": wall, "wsm": wsm, "wa2": wa2, "smT": sm, "bcp": bc, "constf": constf, "rope": rope}
    return common, offs, sizes, total


def run_layers(x, c, inputs, layers, dbg=None, ncores=NB, trace=False):
    common, offs, sizes, total = _prep_common(inputs, layers)
    import os
    key = (len(layers), total, str(dbg), os.environ.get("MK_STOP"), os.environ.get("MK_NTB"))
    if key not in _CACHE:
        _CACHE[key] = build_program(len(layers), total, offs, sizes, dbg=dbg)
    nc = _CACHE[key]
    in_maps = []
    for b in range(ncores):
        m = dict(common)
        m["x"] = np.ascontiguousarray(x[b])
        m["cT"] = np.ascontiguousarray(c[b].reshape(8, 128).T)
        in_maps.append(m)
    res = run_bass_kernel_spmd(nc, in_maps, core_ids=list(range(ncores)), **({"trace": True} if trace else {}))
    return res


def kernel(**inputs):
    inputs = {k: np.asarray(v) for k, v in inputs.items()}
    x = np.ascontiguousarray(inputs["x"], dtype=np.float32)
    c = np.ascontiguousarray(inputs["c"], dtype=np.float32)
    res = run_layers(x, c, inputs, list(range(DEPTH)))
    out = np.stack([np.asarray(r["out"]) for r in res.results], axis=0)
    return out.astype(np.float32)
```

```python
import contextlib
import numpy as np
import concourse.bass as bass
import concourse.mybir as mybir
from concourse.bass_utils import run_bass_kernel_spmd

F32 = mybir.dt.float32
BF16 = mybir.dt.bfloat16
AF = mybir.ActivationFunctionType
ALU = mybir.AluOpType

PE, ACT, DVE, POOL, SP = "pe", "act", "dve", "pool", "sp"
ENGS = [PE, ACT, DVE, POOL, SP]

D = 1024
S = 2048
NB = 8
DEPTH = 4
KC = 8
TB = 512
NTB = S // TB
NT = S // 128
D_FF = 2816
NFC = D_FF // 128
IN_W = 5140
EPS = 1e-6
WBUF_ELEMS = 4096
N_WBUF = 4
N_F32 = 8
N_BF = 14
NSM = 184
SM_N1, SM_N2, SM_BADA, SM_BMG, SM_BCONV, SM_WCONV, SM_RETG, SM_GLAG, SM_QG, SM_KG = 0, 8, 16, 64, 88, 110, 176, 180, 181, 182
DBG_TB = 0
NBC = 260


class Res:
    __slots__ = ("name", "w", "rs", "dma_rs", "excl")

    def __init__(self, name, excl=False):
        self.name = name
        self.w = None
        self.rs = {}
        self.dma_rs = []
        self.excl = excl


class Op:
    __slots__ = ("eng", "fn", "deps", "pos", "signal", "sigval", "is_dma", "sem", "semval")

    def __init__(self, eng, fn, is_dma=False):
        self.eng = eng
        self.fn = fn
        self.deps = ()
        self.pos = 0
        self.signal = False
        self.sigval = 0
        self.is_dma = is_dma
        self.sem = None
        self.semval = 0


class Sched:
    def __init__(self, nc):
        self.nc = nc
        self.q = {e: [] for e in ENGS}
        self.dma_sems = {}

    def _track(self, op, reads, writes):
        deps = {}
        for r in reads:
            if r.w is not None:
                deps[id(r.w)] = r.w
            if r.excl:
                for e_, o in r.rs.items():
                    if e_ != op.eng:
                        deps[id(o)] = o
        for w in writes:
            if w.w is not None:
                deps[id(w.w)] = w.w
            for o in w.rs.values():
                deps[id(o)] = o
            for o in w.dma_rs:
                deps[id(o)] = o
        deps.pop(id(op), None)
        op.deps = tuple(deps.values())
        for r in reads:
            if op.is_dma:
                r.dma_rs.append(op)
            else:
                r.rs[op.eng] = op
        for w in writes:
            w.w = op
            w.rs = {}
            w.dma_rs = []

    muted = False

    def add(self, eng, fn, reads=(), writes=()):
        if self.muted:
            return None
        op = Op(eng, fn)
        op.pos = len(self.q[eng])
        self.q[eng].append(op)
        self._track(op, reads, writes)
        return op

    def dma(self, queue, semkey, fn, reads=(), writes=()):
        if self.muted:
            return None
        op = Op(queue, fn, is_dma=True)
        op.pos = len(self.q[queue])
        self.q[queue].append(op)
        st = self.dma_sems.setdefault(semkey, [0, queue])
        assert st[1] == queue
        st[0] += 16
        op.sem = semkey
        op.semval = st[0]
        self._track(op, reads, writes)
        return op

    def emit(self, final_wait_ops=()):
        nc = self.nc
        for e in ENGS:
            for op in self.q[e]:
                for d in op.deps:
                    if d.is_dma:
                        continue
                    if d.eng != op.eng:
                        d.signal = True
                    elif op.eng != PE:
                        d.signal = True
        for e in ENGS:
            c = 0
            for op in self.q[e]:
                if op.signal and not op.is_dma:
                    c += 1
                    op.sigval = c
        with contextlib.ExitStack() as st:
            esem = {e: st.enter_context(nc.semaphore("s_" + e)) for e in ENGS}
            dsem = {k: st.enter_context(nc.semaphore("d_%s" % (k,))) for k in self.dma_sems}
            block = st.enter_context(nc.Block())
            sched = self

            def run(e, eng):
                waited = {}
                for op in sched.q[e]:
                    for d in op.deps:
                        if d.is_dma:
                            key = ("d", d.sem)
                            val = d.semval
                            sem = dsem[d.sem]
                        else:
                            if d.eng == e and e == PE:
                                continue
                            key = ("e", d.eng)
                            val = d.sigval
                            sem = esem[d.eng]
                        if waited.get(key, 0) >= val:
                            continue
                        waited[key] = val
                        eng.wait_ge(sem, val)
                    ins = op.fn(eng)
                    if op.is_dma:
                        ins.then_inc(dsem[op.sem], 16)
                    elif op.signal:
                        ins.then_inc(esem[e], 1)
                if e == SP:
                    for d in final_wait_ops:
                        if d is not None:
                            eng.wait_ge(dsem[d.sem], d.semval)

            @block.tensor
            def _(eng):
                run(PE, eng)

            @block.scalar
            def _(eng):
                run(ACT, eng)

            @block.vector
            def _(eng):
                run(DVE, eng)

            @block.gpsimd
            def _(eng):
                run(POOL, eng)

            @block.sync
            def _(eng):
                run(SP, eng)


class Slot:
    __slots__ = ("t", "r", "idx")

    def __init__(self, t, r, idx):
        self.t = t
        self.r = r
        self.idx = idx


class Pool_:
    def __init__(self, slots, name):
        self.free_list = list(slots)
        self.name = name

    def alloc(self):
        assert self.free_list, "pool %s exhausted" % self.name
        return self.free_list.pop(0)

    def free(self, s):
        self.free_list.append(s)


def MM(out, lhsT, rhs, start=True, stop=True):
    return lambda e: e.matmul(out, lhsT, rhs, start=start, stop=stop)


def TR(out, in_, ident):
    return lambda e: e.transpose(out, in_, ident)


def ACTF(out, in_, func, bias=None, scale=None):
    kw = {}
    if bias is not None:
        kw["bias"] = bias
    if scale is not None:
        kw["scale"] = scale
    return lambda e: e.activation(out, in_, func, **kw)


def ACP(out, in_):
    return lambda e: e.copy(out, in_)


def TT(out, a, b, op):
    return lambda e: e.tensor_tensor(out, a, b, op)


def TS(out, a, s1, s2, op0, op1=None):
    if op1 is None:
        return lambda e: e.tensor_scalar(out, a, s1, s2, op0)
    return lambda e: e.tensor_scalar(out, a, s1, s2, op0, op1)


def STT(out, in0, scalar, in1, op0, op1):
    return lambda e: e.scalar_tensor_tensor(out, in0, scalar, in1, op0, op1)


def CP(out, in_):
    return lambda e: e.tensor_copy(out, in_)


def MEMSET(ap, v):
    return lambda e: e.memset(ap, v)


def DMA(out, in_):
    return lambda e: e.dma_start(out=out, in_=in_)


def weight_stream_layout():
    head = [("ada%d" % i, 4096) for i in range(12)]
    tb = [("ret%d" % h, 4096) for h in range(4)]
    tb += [("gla%d" % h, 8 * 384) for h in range(4)]
    tb += [("fox%d" % h, 8 * 384) for h in range(4)]
    for n in range(8):
        tb += [("mg%d" % n, 3072), ("br%d" % n, 1536)]
    tb += [("wo0", 4096), ("wo1", 4096)]
    for i in range(6):
        wdt = 512 if i < 5 else 256
        tb += [("up_u%d" % i, 8 * wdt), ("up_g%d" % i, 8 * wdt)]
    for ch in range(2):
        for g, nfc in enumerate((8, 8, 6)):
            tb.append(("dn%d_%d" % (ch, g), nfc * 512))
    return head, tb


def host_pack_weights(w_ada, w_in, w_br, w_mg, w_o, w_up, w_down):
    L = w_in.shape[0]
    head, tb = weight_stream_layout()
    names = [n for n, _ in head] + [n for n, _ in tb]
    sizes = dict(head + tb)
    offs = {}
    o = 0
    for n in names:
        offs[n] = o
        o += sizes[n]
    total = o
    out = np.empty((L, 128, total), np.float32)

    def kcp(w2d):
        n = w2d.shape[1]
        return w2d.reshape(8, 128, n).transpose(1, 0, 2)

    for l in range(L):
        dst = out[l]

        def put(name, arr):
            a = arr.reshape(128, -1)
            assert a.shape[1] == sizes[name], (name, a.shape, sizes[name])
            dst[:, offs[name]:offs[name] + sizes[name]] = a

        for i in range(12):
            put("ada%d" % i, kcp(w_ada[l][:, i * 512:(i + 1) * 512]))
        wi = w_in[l]
        for h in range(4):
            cols = np.concatenate([np.arange(0 + 128 * h, 128 * h + 128), np.arange(512 + 128 * h, 512 + 128 * h + 128),
                                   np.arange(1024 + 128 * h, 1024 + 128 * h + 128), np.arange(1536 + 128 * h, 1536 + 128 * h + 128)])
            put("ret%d" % h, kcp(wi[:, cols]))
            cols = np.concatenate([np.arange(2048 + 64 * h, 2048 + 64 * h + 64), np.arange(2304 + 64 * h, 2304 + 64 * h + 64),
                                   np.arange(2560 + 128 * h, 2560 + 128 * h + 128), np.arange(3088 + 128 * h, 3088 + 128 * h + 128)])
            put("gla%d" % h, kcp(wi[:, cols]))
            cols = np.concatenate([np.arange(3600 + 128 * h, 3600 + 128 * h + 128), np.arange(4112 + 128 * h, 4112 + 128 * h + 128),
                                   np.arange(4624 + 128 * h, 4624 + 128 * h + 128)])
            put("fox%d" % h, kcp(wi[:, cols]))
        mg = w_mg[l].reshape(8, 128, 3, 8, 128)
        br = w_br[l].reshape(3, 4, 128, 8, 128)
        for n in range(8):
            put("mg%d" % n, mg[:, :, :, n, :].transpose(1, 0, 2, 3))
            put("br%d" % n, br[:, :, :, n, :].transpose(2, 0, 1, 3))
        put("wo0", kcp(w_o[l][:, 0:512]))
        put("wo1", kcp(w_o[l][:, 512:1024]))
        for i in range(6):
            wdt = 512 if i < 5 else 256
            put("up_u%d" % i, kcp(w_up[l][:, i * 512:i * 512 + wdt]))
            put("up_g%d" % i, kcp(w_up[l][:, D_FF + i * 512:D_FF + i * 512 + wdt]))
        wd = w_down[l].reshape(NFC, 128, D)
        fc0 = 0
        for g, nfc in enumerate((8, 8, 6)):
            for ch in range(2):
                put("dn%d_%d" % (ch, g), wd[fc0:fc0 + nfc, :, ch * 512:(ch + 1) * 512].transpose(1, 0, 2))
            fc0 += nfc
    return out, offs, sizes, total


def host_consts():
    c = {}
    half = 64
    inv_freq = 10000.0 ** (-np.arange(half, dtype=np.float64) / half)
    pos = np.arange(S, dtype=np.float64)
    ang = inv_freq[:, None] * pos[None, :]
    cos = np.cos(ang)
    sin = np.sin(ang)
    c["rope_cos"] = np.concatenate([cos, cos], 0).astype(np.float32)
    c["rope_sin"] = np.concatenate([sin, -sin], 0).astype(np.float32)
    log_g = np.log1p(-np.exp2(-5.0 - np.arange(4, dtype=np.float64)))
    sc = 128.0 ** -0.5
    i = np.arange(128)
    wt = np.zeros((4, 128, 128), np.float64)
    for h in range(4):
        jj, ii = np.meshgrid(i, i, indexing="ij")
        same = (jj // 64) == (ii // 64)
        later = (ii // 64) > (jj // 64)
        w = np.where(same, np.exp(np.abs(ii - jj) * log_g[h]), np.where(later, np.exp((ii - jj) * log_g[h]), 0.0))
        wt[h] = w * sc
    c["ret_wt"] = wt.transpose(1, 0, 2).reshape(128, 512).astype(np.float32)
    qd = np.stack([np.exp((i + 1.0) * log_g[h]) * sc for h in range(4)], 0)
    c["ret_qdec"] = np.broadcast_to(qd.reshape(1, 512), (128, 512)).astype(np.float32).copy()
    kw = np.stack([np.exp((127.0 - i) * log_g[h]) for h in range(4)], 1)
    c["ret_kw"] = kw.astype(np.float32)
    c["ret_g128"] = [float(np.exp(128.0 * log_g[h])) for h in range(4)]
    ident = np.eye(128, dtype=np.float32)
    c["ident"] = ident
    jj, ii = np.meshgrid(i, i, indexing="ij")
    c["tri_incl"] = (jj <= ii).astype(np.float32)
    c["t3"] = ((jj > ii) & ((jj // 64) == (ii // 64))).astype(np.float32)
    c["chunk_ind"] = np.stack([(i < 64), (i >= 64)], 1).astype(np.float32)
    c["maskT"] = np.where(jj > ii, -30000.0, 0.0).astype(np.float32)
    return c


CONST_F32 = [("ident", 128), ("tri_incl", 128), ("t3", 128), ("chunk_ind", 2), ("ret_wt", 512),
             ("ret_qdec", 512), ("ret_kw", 4), ("maskT", 128)]


def build_program(n_layers, layer_elems, offs, sizes, dbg=None):
    nc = bass.Bass("TRN2", target_bir_lowering=False)
    consts = host_consts()
    g128 = consts["ret_g128"]

    def dram(name, shape, dt=F32, kind="ExternalInput"):
        return nc.dram_tensor(name, list(shape), dt, kind=kind).ap()

    x_d = dram("x", [S, D])
    out_d = dram("out", [S, D], kind="ExternalOutput")
    cT_d = dram("cT", [128, KC])
    wall_d = dram("wall", [n_layers, 128, layer_elems])
    wsm_d = dram("wsm", [n_layers, 128, KC * 32])
    wa2_d = dram("wa2", [n_layers, 16, 256])
    smT_d = dram("smT", [128, n_layers * NSM])
    bc_d = dram("bcp", [128, n_layers * NBC])
    cf_w = sum(w for _, w in CONST_F32)
    constf_d = dram("constf", [128, cf_w])
    rope_d = dram("rope", [128, 2, S])
    dbg_d = {}
    if dbg:
        for name, shape, dt in dbg:
            dbg_d[name] = dram("dbg_" + name, shape, dt, kind="ExternalOutput")

    with contextlib.ExitStack() as st:
        def sb(name, shape, dt):
            return st.enter_context(nc.sbuf_tensor("sb_" + name, list(shape), dt))

        Sx = Sched(nc)
        xT = sb("xT", [128, KC, S], F32)
        xT_r = [[Res("xT%d_%d" % (c, t)) for t in range(NTB)] for c in range(KC)]
        hT = sb("hT", [128, KC, TB], BF16)
        hT_r = [Res("hT%d" % c) for c in range(KC)]
        region = sb("region", [128, NFC, TB], BF16)
        reg_r = [Res("reg%d" % i) for i in range(NFC)]
        kcache = sb("kcache", [128, 4, S], BF16)
        kc_r = [[Res("kc%d_%d" % (h, t)) for t in range(NTB)] for h in range(4)]
        vcache = sb("vcache", [128, NT, 512], BF16)
        vc_r = [[Res("vc%d_%d" % (h, t)) for t in range(NTB)] for h in range(4)]
        ctok = sb("ctok", [128, NT, 4], F32)
        ctok_r = [Res("ctok%d" % t) for t in range(NTB)]
        carry = sb("carry", [128, 4], F32)
        carry_r = Res("carry")
        wbufs = [Slot(sb("wb%d" % i, [128, WBUF_ELEMS], BF16), Res("wb%d" % i), i) for i in range(N_WBUF)]
        wpool = Pool_(wbufs, "w")
        f32s = [Slot(sb("f%d" % i, [128, TB], F32), Res("f%d" % i), i) for i in range(N_F32)]
        fpool = Pool_(f32s, "f32")
        bfs = [Slot(sb("b%d" % i, [128, TB], BF16), Res("b%d" % i), i) for i in range(N_BF)]
        bpool = Pool_(bfs, "bf16")
        banks = [Slot(st.enter_context(nc.psum_tensor("ps%d" % i, [128, TB], F32)), Res("ps%d" % i, excl=True), i) for i in range(8)]
        ppool = Pool_(banks, "psum")
        constf = sb("constf", [128, cf_w], F32)
        constf_r = Res("constf")
        coff = {}
        o = 0
        for n_, w_ in CONST_F32:
            coff[n_] = (o, w_)
            o += w_

        def CF(name, lo=0, hi=None):
            o_, w_ = coff[name]
            hi = w_ if hi is None else hi
            return constf[:, o_ + lo:o_ + hi]

        ident_bf = sb("ident_bf", [128, 128], BF16)
        maskT_bf = sb("maskT_bf", [128, 128], BF16)
        ones_bf = sb("ones_bf", [128, 128], BF16)
        onesD_bf = sb("onesD_bf", [128, 128], BF16)
        ones_f = sb("ones_f", [128, 128], F32)
        onesrow0_bf = sb("onesrow0_bf", [128, 128], BF16)
        cbf_r = Res("cbf")
        smT2 = [sb("smT%d" % i, [128, NSM], F32) for i in range(2)]
        smT2_r = [Res("smT%d" % i) for i in range(2)]
        cT = sb("cT", [128, KC], F32)
        cact = sb("cact", [128, KC], BF16)
        cact_r = Res("cact")
        modT2 = [sb("modT%d" % i, [128, 48], F32) for i in range(2)]
        modT2_r = [Res("modT%d" % i) for i in range(2)]
        lay2 = [sb("lay%d" % i, [128, 24], F32) for i in range(2)]
        lay2_r = [Res("lay%d" % i) for i in range(2)]
        bcp = sb("bcp", [128, NBC], F32)
        bcp_r = Res("bcp")
        wsm = sb("wsm", [128, KC, 32], BF16)
        wsm_r = Res("wsm")
        wa2 = sb("wa2", [128, 256], BF16)
        wa2_r = Res("wa2")
        eps_t = sb("eps_t", [128, 1], F32)
        eps_ap = eps_t[:, 0:1]
        Rst = sb("Rst", [128, 4, 128], F32)
        Rst_r = [Res("R%d" % h) for h in range(4)]
        Sst = sb("Sst", [64, 4, 128], F32)
        Sst_r = [Res("S%d" % h) for h in range(4)]
        ucarry = sb("ucarry", [128, NFC, 2], F32)
        ucarry_r = [Res("uc%d" % f) for f in range(NFC)]
        glrT = sb("glrT", [128, TB], BF16)
        glrT_r = Res("glrT")
        lp = sb("lp", [128, 4, 64], F32)
        lp_r = Res("lp")
        kdec = sb("kdec", [128, 4, 64], BF16)
        kdec_r = Res("kdec")
        a_gla = sb("a_gla", [64, 8], F32)
        agla_r = Res("agla")
        lpf = sb("lpf", [128, 16], F32)
        lpf_r = Res("lpf")

        final_ops = []

        def wload(l, name):
            s = wpool.alloc()
            n = sizes[name]
            o_ = offs[name]
            Sx.dma(POOL, "w%d" % s.idx, DMA(s.t[:, 0:n], wall_d[l, :, o_:o_ + n]), writes=[s.r])
            return s

        def dump(name, ap, reads):
            if name in dbg_d:
                op = Sx.dma(SP, "dbg_" + name, DMA(dbg_d[name], ap), reads=reads)
                final_ops.append(op)

        Sx.dma(SP, "constf", DMA(constf[:], constf_d), writes=[constf_r])
        Sx.dma(SP, "cT", DMA(cT[:], cT_d), writes=[cact_r])
        Sx.add(ACT, ACTF(cact[:], cT[:], AF.Silu), reads=[cact_r], writes=[cact_r])
        Sx.add(DVE, CP(ident_bf[:], CF("ident")), reads=[constf_r], writes=[cbf_r])
        Sx.add(DVE, CP(maskT_bf[:], CF("maskT")), reads=[constf_r], writes=[cbf_r])
        Sx.add(DVE, MEMSET(ones_bf[:], 1.0), writes=[cbf_r])
        Sx.add(DVE, MEMSET(onesD_bf[:], 1.0 / 128.0), writes=[cbf_r])
        Sx.add(DVE, MEMSET(ones_f[:], 1.0), writes=[cbf_r])
        Sx.add(DVE, MEMSET(wa2[:], 0.0), writes=[wa2_r])
        Sx.add(DVE, MEMSET(glrT[:], 0.0), writes=[glrT_r])
        Sx.add(DVE, MEMSET(onesrow0_bf[:], 0.0), writes=[cbf_r])
        Sx.add(DVE, MEMSET(onesrow0_bf[0:1, :], 1.0), writes=[cbf_r])
        Sx.add(DVE, MEMSET(eps_t[:], EPS), writes=[cbf_r])
        identf = CF("ident")

        for t in range(NT):
            s0 = fpool.alloc()
            s1 = fpool.alloc()
            Sx.dma(SP, "f%d" % s0.idx, DMA(s0.t[:], x_d[t * 128:(t + 1) * 128, 0:512]), writes=[s0.r])
            Sx.dma(SP, "f%d" % s1.idx, DMA(s1.t[:], x_d[t * 128:(t + 1) * 128, 512:1024]), writes=[s1.r])
            for half, sl in enumerate((s0, s1)):
                pb = ppool.alloc()
                for j in range(4):
                    Sx.add(PE, TR(pb.t[:, j * 128:(j + 1) * 128], sl.t[:, j * 128:(j + 1) * 128], identf),
                           reads=[sl.r, constf_r], writes=[pb.r])
                dst = xT[:, half * 4:(half + 1) * 4, t * 128:(t + 1) * 128]
                src = pb.t[:].rearrange("p (c n) -> p c n", c=4)
                eng = ACT if half == 0 else DVE
                fn = ACP(dst, src) if eng == ACT else CP(dst, src)
                Sx.add(eng, fn, reads=[pb.r], writes=[xT_r[c][t // 4] for c in range(half * 4, half * 4 + 4)])
                ppool.free(pb)
            fpool.free(s0)
            fpool.free(s1)

        def mod_gen(l):
            smT, smT_r, modT, modT_r, lay, lay_r = smT2[l % 2], smT2_r[l % 2], modT2[l % 2], modT2_r[l % 2], lay2[l % 2], lay2_r[l % 2]
            sm0 = 0
            Sx.dma(SP, "smT", DMA(smT[:], smT_d[:, l * NSM:(l + 1) * NSM]), writes=[smT_r])
            pm = ppool.alloc()
            for cb in range(12):
                ws = wload(l, "ada%d" % cb)
                wv = ws.t[:, 0:4096].rearrange("p (k n) -> p k n", k=KC)
                pb = ppool.alloc()
                for kc in range(KC):
                    Sx.add(PE, MM(pb.t[0:1, :], cact[:, kc:kc + 1], wv[:, kc, :], start=(kc == 0), stop=(kc == KC - 1)),
                           reads=[cact_r, ws.r], writes=[pb.r])
                row = fpool.alloc()
                Sx.add(ACT, ACP(row.t[0:1, :], pb.t[0:1, :]), reads=[pb.r], writes=[row.r])
                ppool.free(pb)
                wpool.free(ws)
                for jj in range(4):
                    j = cb * 4 + jj
                    Sx.add(PE, MM(pm.t[:, j:j + 1], row.t[0:1, jj * 128:(jj + 1) * 128], ones_f[0:1, 0:1]),
                           reads=[row.r, cbf_r], writes=[pm.r])
                fpool.free(row)
                yield
            Sx.add(DVE, TT(modT[:], pm.t[:, 0:48], smT[:, sm0 + SM_BADA:sm0 + SM_BADA + 48], ALU.add),
                   reads=[pm.r, smT_r], writes=[modT_r])
            ppool.free(pm)
            Sx.add(DVE, STT(lay[:, 0:8], modT[:, 8:16], 1.0, smT[:, sm0 + SM_N1:sm0 + SM_N1 + 8], ALU.add, ALU.mult),
                   reads=[modT_r, smT_r], writes=[lay_r])
            Sx.add(DVE, STT(lay[:, 8:16], modT[:, 32:40], 1.0, smT[:, sm0 + SM_N2:sm0 + SM_N2 + 8], ALU.add, ALU.mult),
                   reads=[modT_r, smT_r], writes=[lay_r])
            Sx.add(DVE, TS(lay[:, 16:17], smT[:, sm0 + SM_QG:sm0 + SM_QG + 1], 128.0 ** -0.5, None, ALU.mult),
                   reads=[smT_r], writes=[lay_r])

        def emit_layer_small_loads(l):
            Sx.dma(SP, "bcp", DMA(bcp[:], bc_d[:, l * NBC:(l + 1) * NBC]), writes=[bcp_r])
            Sx.dma(POOL, "wsm", DMA(wsm[:].rearrange("p k n -> p (k n)"), wsm_d[l]), writes=[wsm_r])
            Sx.dma(POOL, "wa2", DMA(wa2[0:16, :], wa2_d[l]), writes=[wa2_r])

        def emit_mod(l):
            for _ in mod_gen(l):
                pass
            emit_layer_small_loads(l)

        def emit_norm(l, tb, which):
            smT, smT_r, modT, modT_r, lay, lay_r = smT2[l % 2], smT2_r[l % 2], modT2[l % 2], modT2_r[l % 2], lay2[l % 2], lay2_r[l % 2]
            a_off = 0 if which == 1 else 8
            b_off = 0 if which == 1 else 24
            pb = ppool.alloc()
            for c in range(KC):
                sq = bpool.alloc()
                xs = xT[:, c, tb * TB:(tb + 1) * TB]
                Sx.add(ACT if c % 2 == 0 else DVE, (ACTF(sq.t[:], xs, AF.Square) if c % 2 == 0 else TT(sq.t[:], xs, xs, ALU.mult)), reads=[xT_r[c][tb]], writes=[sq.r])
                Sx.add(PE, MM(pb.t[:], ones_bf[:], sq.t[:], start=(c == 0), stop=(c == KC - 1)),
                       reads=[sq.r, cbf_r], writes=[pb.r])
                bpool.free(sq)
            rstd = fpool.alloc()
            Sx.add(ACT, ACTF(rstd.t[:], pb.t[:], AF.Ln, bias=eps_ap, scale=1.0 / D), reads=[pb.r, cbf_r], writes=[rstd.r])
            ppool.free(pb)
            Sx.add(ACT, ACTF(rstd.t[:], rstd.t[:], AF.Exp, scale=-0.5), reads=[rstd.r], writes=[rstd.r])
            for c in range(KC):
                tmp = fpool.alloc()
                xs = xT[:, c, tb * TB:(tb + 1) * TB]
                Sx.add(DVE, STT(tmp.t[:], xs, lay[:, a_off + c:a_off + c + 1], rstd.t[:], ALU.mult, ALU.mult),
                       reads=[xT_r[c][tb], lay_r, rstd.r], writes=[tmp.r])
                Sx.add(ACT, ACTF(hT[:, c, :], tmp.t[:], AF.Identity, bias=modT[:, b_off + c:b_off + c + 1]),
                       reads=[tmp.r, modT_r], writes=[hT_r[c]])
                fpool.free(tmp)
            fpool.free(rstd)

        def proj_F(ws, ncols_blk, col0, m, pb):
            wv = ws.t[:, 0:8 * ncols_blk].rearrange("p (k n) -> p k n", k=KC)
            for kc in range(KC):
                Sx.add(PE, MM(pb.t[0:m, :], wv[:, kc, col0:col0 + m], hT[:, kc, :], start=(kc == 0), stop=(kc == KC - 1)),
                       reads=[ws.r, hT_r[kc]], writes=[pb.r])

        def proj_T4(ws, ncols_blk, col0, n, pb):
            wv = ws.t[:, 0:8 * ncols_blk].rearrange("p (k n) -> p k n", k=KC)
            for t in range(4):
                for kc in range(KC):
                    Sx.add(PE, MM(pb.t[:, t * n:(t + 1) * n], hT[:, kc, t * 128:(t + 1) * 128], wv[:, kc, col0:col0 + n],
                                  start=(kc == 0), stop=(kc == KC - 1)),
                           reads=[ws.r, hT_r[kc]], writes=[pb.r])

        def part_norm_stats(src_f, want_mean):
            sq = bpool.alloc()
            Sx.add(DVE, TT(sq.t[:], src_f.t[:], src_f.t[:], ALU.mult), reads=[src_f.r], writes=[sq.r])
            p2 = ppool.alloc()
            Sx.add(PE, MM(p2.t[:], onesD_bf[:], sq.t[:]), reads=[sq.r, cbf_r], writes=[p2.r])
            bpool.free(sq)
            if want_mean:
                ob = bpool.alloc()
                Sx.add(ACT, ACP(ob.t[:], src_f.t[:]), reads=[src_f.r], writes=[ob.r])
                p1 = ppool.alloc()
                Sx.add(PE, MM(p1.t[:], onesD_bf[:], ob.t[:]), reads=[ob.r, cbf_r], writes=[p1.r])
                bpool.free(ob)
                mean = fpool.alloc()
                Sx.add(ACT, ACP(mean.t[:], p1.t[:]), reads=[p1.r], writes=[mean.r])
                ppool.free(p1)
                var = fpool.alloc()
                Sx.add(ACT, ACTF(var.t[:], mean.t[:], AF.Square), reads=[mean.r], writes=[var.r])
                Sx.add(DVE, TT(var.t[:], p2.t[:], var.t[:], ALU.subtract), reads=[p2.r, var.r], writes=[var.r])
                ppool.free(p2)
                Sx.add(ACT, ACTF(var.t[:], var.t[:], AF.Ln, bias=eps_ap), reads=[var.r, cbf_r], writes=[var.r])
                Sx.add(ACT, ACTF(var.t[:], var.t[:], AF.Exp, scale=-0.5), reads=[var.r], writes=[var.r])
                return mean, var
            rstd = fpool.alloc()
            Sx.add(ACT, ACTF(rstd.t[:], p2.t[:], AF.Ln, bias=eps_ap), reads=[p2.r, cbf_r], writes=[rstd.r])
            ppool.free(p2)
            Sx.add(ACT, ACTF(rstd.t[:], rstd.t[:], AF.Exp, scale=-0.5), reads=[rstd.r], writes=[rstd.r])
            return None, rstd

        import os as _os

        def rotary(pb, dst, cos_s, sin_s):
            t1 = fpool.alloc()
            t2 = fpool.alloc()
            Sx.add(DVE, TT(t1.t[:], pb.t[:], cos_s.t[:], ALU.mult), reads=[pb.r, cos_s.r], writes=[t1.r])
            Sx.add(DVE, TT(t2.t[0:64, :], pb.t[64:128, :], sin_s.t[64:128, :], ALU.mult), reads=[pb.r, sin_s.r], writes=[t2.r])
            Sx.add(DVE, TT(t2.t[64:128, :], pb.t[0:64, :], sin_s.t[0:64, :], ALU.mult), reads=[pb.r, sin_s.r], writes=[t2.r])
            Sx.add(DVE, TT(dst.t[:], t1.t[:], t2.t[:], ALU.add), reads=[t1.r, t2.r], writes=[dst.r])
            fpool.free(t1)
            fpool.free(t2)

        def ret_head(l, tb, h):
            smT, smT_r, modT, modT_r, lay, lay_r = smT2[l % 2], smT2_r[l % 2], modT2[l % 2], modT2_r[l % 2], lay2[l % 2], lay2_r[l % 2]
            sm0 = 0
            ws = wload(l, "ret%d" % h)
            cos_s = fpool.alloc()
            sin_s = fpool.alloc()
            Sx.dma(SP, "f%d" % cos_s.idx, DMA(cos_s.t[:], rope_d[:, 0, tb * TB:(tb + 1) * TB]), writes=[cos_s.r])
            Sx.dma(SP, "f%d" % sin_s.idx, DMA(sin_s.t[:], rope_d[:, 1, tb * TB:(tb + 1) * TB]), writes=[sin_s.r])
            q_rot = bpool.alloc()
            k_rot = bpool.alloc()
            pb = ppool.alloc()
            proj_F(ws, 512, 0, 128, pb)
            rotary(pb, q_rot, cos_s, sin_s)
            ppool.free(pb)
            pb = ppool.alloc()
            proj_F(ws, 512, 128, 128, pb)
            rotary(pb, k_rot, cos_s, sin_s)
            ppool.free(pb)
            fpool.free(cos_s)
            fpool.free(sin_s)
            pb = ppool.alloc()
            proj_T4(ws, 512, 256, 128, pb)
            v_t = bpool.alloc()
            Sx.add(ACT, ACP(v_t.t[:], pb.t[:]), reads=[pb.r], writes=[v_t.r])
            ppool.free(pb)
            pb = ppool.alloc()
            proj_F(ws, 512, 384, 128, pb)
            wpool.free(ws)
            rgs = fpool.alloc()
            Sx.add(ACT, ACTF(rgs.t[:], pb.t[:], AF.Silu), reads=[pb.r], writes=[rgs.r])
            ppool.free(pb)
            yield
            pk = ppool.alloc()
            pk_bf = pk.t[:].bitcast(BF16)
            for t in range(4):
                Sx.add(PE, TR(pk_bf[:, t * 128:(t + 1) * 128], k_rot.t[:, t * 128:(t + 1) * 128], ident_bf[:]),
                       reads=[k_rot.r, cbf_r], writes=[pk.r])
            kwtok = bpool.alloc()
            Sx.add(DVE, TS(kwtok.t[:], pk_bf[:, 0:512], CF("ret_kw", h, h + 1), None, ALU.mult),
                   reads=[pk.r, constf_r], writes=[kwtok.r])
            ppool.free(pk)
            ps_s = ppool.alloc()
            for t in range(4):
                sl = slice(t * 128, (t + 1) * 128)
                Sx.add(PE, MM(ps_s.t[:, sl], k_rot.t[:, sl], q_rot.t[:, sl]),
                       reads=[k_rot.r, q_rot.r], writes=[ps_s.r])
            pt = bpool.alloc()
            for t in range(4):
                sl = slice(t * 128, (t + 1) * 128)
                Sx.add(DVE, TT(pt.t[:, sl], ps_s.t[:, sl], CF("ret_wt", h * 128, (h + 1) * 128), ALU.mult),
                       reads=[ps_s.r, constf_r], writes=[pt.r])
            ppool.free(ps_s)
            qw = bpool.alloc()
            for t in range(4):
                sl = slice(t * 128, (t + 1) * 128)
                Sx.add(DVE, TT(qw.t[:, sl], q_rot.t[:, sl], CF("ret_qdec", h * 128, (h + 1) * 128), ALU.mult),
                       reads=[q_rot.r, constf_r], writes=[qw.r])
            bpool.free(q_rot)
            bpool.free(k_rot)
            yield
            ps_kv = ppool.alloc()
            for t in range(4):
                sl = slice(t * 128, (t + 1) * 128)
                Sx.add(PE, MM(ps_kv.t[:, sl], kwtok.t[:, sl], v_t.t[:, sl]),
                       reads=[kwtok.r, v_t.r], writes=[ps_kv.r])
            bpool.free(kwtok)
            rb = bpool.alloc()
            for t in range(4):
                sl = slice(t * 128, (t + 1) * 128)
                gt = tb * 4 + t
                if gt > 0:
                    Sx.add(ACT, ACP(rb.t[:, sl], Rst[:, h, :]), reads=[Rst_r[h]], writes=[rb.r])
                if gt == 0:
                    Sx.add(DVE, CP(Rst[:, h, :], ps_kv.t[:, sl]), reads=[ps_kv.r], writes=[Rst_r[h]])
                else:
                    Sx.add(DVE, STT(Rst[:, h, :], Rst[:, h, :], g128[h], ps_kv.t[:, sl], ALU.mult, ALU.add),
                           reads=[ps_kv.r, Rst_r[h]], writes=[Rst_r[h]])
            ppool.free(ps_kv)
            yield
            ps_o = ppool.alloc()
            for t in range(4):
                sl = slice(t * 128, (t + 1) * 128)
                gt = tb * 4 + t
                Sx.add(PE, MM(ps_o.t[:, sl], v_t.t[:, sl], pt.t[:, sl], start=True, stop=(gt == 0)),
                       reads=[v_t.r, pt.r], writes=[ps_o.r])
                if gt > 0:
                    Sx.add(PE, MM(ps_o.t[:, sl], rb.t[:, sl], qw.t[:, sl], start=False, stop=True),
                           reads=[rb.r, qw.r], writes=[ps_o.r])
            for s_ in (pt, qw, v_t, rb):
                bpool.free(s_)
            o_f = fpool.alloc()
            Sx.add(ACT, ACP(o_f.t[:], ps_o.t[:]), reads=[ps_o.r], writes=[o_f.r])
            ppool.free(ps_o)
            ob = bpool.alloc()
            Sx.add(ACT, ACP(ob.t[:], o_f.t[:]), reads=[o_f.r], writes=[ob.r])
            yield
            p1 = ppool.alloc()
            Sx.add(PE, MM(p1.t[:], onesD_bf[:], ob.t[:]), reads=[ob.r, cbf_r], writes=[p1.r])
            bpool.free(ob)
            Sx.add(DVE, TT(o_f.t[:], o_f.t[:], p1.t[:], ALU.subtract), reads=[o_f.r, p1.r], writes=[o_f.r])
            ppool.free(p1)
            sq = bpool.alloc()
            Sx.add(ACT, ACTF(sq.t[:], o_f.t[:], AF.Square), reads=[o_f.r], writes=[sq.r])
            yield
            p2 = ppool.alloc()
            Sx.add(PE, MM(p2.t[:], onesD_bf[:], sq.t[:]), reads=[sq.r, cbf_r], writes=[p2.r])
            bpool.free(sq)
            rstd = fpool.alloc()
            Sx.add(ACT, ACTF(rstd.t[:], p2.t[:], AF.Ln, bias=eps_ap), reads=[p2.r, cbf_r], writes=[rstd.r])
            ppool.free(p2)
            Sx.add(ACT, ACTF(rstd.t[:], rstd.t[:], AF.Exp, scale=-0.5), reads=[rstd.r], writes=[rstd.r])
            Sx.add(DVE, TT(o_f.t[:], o_f.t[:], rstd.t[:], ALU.mult), reads=[o_f.r, rstd.r], writes=[o_f.r])
            Sx.add(DVE, STT(region[:, 8 + h, :], o_f.t[:], smT[:, sm0 + SM_RETG + h:sm0 + SM_RETG + h + 1], rgs.t[:], ALU.mult, ALU.mult),
                   reads=[o_f.r, smT_r, rgs.r], writes=[reg_r[8 + h]])
            for s_ in (rstd, o_f, rgs):
                fpool.free(s_)

        def rms_tail(o_f, gain_ap, extra_reads, post):
            sq = bpool.alloc()
            Sx.add(ACT, ACTF(sq.t[:], o_f.t[:], AF.Square), reads=[o_f.r], writes=[sq.r])
            yield
            p2 = ppool.alloc()
            Sx.add(PE, MM(p2.t[:], onesD_bf[:], sq.t[:]), reads=[sq.r, cbf_r], writes=[p2.r])
            bpool.free(sq)
            rstd = fpool.alloc()
            Sx.add(ACT, ACTF(rstd.t[:], p2.t[:], AF.Ln, bias=eps_ap), reads=[p2.r, cbf_r], writes=[rstd.r])
            ppool.free(p2)
            Sx.add(ACT, ACTF(rstd.t[:], rstd.t[:], AF.Exp, scale=-0.5), reads=[rstd.r], writes=[rstd.r])
            post(rstd)
            fpool.free(rstd)

        def gla_pre(l, tb):
            pb = ppool.alloc()
            for kc in range(KC):
                Sx.add(PE, MM(pb.t[0:16, :], wsm[:, kc, 0:16], hT[:, kc, :], start=(kc == 0), stop=(kc == KC - 1)),
                       reads=[wsm_r, hT_r[kc]], writes=[pb.r])
            Sx.add(ACT, ACP(glrT[0:16, :], pb.t[0:16, :]), reads=[pb.r, glrT_r], writes=[glrT_r])
            ppool.free(pb)

        def gla_head(l, tb, h):
            smT, smT_r, modT, modT_r, lay, lay_r = smT2[l % 2], smT2_r[l % 2], modT2[l % 2], modT2_r[l % 2], lay2[l % 2], lay2_r[l % 2]
            sm0 = 0
            ws = wload(l, "gla%d" % h)
            pz = ppool.alloc()
            for t in range(4):
                Sx.add(PE, MM(pz.t[:, t * 64:(t + 1) * 64], glrT[:, t * 128:(t + 1) * 128], wa2[:, h * 64:(h + 1) * 64]),
                       reads=[glrT_r, wa2_r], writes=[pz.r])
            for t in range(4):
                Sx.add(DVE, TT(lp[:, t, :], pz.t[:, t * 64:(t + 1) * 64], bcp[:, h * 64:(h + 1) * 64], ALU.add),
                       reads=[pz.r, bcp_r], writes=[lp_r])
            ppool.free(pz)
            lpv = lp[:].rearrange("p t d -> p (t d)")
            Sx.add(ACT, ACTF(lpv, lpv, AF.Exp, scale=-1.0), reads=[lp_r], writes=[lp_r])
            Sx.add(ACT, ACTF(lpv, lpv, AF.Ln, bias=1.0), reads=[lp_r], writes=[lp_r])
            pb = ppool.alloc()
            proj_F(ws, 384, 0, 128, pb)
            qg = bpool.alloc()
            Sx.add(ACT, ACTF(qg.t[0:64, :], pb.t[0:64, :], AF.Identity, scale=0.125), reads=[pb.r], writes=[qg.r])
            ppool.free(pb)
            pb = ppool.alloc()
            proj_T4(ws, 384, 128, 128, pb)
            v_t = bpool.alloc()
            Sx.add(ACT, ACP(v_t.t[:], pb.t[:]), reads=[pb.r], writes=[v_t.r])
            ppool.free(pb)
            pb = ppool.alloc()
            proj_F(ws, 384, 256, 128, pb)
            ggs = fpool.alloc()
            Sx.add(ACT, ACTF(ggs.t[:], pb.t[:], AF.Silu), reads=[pb.r], writes=[ggs.r])
            ppool.free(pb)
            pk = ppool.alloc()
            proj_T4(ws, 384, 64, 64, pk)
            wpool.free(ws)
            yield
            pe_ = ppool.alloc()
            for t in range(4):
                Sx.add(PE, MM(pe_.t[:, t * 64:(t + 1) * 64], CF("t3"), lp[:, t, :]), reads=[constf_r, lp_r], writes=[pe_.r])
            ef = fpool.alloc()
            Sx.add(ACT, ACTF(ef.t[:, 0:256], pe_.t[:, 0:256], AF.Exp, scale=-1.0 / 16.0), reads=[pe_.r], writes=[ef.r])
            ppool.free(pe_)
            pa = ppool.alloc()
            for t in range(4):
                Sx.add(PE, MM(pa.t[0:64, t * 2:t * 2 + 2], lp[:, t, :], CF("chunk_ind")),
                       reads=[lp_r, constf_r], writes=[pa.r])
            Sx.add(ACT, ACTF(a_gla[:], pa.t[0:64, 0:8], AF.Exp, scale=-1.0 / 16.0), reads=[pa.r], writes=[agla_r])
            ppool.free(pa)
            Sx.add(DVE, TT(kdec[:].rearrange("p t d -> p (t d)"), pk.t[:, 0:256], ef.t[:, 0:256], ALU.mult),
                   reads=[pk.r, ef.r], writes=[kdec_r])
            ppool.free(pk)
            fpool.free(ef)
            yield
            ps_kv = [ppool.alloc(), ppool.alloc()]
            for n in range(8):
                t, half = n // 2, n % 2
                rows = slice(half * 64, half * 64 + 64)
                kvb = ps_kv[n % 2]
                kvs = slice((n // 2) * 128, (n // 2) * 128 + 128)
                Sx.add(PE, MM(kvb.t[0:64, kvs], kdec[rows, t, :], v_t.t[rows, t * 128:(t + 1) * 128]),
                       reads=[kdec_r, v_t.r], writes=[kvb.r])
            bpool.free(v_t)
            sb8 = [bpool.alloc(), bpool.alloc()]
            for n in range(8):
                kvb = ps_kv[n % 2]
                kvs = slice((n // 2) * 128, (n // 2) * 128 + 128)
                if tb == 0 and n == 0:
                    Sx.add(DVE, CP(Sst[:, h, :], kvb.t[0:64, kvs]), reads=[kvb.r], writes=[Sst_r[h]])
                else:
                    Sx.add(DVE, STT(Sst[:, h, :], Sst[:, h, :], a_gla[:, n:n + 1], kvb.t[0:64, kvs], ALU.mult, ALU.add),
                           reads=[kvb.r, Sst_r[h], agla_r], writes=[Sst_r[h]])
                sbn = sb8[n // 4]
                Sx.add(ACT, ACP(sbn.t[0:64, (n % 4) * 128:(n % 4) * 128 + 128], Sst[:, h, :]), reads=[Sst_r[h]], writes=[sbn.r])
            ppool.free(ps_kv[0])
            ppool.free(ps_kv[1])
            yield
            ps_o = ppool.alloc()
            for n in range(8):
                sbn = sb8[n // 4]
                cs = slice(n * 64, n * 64 + 64)
                Sx.add(PE, MM(ps_o.t[:, cs], sbn.t[0:64, (n % 4) * 128:(n % 4) * 128 + 128], qg.t[0:64, cs]),
                       reads=[sbn.r, qg.r], writes=[ps_o.r])
            bpool.free(qg)
            bpool.free(sb8[0])
            bpool.free(sb8[1])
            o_f = fpool.alloc()
            Sx.add(ACT, ACP(o_f.t[:], ps_o.t[:]), reads=[ps_o.r], writes=[o_f.r])
            ppool.free(ps_o)

            def post(rstd):
                Sx.add(DVE, STT(o_f.t[:], o_f.t[:], smT[:, sm0 + SM_GLAG:sm0 + SM_GLAG + 1], rstd.t[:], ALU.mult, ALU.mult),
                       reads=[o_f.r, smT_r, rstd.r], writes=[o_f.r])
                Sx.add(DVE, TT(region[:, 12 + h, :], o_f.t[:], ggs.t[:], ALU.mult), reads=[o_f.r, ggs.r], writes=[reg_r[12 + h]])
            yield from rms_tail(o_f, None, None, post)
            fpool.free(o_f)
            fpool.free(ggs)

        def fox_pre(l, tb):
            pf = ppool.alloc()
            for t in range(4):
                for kc in range(KC):
                    Sx.add(PE, MM(pf.t[:, t * 4:t * 4 + 4], hT[:, kc, t * 128:(t + 1) * 128], wsm[:, kc, 16:20],
                                  start=(kc == 0), stop=(kc == KC - 1)),
                           reads=[wsm_r, hT_r[kc]], writes=[pf.r])
            for t in range(4):
                Sx.add(DVE, TT(lpf[:, t * 4:t * 4 + 4], pf.t[:, t * 4:t * 4 + 4], bcp[:, 256:260], ALU.add),
                       reads=[pf.r, bcp_r], writes=[lpf_r])
            ppool.free(pf)
            Sx.add(ACT, ACTF(lpf[:], lpf[:], AF.Exp, scale=-1.0), reads=[lpf_r], writes=[lpf_r])
            Sx.add(ACT, ACTF(lpf[:], lpf[:], AF.Ln, bias=1.0), reads=[lpf_r], writes=[lpf_r])
            yield
            pc = ppool.alloc()
            for t in range(4):
                for t2 in range(t + 1):
                    lhs = CF("tri_incl") if t2 == t else ones_f[:]
                    Sx.add(PE, MM(pc.t[:, t * 4:t * 4 + 4], lhs, lpf[:, t2 * 4:t2 * 4 + 4], start=(t2 == 0), stop=(t2 == t)),
                           reads=[constf_r, cbf_r, lpf_r], writes=[pc.r])
            if tb == 0:
                Sx.add(DVE, CP(ctok[:, 0:4, :].rearrange("p t h -> p (t h)"), pc.t[:, 0:16]), reads=[pc.r], writes=[ctok_r[tb]])
            else:
                for t in range(4):
                    Sx.add(DVE, TT(ctok[:, tb * 4 + t, :], pc.t[:, t * 4:t * 4 + 4], carry[:], ALU.add),
                           reads=[pc.r, carry_r], writes=[ctok_r[tb]])
            ppool.free(pc)
            pt_ = ppool.alloc()
            for t in range(4):
                Sx.add(PE, MM(pt_.t[:, 0:4], ones_f[:], lpf[:, t * 4:t * 4 + 4], start=(t == 0), stop=(t == 3)),
                       reads=[cbf_r, lpf_r], writes=[pt_.r])
            if tb == 0:
                Sx.add(DVE, CP(carry[:], pt_.t[:, 0:4]), reads=[pt_.r], writes=[carry_r])
            else:
                Sx.add(DVE, TT(carry[:], carry[:], pt_.t[:, 0:4], ALU.add), reads=[pt_.r, carry_r], writes=[carry_r])
            ppool.free(pt_)

        def fox_head(l, tb, h):
            smT, smT_r, modT, modT_r, lay, lay_r = smT2[l % 2], smT2_r[l % 2], modT2[l % 2], modT2_r[l % 2], lay2[l % 2], lay2_r[l % 2]
            sm0 = 0
            ws = wload(l, "fox%d" % h)
            pq = ppool.alloc()
            proj_F(ws, 384, 0, 128, pq)
            qf = fpool.alloc()
            Sx.add(ACT, ACP(qf.t[:], pq.t[:]), reads=[pq.r], writes=[qf.r])
            ppool.free(pq)
            sqq = bpool.alloc()
            Sx.add(DVE, TT(sqq.t[:], qf.t[:], qf.t[:], ALU.mult), reads=[qf.r], writes=[sqq.r])
            pk = ppool.alloc()
            proj_F(ws, 384, 128, 128, pk)
            kf = fpool.alloc()
            Sx.add(ACT, ACP(kf.t[:], pk.t[:]), reads=[pk.r], writes=[kf.r])
            ppool.free(pk)
            sqk = bpool.alloc()
            Sx.add(DVE, TT(sqk.t[:], kf.t[:], kf.t[:], ALU.mult), reads=[kf.r], writes=[sqk.r])
            pb = ppool.alloc()
            proj_T4(ws, 384, 256, 128, pb)
            wpool.free(ws)
            Sx.add(ACT, ACP(vcache[:, tb * 4:(tb + 1) * 4, h * 128:(h + 1) * 128], pb.t[:].rearrange("p (t e) -> p t e", t=4)),
                   reads=[pb.r], writes=[vc_r[h][tb]])
            ppool.free(pb)
            yield
            qh = bpool.alloc()
            for src, sq_, gain_ap, dst_ap, dst_res in ((qf, sqq, lay[:, 16:17], qh.t[:], qh.r),
                                                       (kf, sqk, smT[:, sm0 + SM_KG:sm0 + SM_KG + 1], kcache[:, h, tb * TB:(tb + 1) * TB], kc_r[h][tb])):
                p2 = ppool.alloc()
                Sx.add(PE, MM(p2.t[:], onesD_bf[:], sq_.t[:]), reads=[sq_.r, cbf_r], writes=[p2.r])
                bpool.free(sq_)
                rstd = fpool.alloc()
                Sx.add(ACT, ACTF(rstd.t[:], p2.t[:], AF.Ln, bias=eps_ap), reads=[p2.r, cbf_r], writes=[rstd.r])
                ppool.free(p2)
                Sx.add(ACT, ACTF(rstd.t[:], rstd.t[:], AF.Exp, scale=-0.5), reads=[rstd.r], writes=[rstd.r])
                Sx.add(DVE, STT(dst_ap, src.t[:], gain_ap, rstd.t[:], ALU.mult, ALU.mult),
                       reads=[src.r, rstd.r, lay_r, smT_r], writes=[dst_res])
                fpool.free(rstd)
                fpool.free(src)
            pa = ppool.alloc()
            for t in range(4):
                gt = tb * 4 + t
                Sx.add(PE, MM(pa.t[0:1, t * 128:(t + 1) * 128], ctok[:, gt, h:h + 1], identf),
                       reads=[ctok_r[tb], constf_r], writes=[pa.r])
            augs = bpool.alloc()
            Sx.add(DVE, MEMSET(augs.t[:], 0.0), writes=[augs.r])
            Sx.add(ACT, ACTF(augs.t[0:1, :], pa.t[0:1, :], AF.Identity, scale=-1.0), reads=[pa.r, augs.r], writes=[augs.r])
            ppool.free(pa)
            yield
            ps_o = ppool.alloc()
            ps_n = ppool.alloc()
            njb = tb * 4 + 4

            def scores(jb):
                tau = jb - tb * 4
                c0 = 0 if tau < 0 else tau * 128
                cs = slice(c0, TB)
                ps_s = ppool.alloc()
                Sx.add(PE, MM(ps_s.t[:, cs], kcache[:, h, jb * 128:(jb + 1) * 128], qh.t[:, cs], start=True, stop=False),
                       reads=[kc_r[h][jb // 4], qh.r], writes=[ps_s.r])
                Sx.add(PE, MM(ps_s.t[:, cs], onesrow0_bf[:], augs.t[:, cs], start=False, stop=(tau < 0)),
                       reads=[cbf_r, augs.r], writes=[ps_s.r])
                if tau >= 0:
                    Sx.add(PE, MM(ps_s.t[:, c0:c0 + 128], ident_bf[:], maskT_bf[:], start=False, stop=True),
                           reads=[cbf_r], writes=[ps_s.r])
                pt = bpool.alloc()
                Sx.add(ACT, ACTF(pt.t[:, cs], ps_s.t[:, cs], AF.Exp, bias=ctok[:, jb, h:h + 1]),
                       reads=[ps_s.r, ctok_r[jb // 4]], writes=[pt.r])
                ppool.free(ps_s)
                return pt, cs

            LOOK = int(_os.environ.get("MK_LOOK", "3"))
            pend = []
            nissued = 0
            while nissued < min(LOOK, njb):
                pend.append(scores(nissued))
                nissued += 1
            for jb in range(njb):
                pt, cs = pend.pop(0)
                if nissued < njb:
                    pend.append(scores(nissued))
                    nissued += 1
                Sx.add(PE, MM(ps_o.t[:, cs], vcache[:, jb, h * 128:(h + 1) * 128], pt.t[:, cs], start=(jb == 0), stop=(jb == njb - 1)),
                       reads=[vc_r[h][jb // 4], pt.r], writes=[ps_o.r])
                Sx.add(PE, MM(ps_n.t[:, cs], ones_bf[:], pt.t[:, cs], start=(jb == 0), stop=(jb == njb - 1)),
                       reads=[cbf_r, pt.r], writes=[ps_n.r])
                bpool.free(pt)
                if jb % 2 == 1 and jb + 1 < njb:
                    yield
            bpool.free(qh)
            bpool.free(augs)
            rs = fpool.alloc()
            Sx.add(DVE, (lambda e, o_=rs.t[:], i_=ps_n.t[:]: e.reciprocal(o_, i_)), reads=[ps_n.r], writes=[rs.r])
            ppool.free(ps_n)
            Sx.add(DVE, TT(region[:, 16 + h, :], ps_o.t[:], rs.t[:], ALU.mult), reads=[ps_o.r, rs.r], writes=[reg_r[16 + h]])
            ppool.free(ps_o)
            fpool.free(rs)

        def run_interleaved(items, width, background=None):
            items = list(items)
            live = []
            while items or live or background is not None:
                if background is not None:
                    try:
                        next(background)
                    except StopIteration:
                        background = None
                while items and len(live) < width:
                    k = items[0][0]
                    if k in ("gla", "foxpre", "fox") and any(k == lk for lk, _ in live):
                        break
                    if k == "fox" and any(lk == "foxpre" for lk, _ in live):
                        break
                    live.append(items.pop(0))
                for it in list(live):
                    try:
                        next(it[1])
                    except StopIteration:
                        live.remove(it)

        def emit_mixers(l, tb, background=None):
            gla_pre(l, tb)
            items = [("foxpre", fox_pre(l, tb))]
            for h in range(4):
                items += [("ret", ret_head(l, tb, h)), ("gla", gla_head(l, tb, h)), ("fox", fox_head(l, tb, h))]
            run_interleaved(items, int(_os.environ.get("MK_WIDTH", "3")), background)

        def emit_gate(l, tb):
            smT, smT_r, modT, modT_r, lay, lay_r = smT2[l % 2], smT2_r[l % 2], modT2[l % 2], modT2_r[l % 2], lay2[l % 2], lay2_r[l % 2]
            sm0 = 0
            for n in range(8):
                wm = wload(l, "mg%d" % n)
                wb = wload(l, "br%d" % n)
                mgv = wm.t[:, 0:3072].rearrange("p (k b j) -> p k b j", k=KC, b=3)
                brv = wb.t[:, 0:1536].rearrange("p (b k j) -> p b k j", b=3, k=4)
                prods = []
                for b in range(3):
                    pg = ppool.alloc()
                    for kc in range(KC):
                        Sx.add(PE, MM(pg.t[:], mgv[:, kc, b, :], hT[:, kc, :], start=(kc == 0), stop=(kc == KC - 1)),
                               reads=[wm.r, hT_r[kc]], writes=[pg.r])
                    py = ppool.alloc()
                    for k4 in range(4):
                        ch = 8 + b * 4 + k4
                        Sx.add(PE, MM(py.t[:], brv[:, b, k4, :], region[:, ch, :], start=(k4 == 0), stop=(k4 == 3)),
                               reads=[wb.r, reg_r[ch]], writes=[py.r])
                    sg = fpool.alloc()
                    Sx.add(ACT, ACTF(sg.t[:], pg.t[:], AF.Sigmoid, bias=smT[:, sm0 + SM_BMG + b * 8 + n:sm0 + SM_BMG + b * 8 + n + 1]),
                           reads=[pg.r, smT_r], writes=[sg.r])
                    ppool.free(pg)
                    Sx.add(DVE, TT(sg.t[:], py.t[:], sg.t[:], ALU.mult), reads=[py.r, sg.r], writes=[sg.r])
                    ppool.free(py)
                    prods.append(sg)
                wpool.free(wm)
                wpool.free(wb)
                Sx.add(DVE, TT(prods[0].t[:], prods[0].t[:], prods[1].t[:], ALU.add), reads=[prods[0].r, prods[1].r], writes=[prods[0].r])
                Sx.add(DVE, TT(region[:, n, :], prods[0].t[:], prods[2].t[:], ALU.add), reads=[prods[0].r, prods[2].r], writes=[reg_r[n]])
                for p_ in prods:
                    fpool.free(p_)
            for ch in range(2):
                ws = wload(l, "wo%d" % ch)
                wv = ws.t[:, 0:4096].rearrange("p (k n) -> p k n", k=KC)
                for dl in range(4):
                    dc = ch * 4 + dl
                    pb = ppool.alloc()
                    for n in range(8):
                        Sx.add(PE, MM(pb.t[:], wv[:, n, dl * 128:(dl + 1) * 128], region[:, n, :], start=(n == 0), stop=(n == 7)),
                               reads=[ws.r, reg_r[n]], writes=[pb.r])
                    xs = xT[:, dc, tb * TB:(tb + 1) * TB]
                    Sx.add(DVE, STT(xs, pb.t[:], modT[:, 16 + dc:17 + dc], xs, ALU.mult, ALU.add),
                           reads=[pb.r, modT_r, xT_r[dc][tb]], writes=[xT_r[dc][tb]])
                    ppool.free(pb)
                wpool.free(ws)

        def emit_ffn(l, tb, between=None):
            smT, smT_r, modT, modT_r, lay, lay_r = smT2[l % 2], smT2_r[l % 2], modT2[l % 2], modT2_r[l % 2], lay2[l % 2], lay2_r[l % 2]
            sm0 = 0
            S3 = int(_os.environ.get("MK_SUB3", "99"))

            def c3(k):
                if k > S3:
                    Sx.muted = True
            for i in range(6):
                nf = 4 if i < 5 else 2
                wdt = 128 * nf
                wu = wload(l, "up_u%d" % i)
                wg = wload(l, "up_g%d" % i)
                wuv = wu.t[:, 0:8 * wdt].rearrange("p (k n) -> p k n", k=KC)
                wgv = wg.t[:, 0:8 * wdt].rearrange("p (k n) -> p k n", k=KC)
                for j in range(nf):
                    f = i * 4 + j
                    pu = ppool.alloc()
                    pg = ppool.alloc()
                    for kc in range(KC):
                        Sx.add(PE, MM(pu.t[:], wuv[:, kc, j * 128:(j + 1) * 128], hT[:, kc, :], start=(kc == 0), stop=(kc == KC - 1)),
                               reads=[wu.r, hT_r[kc]], writes=[pu.r])
                    for kc in range(KC):
                        Sx.add(PE, MM(pg.t[:], wgv[:, kc, j * 128:(j + 1) * 128], hT[:, kc, :], start=(kc == 0), stop=(kc == KC - 1)),
                               reads=[wg.r, hT_r[kc]], writes=[pg.r])
                    u_sb = fpool.alloc()
                    Sx.add(ACT, ACP(u_sb.t[:], pu.t[:]), reads=[pu.r], writes=[u_sb.r])
                    ppool.free(pu)
                    c0 = fpool.alloc()
                    wc = sm0 + SM_WCONV
                    w2 = smT[:, wc + 2 * NFC + f:wc + 2 * NFC + f + 1]
                    w1 = smT[:, wc + NFC + f:wc + NFC + f + 1]
                    w0 = smT[:, wc + f:wc + f + 1]
                    Sx.add(DVE, TS(c0.t[:], u_sb.t[:], w2, smT[:, sm0 + SM_BCONV + f:sm0 + SM_BCONV + f + 1], ALU.mult, ALU.add),
                           reads=[u_sb.r, smT_r], writes=[c0.r])
                    Sx.add(DVE, STT(c0.t[:, 1:TB], u_sb.t[:, 0:TB - 1], w1, c0.t[:, 1:TB], ALU.mult, ALU.add),
                           reads=[u_sb.r, smT_r, c0.r], writes=[c0.r])
                    Sx.add(DVE, STT(c0.t[:, 2:TB], u_sb.t[:, 0:TB - 2], w0, c0.t[:, 2:TB], ALU.mult, ALU.add),
                           reads=[u_sb.r, smT_r, c0.r], writes=[c0.r])
                    if tb > 0:
                        Sx.add(DVE, STT(c0.t[:, 0:1], ucarry[:, f, 1:2], w1, c0.t[:, 0:1], ALU.mult, ALU.add),
                               reads=[ucarry_r[f], smT_r, c0.r], writes=[c0.r])
                        Sx.add(DVE, STT(c0.t[:, 0:2], ucarry[:, f, 0:2], w0, c0.t[:, 0:2], ALU.mult, ALU.add),
                               reads=[ucarry_r[f], smT_r, c0.r], writes=[c0.r])
                    Sx.add(DVE, CP(ucarry[:, f, :], u_sb.t[:, TB - 2:TB]), reads=[u_sb.r], writes=[ucarry_r[f]])
                    fpool.free(u_sb)
                    Sx.add(ACT, ACTF(c0.t[:], c0.t[:], AF.Silu), reads=[c0.r], writes=[c0.r])
                    c3(9)
                    Sx.add(DVE, TT(region[:, f, :], pg.t[:], c0.t[:], ALU.mult), reads=[pg.r, c0.r], writes=[reg_r[f]])
                    ppool.free(pg)
                    fpool.free(c0)
                wpool.free(wu)
                wpool.free(wg)
            for ch in range(2):
                if ch == 1 and between is not None:
                    between()
                accs = [ppool.alloc() for _ in range(4)]
                fc0 = 0
                for g, nfc in enumerate((8, 8, 6)):
                    ws = wload(l, "dn%d_%d" % (ch, g))
                    wv = ws.t[:, 0:nfc * 512].rearrange("p (k n) -> p k n", k=nfc)
                    for k in range(nfc):
                        fc = fc0 + k
                        for dl in range(4):
                            Sx.add(PE, MM(accs[dl].t[:], wv[:, k, dl * 128:(dl + 1) * 128], region[:, fc, :],
                                          start=(fc == 0), stop=(fc == NFC - 1)),
                                   reads=[ws.r, reg_r[fc]], writes=[accs[dl].r])
                    wpool.free(ws)
                    fc0 += nfc
                for dl in range(4):
                    dc = ch * 4 + dl
                    xs = xT[:, dc, tb * TB:(tb + 1) * TB]
                    Sx.add(DVE, STT(xs, accs[dl].t[:], modT[:, 40 + dc:41 + dc], xs, ALU.mult, ALU.add),
                           reads=[accs[dl].r, modT_r, xT_r[dc][tb]], writes=[xT_r[dc][tb]])
                    ppool.free(accs[dl])
            Sx.muted = False

        def emit_out(tb):
            for t in range(tb * 4, tb * 4 + 4):
                for half in range(2):
                    pb = ppool.alloc()
                    for j in range(4):
                        c = half * 4 + j
                        Sx.add(PE, TR(pb.t[:, j * 128:(j + 1) * 128], xT[:, c, t * 128:(t + 1) * 128], identf),
                               reads=[xT_r[c][t // 4], constf_r], writes=[pb.r])
                    so = fpool.alloc()
                    eng = ACT if half == 0 else DVE
                    fn_ = ACP(so.t[:], pb.t[:]) if eng == ACT else CP(so.t[:], pb.t[:])
                    Sx.add(eng, fn_, reads=[pb.r], writes=[so.r])
                    ppool.free(pb)
                    op = Sx.dma(SP, "f%d" % so.idx, DMA(out_d[t * 128:(t + 1) * 128, half * 512:(half + 1) * 512], so.t[:]), reads=[so.r])
                    final_ops.append(op)
                    fpool.free(so)

        import os
        STOP = int(os.environ.get("MK_STOP", "99"))
        NTB_RUN = int(os.environ.get("MK_NTB", str(NTB)))
        emit_mod(0)
        emit_norm(0, 0, 1)
        for l in range(n_layers):
            for tb in range(NTB):
                last = (tb == NTB - 1)
                nxt_mod = mod_gen(l + 1) if (last and l + 1 < n_layers) else None
                emit_mixers(l, tb, background=nxt_mod)
                if l == n_layers - 1 and tb > 0:
                    emit_out(tb - 1)
                if last and l + 1 < n_layers:
                    emit_layer_small_loads(l + 1)
                emit_gate(l, tb)
                emit_norm(l, tb, 2)
                if not last:
                    nxt = (lambda l_=l, t_=tb + 1: emit_norm(l_, t_, 1))
                elif l + 1 < n_layers:
                    nxt = (lambda l_=l + 1: emit_norm(l_, 0, 1))
                else:
                    nxt = None
                emit_ffn(l, tb, between=nxt)

        emit_out(NTB - 1)
        print("sbuf bytes remaining:", nc.sbuf_bytes_remaining, "ops:", {e: len(Sx.q[e]) for e in ENGS})
        Sx.emit(final_wait_ops=final_ops)
    return nc


_CACHE = {}


def _small_params(norm1_g, norm2_g, b_ada, b_mg, b_conv, w_conv, ret_norm_g, gla_norm_g, q_norm_g, k_norm_g, layers):
    L = len(layers)
    sm = np.zeros((128, L * NSM), np.float32)
    for i, l in enumerate(layers):
        o = i * NSM
        sm[:, o + SM_N1:o + SM_N1 + 8] = norm1_g[l].reshape(8, 128).T
        sm[:, o + SM_N2:o + SM_N2 + 8] = norm2_g[l].reshape(8, 128).T
        sm[:, o + SM_BADA:o + SM_BADA + 48] = b_ada[l].reshape(48, 128).T
        sm[:, o + SM_BMG:o + SM_BMG + 24] = b_mg[l].reshape(24, 128).T
        sm[:, o + SM_BCONV:o + SM_BCONV + 22] = b_conv[l].reshape(22, 128).T
        sm[:, o + SM_WCONV:o + SM_WCONV + 66] = w_conv[l].reshape(66, 128).T
        sm[:, o + SM_RETG:o + SM_RETG + 4] = ret_norm_g[l].reshape(4, 128).T
        sm[:, o + SM_GLAG] = gla_norm_g[l]
        sm[:, o + SM_QG] = q_norm_g[l]
        sm[:, o + SM_KG] = k_norm_g[l]
    return sm


def _prep_common(inputs, layers):
    L = len(layers)
    idx = list(layers)
    wall, offs, sizes, total = host_pack_weights(inputs["w_ada"][idx], inputs["w_in"][idx], inputs["w_br"][idx],
                                                 inputs["w_mg"][idx], inputs["w_o"][idx], inputs["w_up"][idx], inputs["w_down"][idx])
    wsm = np.zeros((L, 128, KC, 32), np.float32)
    for i, l in enumerate(layers):
        wi = inputs["w_in"][l]
        wsm[i, :, :, 0:16] = wi[:, 3072:3088].reshape(8, 128, 16).transpose(1, 0, 2)
        wsm[i, :, :, 16:20] = wi[:, 5136:5140].reshape(8, 128, 4).transpose(1, 0, 2)
    wsm = wsm.reshape(L, 128, KC * 32)
    wa2 = np.ascontiguousarray(inputs["w_gla_a2"][idx])
    sm = _small_params(inputs["norm1_g"], inputs["norm2_g"], inputs["b_ada"], inputs["b_mg"], inputs["b_conv"], inputs["w_conv"],
                       inputs["ret_norm_g"], inputs["gla_norm_g"], inputs["q_norm_g"], inputs["k_norm_g"], layers)
    bc = np.zeros((128, L * NBC), np.float32)
    for i, l in enumerate(layers):
        bc[:, i * NBC:i * NBC + 256] = inputs["b_gla_a"][l][None, :]
        bc[:, i * NBC + 256:i * NBC + 260] = inputs["b_fox_f"][l][None, :]
    c = host_consts()
    constf = np.concatenate([c[n].reshape(128, -1) for n, _ in CONST_F32], axis=1).astype(np.float32)
    rope = np.stack([c["rope_cos"], c["rope_sin"]], axis=1).astype(np.float32)
    common = {"wall": wall, "wsm": wsm, "wa2": wa2, "smT": sm, "bcp": bc, "constf": constf, "rope": rope}
    return common, offs, sizes, total


def run_layers(x, c, inputs, layers, dbg=None, ncores=NB, trace=False):
    common, offs, sizes, total = _prep_common(inputs, layers)
    import os
    key = (len(layers), total, str(dbg), os.environ.get("MK_STOP"), os.environ.get("MK_NTB"))
    if key not in _CACHE:
        _CACHE[key] = build_program(len(layers), total, offs, sizes, dbg=dbg)
    nc = _CACHE[key]
    in_maps = []
    for b in range(ncores):
        m = dict(common)
        m["x"] = np.ascontiguousarray(x[b])
        m["cT"] = np.ascontiguousarray(c[b].reshape(8, 128).T)
        in_maps.append(m)
    res = run_bass_kernel_spmd(nc, in_maps, core_ids=list(range(ncores)), **({"trace": True} if trace else {}))
    return res


def kernel(**inputs):
    inputs = {k: np.asarray(v) for k, v in inputs.items()}
    x = np.ascontiguousarray(inputs["x"], dtype=np.float32)
    c = np.ascontiguousarray(inputs["c"], dtype=np.float32)
    res = run_layers(x, c, inputs, list(range(DEPTH)))
    out = np.stack([np.asarray(r["out"]) for r in res.results], axis=0)
    return out.astype(np.float32)
```

```python
import contextlib
import numpy as np
import concourse.bass as bass
import concourse.mybir as mybir
from concourse.bass_utils import run_bass_kernel_spmd

F32 = mybir.dt.float32
BF16 = mybir.dt.bfloat16
AF = mybir.ActivationFunctionType
ALU = mybir.AluOpType

PE, ACT, DVE, POOL, SP = "pe", "act", "dve", "pool", "sp"
ENGS = [PE, ACT, DVE, POOL, SP]

D = 1024
S = 2048
NB = 8
DEPTH = 4
KC = 8
TB = 512
NTB = S // TB
NT = S // 128
D_FF = 2816
NFC = D_FF // 128
IN_W = 5140
EPS = 1e-6
WBUF_ELEMS = 4096
N_WBUF = 4
N_F32 = 8
N_BF = 14
NSM = 184
SM_N1, SM_N2, SM_BADA, SM_BMG, SM_BCONV, SM_WCONV, SM_RETG, SM_GLAG, SM_QG, SM_KG = 0, 8, 16, 64, 88, 110, 176, 180, 181, 182
DBG_TB = 0
NBC = 260


class Res:
    __slots__ = ("name", "w", "rs", "dma_rs", "excl")

    def __init__(self, name, excl=False):
        self.name = name
        self.w = None
        self.rs = {}
        self.dma_rs = []
        self.excl = excl


class Op:
    __slots__ = ("eng", "fn", "deps", "pos", "signal", "sigval", "is_dma", "sem", "semval")

    def __init__(self, eng, fn, is_dma=False):
        self.eng = eng
        self.fn = fn
        self.deps = ()
        self.pos = 0
        self.signal = False
        self.sigval = 0
        self.is_dma = is_dma
        self.sem = None
        self.semval = 0


class Sched:
    def __init__(self, nc):
        self.nc = nc
        self.q = {e: [] for e in ENGS}
        self.dma_sems = {}

    def _track(self, op, reads, writes):
        deps = {}
        for r in reads:
            if r.w is not None:
                deps[id(r.w)] = r.w
            if r.excl:
                for e_, o in r.rs.items():
                    if e_ != op.eng:
                        deps[id(o)] = o
        for w in writes:
            if w.w is not None:
                deps[id(w.w)] = w.w
            for o in w.rs.values():
                deps[id(o)] = o
            for o in w.dma_rs:
                deps[id(o)] = o
        deps.pop(id(op), None)
        op.deps = tuple(deps.values())
        for r in reads:
            if op.is_dma:
                r.dma_rs.append(op)
            else:
                r.rs[op.eng] = op
        for w in writes:
            w.w = op
            w.rs = {}
            w.dma_rs = []

    muted = False

    def add(self, eng, fn, reads=(), writes=()):
        if self.muted:
            return None
        op = Op(eng, fn)
        op.pos = len(self.q[eng])
        self.q[eng].append(op)
        self._track(op, reads, writes)
        return op

    def dma(self, queue, semkey, fn, reads=(), writes=()):
        if self.muted:
            return None
        op = Op(queue, fn, is_dma=True)
        op.pos = len(self.q[queue])
        self.q[queue].append(op)
        st = self.dma_sems.setdefault(semkey, [0, queue])
        assert st[1] == queue
        st[0] += 16
        op.sem = semkey
        op.semval = st[0]
        self._track(op, reads, writes)
        return op

    def emit(self, final_wait_ops=()):
        nc = self.nc
        for e in ENGS:
            for op in self.q[e]:
                for d in op.deps:
                    if d.is_dma:
                        continue
                    if d.eng != op.eng:
                        d.signal = True
                    elif op.eng != PE:
                        d.signal = True
        for e in ENGS:
            c = 0
            for op in self.q[e]:
                if op.signal and not op.is_dma:
                    c += 1
                    op.sigval = c
        with contextlib.ExitStack() as st:
            esem = {e: st.enter_context(nc.semaphore("s_" + e)) for e in ENGS}
            dsem = {k: st.enter_context(nc.semaphore("d_%s" % (k,))) for k in self.dma_sems}
            block = st.enter_context(nc.Block())
            sched = self

            def run(e, eng):
                waited = {}
                for op in sched.q[e]:
                    for d in op.deps:
                        if d.is_dma:
                            key = ("d", d.sem)
                            val = d.semval
                            sem = dsem[d.sem]
                        else:
                            if d.eng == e and e == PE:
                                continue
                            key = ("e", d.eng)
                            val = d.sigval
                            sem = esem[d.eng]
                        if waited.get(key, 0) >= val:
                            continue
                        waited[key] = val
                        eng.wait_ge(sem, val)
                    ins = op.fn(eng)
                    if op.is_dma:
                        ins.then_inc(dsem[op.sem], 16)
                    elif op.signal:
                        ins.then_inc(esem[e], 1)
                if e == SP:
                    for d in final_wait_ops:
                        if d is not None:
                            eng.wait_ge(dsem[d.sem], d.semval)

            @block.tensor
            def _(eng):
                run(PE, eng)

            @block.scalar
            def _(eng):
                run(ACT, eng)

            @block.vector
            def _(eng):
                run(DVE, eng)

            @block.gpsimd
            def _(eng):
                run(POOL, eng)

            @block.sync
            def _(eng):
                run(SP, eng)


class Slot:
    __slots__ = ("t", "r", "idx")

    def __init__(self, t, r, idx):
        self.t = t
        self.r = r
        self.idx = idx


class Pool_:
    def __init__(self, slots, name):
        self.free_list = list(slots)
        self.name = name

    def alloc(self):
        assert self.free_list, "pool %s exhausted" % self.name
        return self.free_list.pop(0)

    def free(self, s):
        self.free_list.append(s)


def MM(out, lhsT, rhs, start=True, stop=True):
    return lambda e: e.matmul(out, lhsT, rhs, start=start, stop=stop)


def TR(out, in_, ident):
    return lambda e: e.transpose(out, in_, ident)


def ACTF(out, in_, func, bias=None, scale=None):
    kw = {}
    if bias is not None:
        kw["bias"] = bias
    if scale is not None:
        kw["scale"] = scale
    return lambda e: e.activation(out, in_, func, **kw)


def ACP(out, in_):
    return lambda e: e.copy(out, in_)


def TT(out, a, b, op):
    return lambda e: e.tensor_tensor(out, a, b, op)


def TS(out, a, s1, s2, op0, op1=None):
    if op1 is None:
        return lambda e: e.tensor_scalar(out, a, s1, s2, op0)
    return lambda e: e.tensor_scalar(out, a, s1, s2, op0, op1)


def STT(out, in0, scalar, in1, op0, op1):
    return lambda e: e.scalar_tensor_tensor(out, in0, scalar, in1, op0, op1)


def CP(out, in_):
    return lambda e: e.tensor_copy(out, in_)


def MEMSET(ap, v):
    return lambda e: e.memset(ap, v)


def DMA(out, in_):
    return lambda e: e.dma_start(out=out, in_=in_)


def weight_stream_layout():
    head = [("ada%d" % i, 4096) for i in range(12)]
    tb = [("ret%d" % h, 4096) for h in range(4)]
    tb += [("gla%d" % h, 8 * 384) for h in range(4)]
    tb += [("fox%d" % h, 8 * 384) for h in range(4)]
    for n in range(8):
        tb += [("mg%d" % n, 3072), ("br%d" % n, 1536)]
    tb += [("wo0", 4096), ("wo1", 4096)]
    for i in range(6):
        wdt = 512 if i < 5 else 256
        tb += [("up_u%d" % i, 8 * wdt), ("up_g%d" % i, 8 * wdt)]
    for ch in range(2):
        for g, nfc in enumerate((8, 8, 6)):
            tb.append(("dn%d_%d" % (ch, g), nfc * 512))
    return head, tb


def host_pack_weights(w_ada, w_in, w_br, w_mg, w_o, w_up, w_down):
    L = w_in.shape[0]
    head, tb = weight_stream_layout()
    names = [n for n, _ in head] + [n for n, _ in tb]
    sizes = dict(head + tb)
    offs = {}
    o = 0
    for n in names:
        offs[n] = o
        o += sizes[n]
    total = o
    out = np.empty((L, 128, total), np.float32)

    def kcp(w2d):
        n = w2d.shape[1]
        return w2d.reshape(8, 128, n).transpose(1, 0, 2)

    for l in range(L):
        dst = out[l]

        def put(name, arr):
            a = arr.reshape(128, -1)
            assert a.shape[1] == sizes[name], (name, a.shape, sizes[name])
            dst[:, offs[name]:offs[name] + sizes[name]] = a

        for i in range(12):
            put("ada%d" % i, kcp(w_ada[l][:, i * 512:(i + 1) * 512]))
        wi = w_in[l]
        for h in range(4):
            cols = np.concatenate([np.arange(0 + 128 * h, 128 * h + 128), np.arange(512 + 128 * h, 512 + 128 * h + 128),
                                   np.arange(1024 + 128 * h, 1024 + 128 * h + 128), np.arange(1536 + 128 * h, 1536 + 128 * h + 128)])
            put("ret%d" % h, kcp(wi[:, cols]))
            cols = np.concatenate([np.arange(2048 + 64 * h, 2048 + 64 * h + 64), np.arange(2304 + 64 * h, 2304 + 64 * h + 64),
                                   np.arange(2560 + 128 * h, 2560 + 128 * h + 128), np.arange(3088 + 128 * h, 3088 + 128 * h + 128)])
            put("gla%d" % h, kcp(wi[:, cols]))
            cols = np.concatenate([np.arange(3600 + 128 * h, 3600 + 128 * h + 128), np.arange(4112 + 128 * h, 4112 + 128 * h + 128),
                                   np.arange(4624 + 128 * h, 4624 + 128 * h + 128)])
            put("fox%d" % h, kcp(wi[:, cols]))
        mg = w_mg[l].reshape(8, 128, 3, 8, 128)
        br = w_br[l].reshape(3, 4, 128, 8, 128)
        for n in range(8):
            put("mg%d" % n, mg[:, :, :, n, :].transpose(1, 0, 2, 3))
            put("br%d" % n, br[:, :, :, n, :].transpose(2, 0, 1, 3))
        put("wo0", kcp(w_o[l][:, 0:512]))
        put("wo1", kcp(w_o[l][:, 512:1024]))
        for i in range(6):
            wdt = 512 if i < 5 else 256
            put("up_u%d" % i, kcp(w_up[l][:, i * 512:i * 512 + wdt]))
            put("up_g%d" % i, kcp(w_up[l][:, D_FF + i * 512:D_FF + i * 512 + wdt]))
        wd = w_down[l].reshape(NFC, 128, D)
        fc0 = 0
        for g, nfc in enumerate((8, 8, 6)):
            for ch in range(2):
                put("dn%d_%d" % (ch, g), wd[fc0:fc0 + nfc, :, ch * 512:(ch + 1) * 512].transpose(1, 0, 2))
            fc0 += nfc
    return out, offs, sizes, total


def host_consts():
    c = {}
    half = 64
    inv_freq = 10000.0 ** (-np.arange(half, dtype=np.float64) / half)
    pos = np.arange(S, dtype=np.float64)
    ang = inv_freq[:, None] * pos[None, :]
    cos = np.cos(ang)
    sin = np.sin(ang)
    c["rope_cos"] = np.concatenate([cos, cos], 0).astype(np.float32)
    c["rope_sin"] = np.concatenate([sin, -sin], 0).astype(np.float32)
    log_g = np.log1p(-np.exp2(-5.0 - np.arange(4, dtype=np.float64)))
    sc = 128.0 ** -0.5
    i = np.arange(128)
    wt = np.zeros((4, 128, 128), np.float64)
    for h in range(4):
        jj, ii = np.meshgrid(i, i, indexing="ij")
        same = (jj // 64) == (ii // 64)
        later = (ii // 64) > (jj // 64)
        w = np.where(same, np.exp(np.abs(ii - jj) * log_g[h]), np.where(later, np.exp((ii - jj) * log_g[h]), 0.0))
        wt[h] = w * sc
    c["ret_wt"] = wt.transpose(1, 0, 2).reshape(128, 512).astype(np.float32)
    qd = np.stack([np.exp((i + 1.0) * log_g[h]) * sc for h in range(4)], 0)
    c["ret_qdec"] = np.broadcast_to(qd.reshape(1, 512), (128, 512)).astype(np.float32).copy()
    kw = np.stack([np.exp((127.0 - i) * log_g[h]) for h in range(4)], 1)
    c["ret_kw"] = kw.astype(np.float32)
    c["ret_g128"] = [float(np.exp(128.0 * log_g[h])) for h in range(4)]
    ident = np.eye(128, dtype=np.float32)
    c["ident"] = ident
    jj, ii = np.meshgrid(i, i, indexing="ij")
    c["tri_incl"] = (jj <= ii).astype(np.float32)
    c["t3"] = ((jj > ii) & ((jj // 64) == (ii // 64))).astype(np.float32)
    c["chunk_ind"] = np.stack([(i < 64), (i >= 64)], 1).astype(np.float32)
    c["maskT"] = np.where(jj > ii, -30000.0, 0.0).astype(np.float32)
    return c


CONST_F32 = [("ident", 128), ("tri_incl", 128), ("t3", 128), ("chunk_ind", 2), ("ret_wt", 512),
             ("ret_qdec", 512), ("ret_kw", 4), ("maskT", 128)]


def build_program(n_layers, layer_elems, offs, sizes, dbg=None):
    nc = bass.Bass("TRN2", target_bir_lowering=False)
    consts = host_consts()
    g128 = consts["ret_g128"]

    def dram(name, shape, dt=F32, kind="ExternalInput"):
        return nc.dram_tensor(name, list(shape), dt, kind=kind).ap()

    x_d = dram("x", [S, D])
    out_d = dram("out", [S, D], kind="ExternalOutput")
    cT_d = dram("cT", [128, KC])
    wall_d = dram("wall", [n_layers, 128, layer_elems])
    wsm_d = dram("wsm", [n_layers, 128, KC * 32])
    wa2_d = dram("wa2", [n_layers, 16, 256])
    smT_d = dram("smT", [128, n_layers * NSM])
    bc_d = dram("bcp", [128, n_layers * NBC])
    cf_w = sum(w for _, w in CONST_F32)
    constf_d = dram("constf", [128, cf_w])
    rope_d = dram("rope", [128, 2, S])
    dbg_d = {}
    if dbg:
        for name, shape, dt in dbg:
            dbg_d[name] = dram("dbg_" + name, shape, dt, kind="ExternalOutput")

    with contextlib.ExitStack() as st:
        def sb(name, shape, dt):
            return st.enter_context(nc.sbuf_tensor("sb_" + name, list(shape), dt))

        Sx = Sched(nc)
        xT = sb("xT", [128, KC, S], F32)
        xT_r = [[Res("xT%d_%d" % (c, t)) for t in range(NTB)] for c in range(KC)]
        hT = sb("hT", [128, KC, TB], BF16)
        hT_r = [Res("hT%d" % c) for c in range(KC)]
        region = sb("region", [128, NFC, TB], BF16)
        reg_r = [Res("reg%d" % i) for i in range(NFC)]
        kcache = sb("kcache", [128, 4, S], BF16)
        kc_r = [[Res("kc%d_%d" % (h, t)) for t in range(NTB)] for h in range(4)]
        vcache = sb("vcache", [128, NT, 512], BF16)
        vc_r = [[Res("vc%d_%d" % (h, t)) for t in range(NTB)] for h in range(4)]
        ctok = sb("ctok", [128, NT, 4], F32)
        ctok_r = [Res("ctok%d" % t) for t in range(NTB)]
        carry = sb("carry", [128, 4], F32)
        carry_r = Res("carry")
        wbufs = [Slot(sb("wb%d" % i, [128, WBUF_ELEMS], BF16), Res("wb%d" % i), i) for i in range(N_WBUF)]
        wpool = Pool_(wbufs, "w")
        f32s = [Slot(sb("f%d" % i, [128, TB], F32), Res("f%d" % i), i) for i in range(N_F32)]
        fpool = Pool_(f32s, "f32")
        bfs = [Slot(sb("b%d" % i, [128, TB], BF16), Res("b%d" % i), i) for i in range(N_BF)]
        bpool = Pool_(bfs, "bf16")
        banks = [Slot(st.enter_context(nc.psum_tensor("ps%d" % i, [128, TB], F32)), Res("ps%d" % i, excl=True), i) for i in range(8)]
        ppool = Pool_(banks, "psum")
        constf = sb("constf", [128, cf_w], F32)
        constf_r = Res("constf")
        coff = {}
        o = 0
        for n_, w_ in CONST_F32:
            coff[n_] = (o, w_)
            o += w_

        def CF(name, lo=0, hi=None):
            o_, w_ = coff[name]
            hi = w_ if hi is None else hi
            return constf[:, o_ + lo:o_ + hi]

        ident_bf = sb("ident_bf", [128, 128], BF16)
        maskT_bf = sb("maskT_bf", [128, 128], BF16)
        ones_bf = sb("ones_bf", [128, 128], BF16)
        onesD_bf = sb("onesD_bf", [128, 128], BF16)
        ones_f = sb("ones_f", [128, 128], F32)
        onesrow0_bf = sb("onesrow0_bf", [128, 128], BF16)
        cbf_r = Res("cbf")
        smT2 = [sb("smT%d" % i, [128, NSM], F32) for i in range(2)]
        smT2_r = [Res("smT%d" % i) for i in range(2)]
        cT = sb("cT", [128, KC], F32)
        cact = sb("cact", [128, KC], BF16)
        cact_r = Res("cact")
        modT2 = [sb("modT%d" % i, [128, 48], F32) for i in range(2)]
        modT2_r = [Res("modT%d" % i) for i in range(2)]
        lay2 = [sb("lay%d" % i, [128, 24], F32) for i in range(2)]
        lay2_r = [Res("lay%d" % i) for i in range(2)]
        bcp = sb("bcp", [128, NBC], F32)
        bcp_r = Res("bcp")
        wsm = sb("wsm", [128, KC, 32], BF16)
        wsm_r = Res("wsm")
        wa2 = sb("wa2", [16, 256], BF16)
        wa2_r = Res("wa2")
        eps_t = sb("eps_t", [128, 1], F32)
        eps_ap = eps_t[:, 0:1]
        Rst = sb("Rst", [128, 4, 128], F32)
        Rst_r = [Res("R%d" % h) for h in range(4)]
        Sst = sb("Sst", [64, 4, 128], F32)
        Sst_r = [Res("S%d" % h) for h in range(4)]
        ucarry = sb("ucarry", [128, NFC, 2], F32)
        ucarry_r = [Res("uc%d" % f) for f in range(NFC)]
        glrT = sb("glrT", [16, TB], BF16)
        glrT_r = Res("glrT")
        lp = sb("lp", [128, 4, 64], F32)
        lp_r = Res("lp")
        kdec = sb("kdec", [128, 4, 64], BF16)
        kdec_r = Res("kdec")
        a_gla = sb("a_gla", [64, 8], F32)
        agla_r = Res("agla")
        lpf = sb("lpf", [128, 16], F32)
        lpf_r = Res("lpf")

        final_ops = []

        def wload(l, name):
            s = wpool.alloc()
            n = sizes[name]
            o_ = offs[name]
            Sx.dma(POOL, "w%d" % s.idx, DMA(s.t[:, 0:n], wall_d[l, :, o_:o_ + n]), writes=[s.r])
            return s

        def dump(name, ap, reads):
            if name in dbg_d:
                op = Sx.dma(SP, "dbg_" + name, DMA(dbg_d[name], ap), reads=reads)
                final_ops.append(op)

        Sx.dma(SP, "constf", DMA(constf[:], constf_d), writes=[constf_r])
        Sx.dma(SP, "cT", DMA(cT[:], cT_d), writes=[cact_r])
        Sx.add(ACT, ACTF(cact[:], cT[:], AF.Silu), reads=[cact_r], writes=[cact_r])
        Sx.add(DVE, CP(ident_bf[:], CF("ident")), reads=[constf_r], writes=[cbf_r])
        Sx.add(DVE, CP(maskT_bf[:], CF("maskT")), reads=[constf_r], writes=[cbf_r])
        Sx.add(DVE, MEMSET(ones_bf[:], 1.0), writes=[cbf_r])
        Sx.add(DVE, MEMSET(onesD_bf[:], 1.0 / 128.0), writes=[cbf_r])
        Sx.add(DVE, MEMSET(ones_f[:], 1.0), writes=[cbf_r])
        Sx.add(DVE, MEMSET(onesrow0_bf[:], 0.0), writes=[cbf_r])
        Sx.add(DVE, MEMSET(onesrow0_bf[0:1, :], 1.0), writes=[cbf_r])
        Sx.add(DVE, MEMSET(eps_t[:], EPS), writes=[cbf_r])
        identf = CF("ident")

        for t in range(NT):
            s0 = fpool.alloc()
            s1 = fpool.alloc()
            Sx.dma(SP, "f%d" % s0.idx, DMA(s0.t[:], x_d[t * 128:(t + 1) * 128, 0:512]), writes=[s0.r])
            Sx.dma(SP, "f%d" % s1.idx, DMA(s1.t[:], x_d[t * 128:(t + 1) * 128, 512:1024]), writes=[s1.r])
            for half, sl in enumerate((s0, s1)):
                pb = ppool.alloc()
                for j in range(4):
                    Sx.add(PE, TR(pb.t[:, j * 128:(j + 1) * 128], sl.t[:, j * 128:(j + 1) * 128], identf),
                           reads=[sl.r, constf_r], writes=[pb.r])
                dst = xT[:, half * 4:(half + 1) * 4, t * 128:(t + 1) * 128]
                src = pb.t[:].rearrange("p (c n) -> p c n", c=4)
                eng = ACT if half == 0 else DVE
                fn = ACP(dst, src) if eng == ACT else CP(dst, src)
                Sx.add(eng, fn, reads=[pb.r], writes=[xT_r[c][t // 4] for c in range(half * 4, half * 4 + 4)])
                ppool.free(pb)
            fpool.free(s0)
            fpool.free(s1)

        def mod_gen(l):
            smT, smT_r, modT, modT_r, lay, lay_r = smT2[l % 2], smT2_r[l % 2], modT2[l % 2], modT2_r[l % 2], lay2[l % 2], lay2_r[l % 2]
            sm0 = 0
            Sx.dma(SP, "smT", DMA(smT[:], smT_d[:, l * NSM:(l + 1) * NSM]), writes=[smT_r])
            pm = ppool.alloc()
            for cb in range(12):
                ws = wload(l, "ada%d" % cb)
                wv = ws.t[:, 0:4096].rearrange("p (k n) -> p k n", k=KC)
                pb = ppool.alloc()
                for kc in range(KC):
                    Sx.add(PE, MM(pb.t[0:1, :], cact[:, kc:kc + 1], wv[:, kc, :], start=(kc == 0), stop=(kc == KC - 1)),
                           reads=[cact_r, ws.r], writes=[pb.r])
                row = fpool.alloc()
                Sx.add(ACT, ACP(row.t[0:1, :], pb.t[0:1, :]), reads=[pb.r], writes=[row.r])
                ppool.free(pb)
                wpool.free(ws)
                for jj in range(4):
                    j = cb * 4 + jj
                    Sx.add(PE, MM(pm.t[:, j:j + 1], row.t[0:1, jj * 128:(jj + 1) * 128], ones_f[0:1, 0:1]),
                           reads=[row.r, cbf_r], writes=[pm.r])
                fpool.free(row)
                yield
            Sx.add(DVE, TT(modT[:], pm.t[:, 0:48], smT[:, sm0 + SM_BADA:sm0 + SM_BADA + 48], ALU.add),
                   reads=[pm.r, smT_r], writes=[modT_r])
            ppool.free(pm)
            Sx.add(DVE, STT(lay[:, 0:8], modT[:, 8:16], 1.0, smT[:, sm0 + SM_N1:sm0 + SM_N1 + 8], ALU.add, ALU.mult),
                   reads=[modT_r, smT_r], writes=[lay_r])
            Sx.add(DVE, STT(lay[:, 8:16], modT[:, 32:40], 1.0, smT[:, sm0 + SM_N2:sm0 + SM_N2 + 8], ALU.add, ALU.mult),
                   reads=[modT_r, smT_r], writes=[lay_r])
            Sx.add(DVE, TS(lay[:, 16:17], smT[:, sm0 + SM_QG:sm0 + SM_QG + 1], 128.0 ** -0.5, None, ALU.mult),
                   reads=[smT_r], writes=[lay_r])

        def emit_layer_small_loads(l):
            Sx.dma(SP, "bcp", DMA(bcp[:], bc_d[:, l * NBC:(l + 1) * NBC]), writes=[bcp_r])
            Sx.dma(POOL, "wsm", DMA(wsm[:].rearrange("p k n -> p (k n)"), wsm_d[l]), writes=[wsm_r])
            Sx.dma(POOL, "wa2", DMA(wa2[:], wa2_d[l]), writes=[wa2_r])

        def emit_mod(l):
            for _ in mod_gen(l):
                pass
            emit_layer_small_loads(l)

        def emit_norm(l, tb, which):
            smT, smT_r, modT, modT_r, lay, lay_r = smT2[l % 2], smT2_r[l % 2], modT2[l % 2], modT2_r[l % 2], lay2[l % 2], lay2_r[l % 2]
            a_off = 0 if which == 1 else 8
            b_off = 0 if which == 1 else 24
            pb = ppool.alloc()
            for c in range(KC):
                sq = bpool.alloc()
                xs = xT[:, c, tb * TB:(tb + 1) * TB]
                Sx.add(ACT if c % 2 == 0 else DVE, (ACTF(sq.t[:], xs, AF.Square) if c % 2 == 0 else TT(sq.t[:], xs, xs, ALU.mult)), reads=[xT_r[c][tb]], writes=[sq.r])
                Sx.add(PE, MM(pb.t[:], ones_bf[:], sq.t[:], start=(c == 0), stop=(c == KC - 1)),
                       reads=[sq.r, cbf_r], writes=[pb.r])
                bpool.free(sq)
            rstd = fpool.alloc()
            Sx.add(ACT, ACTF(rstd.t[:], pb.t[:], AF.Ln, bias=eps_ap, scale=1.0 / D), reads=[pb.r, cbf_r], writes=[rstd.r])
            ppool.free(pb)
            Sx.add(ACT, ACTF(rstd.t[:], rstd.t[:], AF.Exp, scale=-0.5), reads=[rstd.r], writes=[rstd.r])
            for c in range(KC):
                tmp = fpool.alloc()
                xs = xT[:, c, tb * TB:(tb + 1) * TB]
                Sx.add(DVE, STT(tmp.t[:], xs, lay[:, a_off + c:a_off + c + 1], rstd.t[:], ALU.mult, ALU.mult),
                       reads=[xT_r[c][tb], lay_r, rstd.r], writes=[tmp.r])
                Sx.add(ACT, ACTF(hT[:, c, :], tmp.t[:], AF.Identity, bias=modT[:, b_off + c:b_off + c + 1]),
                       reads=[tmp.r, modT_r], writes=[hT_r[c]])
                fpool.free(tmp)
            fpool.free(rstd)

        def proj_F(ws, ncols_blk, col0, m, pb):
            wv = ws.t[:, 0:8 * ncols_blk].rearrange("p (k n) -> p k n", k=KC)
            for kc in range(KC):
                Sx.add(PE, MM(pb.t[0:m, :], wv[:, kc, col0:col0 + m], hT[:, kc, :], start=(kc == 0), stop=(kc == KC - 1)),
                       reads=[ws.r, hT_r[kc]], writes=[pb.r])

        def proj_T4(ws, ncols_blk, col0, n, pb):
            wv = ws.t[:, 0:8 * ncols_blk].rearrange("p (k n) -> p k n", k=KC)
            for t in range(4):
                for kc in range(KC):
                    Sx.add(PE, MM(pb.t[:, t * n:(t + 1) * n], hT[:, kc, t * 128:(t + 1) * 128], wv[:, kc, col0:col0 + n],
                                  start=(kc == 0), stop=(kc == KC - 1)),
                           reads=[ws.r, hT_r[kc]], writes=[pb.r])

        def part_norm_stats(src_f, want_mean):
            sq = bpool.alloc()
            Sx.add(DVE, TT(sq.t[:], src_f.t[:], src_f.t[:], ALU.mult), reads=[src_f.r], writes=[sq.r])
            p2 = ppool.alloc()
            Sx.add(PE, MM(p2.t[:], onesD_bf[:], sq.t[:]), reads=[sq.r, cbf_r], writes=[p2.r])
            bpool.free(sq)
            if want_mean:
                ob = bpool.alloc()
                Sx.add(ACT, ACP(ob.t[:], src_f.t[:]), reads=[src_f.r], writes=[ob.r])
                p1 = ppool.alloc()
                Sx.add(PE, MM(p1.t[:], onesD_bf[:], ob.t[:]), reads=[ob.r, cbf_r], writes=[p1.r])
                bpool.free(ob)
                mean = fpool.alloc()
                Sx.add(ACT, ACP(mean.t[:], p1.t[:]), reads=[p1.r], writes=[mean.r])
                ppool.free(p1)
                var = fpool.alloc()
                Sx.add(ACT, ACTF(var.t[:], mean.t[:], AF.Square), reads=[mean.r], writes=[var.r])
                Sx.add(DVE, TT(var.t[:], p2.t[:], var.t[:], ALU.subtract), reads=[p2.r, var.r], writes=[var.r])
                ppool.free(p2)
                Sx.add(ACT, ACTF(var.t[:], var.t[:], AF.Ln, bias=eps_ap), reads=[var.r, cbf_r], writes=[var.r])
                Sx.add(ACT, ACTF(var.t[:], var.t[:], AF.Exp, scale=-0.5), reads=[var.r], writes=[var.r])
                return mean, var
            rstd = fpool.alloc()
            Sx.add(ACT, ACTF(rstd.t[:], p2.t[:], AF.Ln, bias=eps_ap), reads=[p2.r, cbf_r], writes=[rstd.r])
            ppool.free(p2)
            Sx.add(ACT, ACTF(rstd.t[:], rstd.t[:], AF.Exp, scale=-0.5), reads=[rstd.r], writes=[rstd.r])
            return None, rstd

        import os as _os

        def rotary(pb, dst, cos_s, sin_s):
            t1 = fpool.alloc()
            t2 = fpool.alloc()
            Sx.add(DVE, TT(t1.t[:], pb.t[:], cos_s.t[:], ALU.mult), reads=[pb.r, cos_s.r], writes=[t1.r])
            Sx.add(DVE, TT(t2.t[0:64, :], pb.t[64:128, :], sin_s.t[64:128, :], ALU.mult), reads=[pb.r, sin_s.r], writes=[t2.r])
            Sx.add(DVE, TT(t2.t[64:128, :], pb.t[0:64, :], sin_s.t[0:64, :], ALU.mult), reads=[pb.r, sin_s.r], writes=[t2.r])
            Sx.add(DVE, TT(dst.t[:], t1.t[:], t2.t[:], ALU.add), reads=[t1.r, t2.r], writes=[dst.r])
            fpool.free(t1)
            fpool.free(t2)

        def ret_head(l, tb, h):
            smT, smT_r, modT, modT_r, lay, lay_r = smT2[l % 2], smT2_r[l % 2], modT2[l % 2], modT2_r[l % 2], lay2[l % 2], lay2_r[l % 2]
            sm0 = 0
            ws = wload(l, "ret%d" % h)
            cos_s = fpool.alloc()
            sin_s = fpool.alloc()
            Sx.dma(SP, "f%d" % cos_s.idx, DMA(cos_s.t[:], rope_d[:, 0, tb * TB:(tb + 1) * TB]), writes=[cos_s.r])
            Sx.dma(SP, "f%d" % sin_s.idx, DMA(sin_s.t[:], rope_d[:, 1, tb * TB:(tb + 1) * TB]), writes=[sin_s.r])
            q_rot = bpool.alloc()
            k_rot = bpool.alloc()
            pb = ppool.alloc()
            proj_F(ws, 512, 0, 128, pb)
            rotary(pb, q_rot, cos_s, sin_s)
            ppool.free(pb)
            pb = ppool.alloc()
            proj_F(ws, 512, 128, 128, pb)
            rotary(pb, k_rot, cos_s, sin_s)
            ppool.free(pb)
            fpool.free(cos_s)
            fpool.free(sin_s)
            pb = ppool.alloc()
            proj_T4(ws, 512, 256, 128, pb)
            v_t = bpool.alloc()
            Sx.add(ACT, ACP(v_t.t[:], pb.t[:]), reads=[pb.r], writes=[v_t.r])
            ppool.free(pb)
            pb = ppool.alloc()
            proj_F(ws, 512, 384, 128, pb)
            wpool.free(ws)
            rgs = fpool.alloc()
            Sx.add(ACT, ACTF(rgs.t[:], pb.t[:], AF.Silu), reads=[pb.r], writes=[rgs.r])
            ppool.free(pb)
            yield
            pk = ppool.alloc()
            pk_bf = pk.t[:].bitcast(BF16)
            for t in range(4):
                Sx.add(PE, TR(pk_bf[:, t * 128:(t + 1) * 128], k_rot.t[:, t * 128:(t + 1) * 128], ident_bf[:]),
                       reads=[k_rot.r, cbf_r], writes=[pk.r])
            kwtok = bpool.alloc()
            Sx.add(DVE, TS(kwtok.t[:], pk_bf[:, 0:512], CF("ret_kw", h, h + 1), None, ALU.mult),
                   reads=[pk.r, constf_r], writes=[kwtok.r])
            ppool.free(pk)
            ps_s = ppool.alloc()
            for t in range(4):
                sl = slice(t * 128, (t + 1) * 128)
                Sx.add(PE, MM(ps_s.t[:, sl], k_rot.t[:, sl], q_rot.t[:, sl]),
                       reads=[k_rot.r, q_rot.r], writes=[ps_s.r])
            pt = bpool.alloc()
            for t in range(4):
                sl = slice(t * 128, (t + 1) * 128)
                Sx.add(DVE, TT(pt.t[:, sl], ps_s.t[:, sl], CF("ret_wt", h * 128, (h + 1) * 128), ALU.mult),
                       reads=[ps_s.r, constf_r], writes=[pt.r])
            ppool.free(ps_s)
            qw = bpool.alloc()
            for t in range(4):
                sl = slice(t * 128, (t + 1) * 128)
                Sx.add(DVE, TT(qw.t[:, sl], q_rot.t[:, sl], CF("ret_qdec", h * 128, (h + 1) * 128), ALU.mult),
                       reads=[q_rot.r, constf_r], writes=[qw.r])
            bpool.free(q_rot)
            bpool.free(k_rot)
            yield
            ps_kv = ppool.alloc()
            for t in range(4):
                sl = slice(t * 128, (t + 1) * 128)
                Sx.add(PE, MM(ps_kv.t[:, sl], kwtok.t[:, sl], v_t.t[:, sl]),
                       reads=[kwtok.r, v_t.r], writes=[ps_kv.r])
            bpool.free(kwtok)
            rb = bpool.alloc()
            for t in range(4):
                sl = slice(t * 128, (t + 1) * 128)
                gt = tb * 4 + t
                if gt > 0:
                    Sx.add(ACT, ACP(rb.t[:, sl], Rst[:, h, :]), reads=[Rst_r[h]], writes=[rb.r])
                if gt == 0:
                    Sx.add(DVE, CP(Rst[:, h, :], ps_kv.t[:, sl]), reads=[ps_kv.r], writes=[Rst_r[h]])
                else:
                    Sx.add(DVE, STT(Rst[:, h, :], Rst[:, h, :], g128[h], ps_kv.t[:, sl], ALU.mult, ALU.add),
                           reads=[ps_kv.r, Rst_r[h]], writes=[Rst_r[h]])
            ppool.free(ps_kv)
            yield
            ps_o = ppool.alloc()
            for t in range(4):
                sl = slice(t * 128, (t + 1) * 128)
                gt = tb * 4 + t
                Sx.add(PE, MM(ps_o.t[:, sl], v_t.t[:, sl], pt.t[:, sl], start=True, stop=(gt == 0)),
                       reads=[v_t.r, pt.r], writes=[ps_o.r])
                if gt > 0:
                    Sx.add(PE, MM(ps_o.t[:, sl], rb.t[:, sl], qw.t[:, sl], start=False, stop=True),
                           reads=[rb.r, qw.r], writes=[ps_o.r])
            for s_ in (pt, qw, v_t, rb):
                bpool.free(s_)
            o_f = fpool.alloc()
            Sx.add(ACT, ACP(o_f.t[:], ps_o.t[:]), reads=[ps_o.r], writes=[o_f.r])
            ppool.free(ps_o)
            ob = bpool.alloc()
            Sx.add(ACT, ACP(ob.t[:], o_f.t[:]), reads=[o_f.r], writes=[ob.r])
            yield
            p1 = ppool.alloc()
            Sx.add(PE, MM(p1.t[:], onesD_bf[:], ob.t[:]), reads=[ob.r, cbf_r], writes=[p1.r])
            bpool.free(ob)
            Sx.add(DVE, TT(o_f.t[:], o_f.t[:], p1.t[:], ALU.subtract), reads=[o_f.r, p1.r], writes=[o_f.r])
            ppool.free(p1)
            sq = bpool.alloc()
            Sx.add(ACT, ACTF(sq.t[:], o_f.t[:], AF.Square), reads=[o_f.r], writes=[sq.r])
            yield
            p2 = ppool.alloc()
            Sx.add(PE, MM(p2.t[:], onesD_bf[:], sq.t[:]), reads=[sq.r, cbf_r], writes=[p2.r])
            bpool.free(sq)
            rstd = fpool.alloc()
            Sx.add(ACT, ACTF(rstd.t[:], p2.t[:], AF.Ln, bias=eps_ap), reads=[p2.r, cbf_r], writes=[rstd.r])
            ppool.free(p2)
            Sx.add(ACT, ACTF(rstd.t[:], rstd.t[:], AF.Exp, scale=-0.5), reads=[rstd.r], writes=[rstd.r])
            Sx.add(DVE, TT(o_f.t[:], o_f.t[:], rstd.t[:], ALU.mult), reads=[o_f.r, rstd.r], writes=[o_f.r])
            Sx.add(DVE, STT(region[:, 8 + h, :], o_f.t[:], smT[:, sm0 + SM_RETG + h:sm0 + SM_RETG + h + 1], rgs.t[:], ALU.mult, ALU.mult),
                   reads=[o_f.r, smT_r, rgs.r], writes=[reg_r[8 + h]])
            for s_ in (rstd, o_f, rgs):
                fpool.free(s_)

        def rms_tail(o_f, gain_ap, extra_reads, post):
            sq = bpool.alloc()
            Sx.add(ACT, ACTF(sq.t[:], o_f.t[:], AF.Square), reads=[o_f.r], writes=[sq.r])
            yield
            p2 = ppool.alloc()
            Sx.add(PE, MM(p2.t[:], onesD_bf[:], sq.t[:]), reads=[sq.r, cbf_r], writes=[p2.r])
            bpool.free(sq)
            rstd = fpool.alloc()
            Sx.add(ACT, ACTF(rstd.t[:], p2.t[:], AF.Ln, bias=eps_ap), reads=[p2.r, cbf_r], writes=[rstd.r])
            ppool.free(p2)
            Sx.add(ACT, ACTF(rstd.t[:], rstd.t[:], AF.Exp, scale=-0.5), reads=[rstd.r], writes=[rstd.r])
            post(rstd)
            fpool.free(rstd)

        def gla_pre(l, tb):
            pb = ppool.alloc()
            for kc in range(KC):
                Sx.add(PE, MM(pb.t[0:16, :], wsm[:, kc, 0:16], hT[:, kc, :], start=(kc == 0), stop=(kc == KC - 1)),
                       reads=[wsm_r, hT_r[kc]], writes=[pb.r])
            Sx.add(ACT, ACP(glrT[:], pb.t[0:16, :]), reads=[pb.r], writes=[glrT_r])
            ppool.free(pb)

        def gla_head(l, tb, h):
            smT, smT_r, modT, modT_r, lay, lay_r = smT2[l % 2], smT2_r[l % 2], modT2[l % 2], modT2_r[l % 2], lay2[l % 2], lay2_r[l % 2]
            sm0 = 0
            ws = wload(l, "gla%d" % h)
            pz = ppool.alloc()
            for t in range(4):
                Sx.add(PE, MM(pz.t[:, t * 64:(t + 1) * 64], glrT[:, t * 128:(t + 1) * 128], wa2[:, h * 64:(h + 1) * 64]),
                       reads=[glrT_r, wa2_r], writes=[pz.r])
            for t in range(4):
                Sx.add(DVE, TT(lp[:, t, :], pz.t[:, t * 64:(t + 1) * 64], bcp[:, h * 64:(h + 1) * 64], ALU.add),
                       reads=[pz.r, bcp_r], writes=[lp_r])
            ppool.free(pz)
            lpv = lp[:].rearrange("p t d -> p (t d)")
            Sx.add(ACT, ACTF(lpv, lpv, AF.Exp, scale=-1.0), reads=[lp_r], writes=[lp_r])
            Sx.add(ACT, ACTF(lpv, lpv, AF.Ln, bias=1.0), reads=[lp_r], writes=[lp_r])
            pb = ppool.alloc()
            proj_F(ws, 384, 0, 128, pb)
            qg = bpool.alloc()
            Sx.add(ACT, ACTF(qg.t[0:64, :], pb.t[0:64, :], AF.Identity, scale=0.125), reads=[pb.r], writes=[qg.r])
            ppool.free(pb)
            pb = ppool.alloc()
            proj_T4(ws, 384, 128, 128, pb)
            v_t = bpool.alloc()
            Sx.add(ACT, ACP(v_t.t[:], pb.t[:]), reads=[pb.r], writes=[v_t.r])
            ppool.free(pb)
            pb = ppool.alloc()
            proj_F(ws, 384, 256, 128, pb)
            ggs = fpool.alloc()
            Sx.add(ACT, ACTF(ggs.t[:], pb.t[:], AF.Silu), reads=[pb.r], writes=[ggs.r])
            ppool.free(pb)
            pk = ppool.alloc()
            proj_T4(ws, 384, 64, 64, pk)
            wpool.free(ws)
            yield
            pe_ = ppool.alloc()
            for t in range(4):
                Sx.add(PE, MM(pe_.t[:, t * 64:(t + 1) * 64], CF("t3"), lp[:, t, :]), reads=[constf_r, lp_r], writes=[pe_.r])
            ef = fpool.alloc()
            Sx.add(ACT, ACTF(ef.t[:, 0:256], pe_.t[:, 0:256], AF.Exp, scale=-1.0 / 16.0), reads=[pe_.r], writes=[ef.r])
            ppool.free(pe_)
            pa = ppool.alloc()
            for t in range(4):
                Sx.add(PE, MM(pa.t[0:64, t * 2:t * 2 + 2], lp[:, t, :], CF("chunk_ind")),
                       reads=[lp_r, constf_r], writes=[pa.r])
            Sx.add(ACT, ACTF(a_gla[:], pa.t[0:64, 0:8], AF.Exp, scale=-1.0 / 16.0), reads=[pa.r], writes=[agla_r])
            ppool.free(pa)
            Sx.add(DVE, TT(kdec[:].rearrange("p t d -> p (t d)"), pk.t[:, 0:256], ef.t[:, 0:256], ALU.mult),
                   reads=[pk.r, ef.r], writes=[kdec_r])
            ppool.free(pk)
            fpool.free(ef)
            yield
            ps_kv = [ppool.alloc(), ppool.alloc()]
            for n in range(8):
                t, half = n // 2, n % 2
                rows = slice(half * 64, half * 64 + 64)
                kvb = ps_kv[n % 2]
                kvs = slice((n // 2) * 128, (n // 2) * 128 + 128)
                Sx.add(PE, MM(kvb.t[0:64, kvs], kdec[rows, t, :], v_t.t[rows, t * 128:(t + 1) * 128]),
                       reads=[kdec_r, v_t.r], writes=[kvb.r])
            bpool.free(v_t)
            sb8 = [bpool.alloc(), bpool.alloc()]
            for n in range(8):
                kvb = ps_kv[n % 2]
                kvs = slice((n // 2) * 128, (n // 2) * 128 + 128)
                if tb == 0 and n == 0:
                    Sx.add(DVE, CP(Sst[:, h, :], kvb.t[0:64, kvs]), reads=[kvb.r], writes=[Sst_r[h]])
                else:
                    Sx.add(DVE, STT(Sst[:, h, :], Sst[:, h, :], a_gla[:, n:n + 1], kvb.t[0:64, kvs], ALU.mult, ALU.add),
                           reads=[kvb.r, Sst_r[h], agla_r], writes=[Sst_r[h]])
                sbn = sb8[n // 4]
                Sx.add(ACT, ACP(sbn.t[0:64, (n % 4) * 128:(n % 4) * 128 + 128], Sst[:, h, :]), reads=[Sst_r[h]], writes=[sbn.r])
            ppool.free(ps_kv[0])
            ppool.free(ps_kv[1])
            yield
            ps_o = ppool.alloc()
            for n in range(8):
                sbn = sb8[n // 4]
                cs = slice(n * 64, n * 64 + 64)
                Sx.add(PE, MM(ps_o.t[:, cs], sbn.t[0:64, (n % 4) * 128:(n % 4) * 128 + 128], qg.t[0:64, cs]),
                       reads=[sbn.r, qg.r], writes=[ps_o.r])
            bpool.free(qg)
            bpool.free(sb8[0])
            bpool.free(sb8[1])
            o_f = fpool.alloc()
            Sx.add(ACT, ACP(o_f.t[:], ps_o.t[:]), reads=[ps_o.r], writes=[o_f.r])
            ppool.free(ps_o)

            def post(rstd):
                Sx.add(DVE, STT(o_f.t[:], o_f.t[:], smT[:, sm0 + SM_GLAG:sm0 + SM_GLAG + 1], rstd.t[:], ALU.mult, ALU.mult),
                       reads=[o_f.r, smT_r, rstd.r], writes=[o_f.r])
                Sx.add(DVE, TT(region[:, 12 + h, :], o_f.t[:], ggs.t[:], ALU.mult), reads=[o_f.r, ggs.r], writes=[reg_r[12 + h]])
            yield from rms_tail(o_f, None, None, post)
            fpool.free(o_f)
            fpool.free(ggs)

        def fox_pre(l, tb):
            pf = ppool.alloc()
            for t in range(4):
                for kc in range(KC):
                    Sx.add(PE, MM(pf.t[:, t * 4:t * 4 + 4], hT[:, kc, t * 128:(t + 1) * 128], wsm[:, kc, 16:20],
                                  start=(kc == 0), stop=(kc == KC - 1)),
                           reads=[wsm_r, hT_r[kc]], writes=[pf.r])
            for t in range(4):
                Sx.add(DVE, TT(lpf[:, t * 4:t * 4 + 4], pf.t[:, t * 4:t * 4 + 4], bcp[:, 256:260], ALU.add),
                       reads=[pf.r, bcp_r], writes=[lpf_r])
            ppool.free(pf)
            Sx.add(ACT, ACTF(lpf[:], lpf[:], AF.Exp, scale=-1.0), reads=[lpf_r], writes=[lpf_r])
            Sx.add(ACT, ACTF(lpf[:], lpf[:], AF.Ln, bias=1.0), reads=[lpf_r], writes=[lpf_r])
            yield
            pc = ppool.alloc()
            for t in range(4):
                for t2 in range(t + 1):
                    lhs = CF("tri_incl") if t2 == t else ones_f[:]
                    Sx.add(PE, MM(pc.t[:, t * 4:t * 4 + 4], lhs, lpf[:, t2 * 4:t2 * 4 + 4], start=(t2 == 0), stop=(t2 == t)),
                           reads=[constf_r, cbf_r, lpf_r], writes=[pc.r])
            if tb == 0:
                Sx.add(DVE, CP(ctok[:, 0:4, :].rearrange("p t h -> p (t h)"), pc.t[:, 0:16]), reads=[pc.r], writes=[ctok_r[tb]])
            else:
                for t in range(4):
                    Sx.add(DVE, TT(ctok[:, tb * 4 + t, :], pc.t[:, t * 4:t * 4 + 4], carry[:], ALU.add),
                           reads=[pc.r, carry_r], writes=[ctok_r[tb]])
            ppool.free(pc)
            pt_ = ppool.alloc()
            for t in range(4):
                Sx.add(PE, MM(pt_.t[:, 0:4], ones_f[:], lpf[:, t * 4:t * 4 + 4], start=(t == 0), stop=(t == 3)),
                       reads=[cbf_r, lpf_r], writes=[pt_.r])
            if tb == 0:
                Sx.add(DVE, CP(carry[:], pt_.t[:, 0:4]), reads=[pt_.r], writes=[carry_r])
            else:
                Sx.add(DVE, TT(carry[:], carry[:], pt_.t[:, 0:4], ALU.add), reads=[pt_.r, carry_r], writes=[carry_r])
            ppool.free(pt_)

        def fox_head(l, tb, h):
            smT, smT_r, modT, modT_r, lay, lay_r = smT2[l % 2], smT2_r[l % 2], modT2[l % 2], modT2_r[l % 2], lay2[l % 2], lay2_r[l % 2]
            sm0 = 0
            ws = wload(l, "fox%d" % h)
            pq = ppool.alloc()
            proj_F(ws, 384, 0, 128, pq)
            qf = fpool.alloc()
            Sx.add(ACT, ACP(qf.t[:], pq.t[:]), reads=[pq.r], writes=[qf.r])
            ppool.free(pq)
            sqq = bpool.alloc()
            Sx.add(DVE, TT(sqq.t[:], qf.t[:], qf.t[:], ALU.mult), reads=[qf.r], writes=[sqq.r])
            pk = ppool.alloc()
            proj_F(ws, 384, 128, 128, pk)
            kf = fpool.alloc()
            Sx.add(ACT, ACP(kf.t[:], pk.t[:]), reads=[pk.r], writes=[kf.r])
            ppool.free(pk)
            sqk = bpool.alloc()
            Sx.add(DVE, TT(sqk.t[:], kf.t[:], kf.t[:], ALU.mult), reads=[kf.r], writes=[sqk.r])
            pb = ppool.alloc()
            proj_T4(ws, 384, 256, 128, pb)
            wpool.free(ws)
            Sx.add(ACT, ACP(vcache[:, tb * 4:(tb + 1) * 4, h * 128:(h + 1) * 128], pb.t[:].rearrange("p (t e) -> p t e", t=4)),
                   reads=[pb.r], writes=[vc_r[h][tb]])
            ppool.free(pb)
            yield
            qh = bpool.alloc()
            for src, sq_, gain_ap, dst_ap, dst_res in ((qf, sqq, lay[:, 16:17], qh.t[:], qh.r),
                                                       (kf, sqk, smT[:, sm0 + SM_KG:sm0 + SM_KG + 1], kcache[:, h, tb * TB:(tb + 1) * TB], kc_r[h][tb])):
                p2 = ppool.alloc()
                Sx.add(PE, MM(p2.t[:], onesD_bf[:], sq_.t[:]), reads=[sq_.r, cbf_r], writes=[p2.r])
                bpool.free(sq_)
                rstd = fpool.alloc()
                Sx.add(ACT, ACTF(rstd.t[:], p2.t[:], AF.Ln, bias=eps_ap), reads=[p2.r, cbf_r], writes=[rstd.r])
                ppool.free(p2)
                Sx.add(ACT, ACTF(rstd.t[:], rstd.t[:], AF.Exp, scale=-0.5), reads=[rstd.r], writes=[rstd.r])
                Sx.add(DVE, STT(dst_ap, src.t[:], gain_ap, rstd.t[:], ALU.mult, ALU.mult),
                       reads=[src.r, rstd.r, lay_r, smT_r], writes=[dst_res])
                fpool.free(rstd)
                fpool.free(src)
            pa = ppool.alloc()
            for t in range(4):
                gt = tb * 4 + t
                Sx.add(PE, MM(pa.t[0:1, t * 128:(t + 1) * 128], ctok[:, gt, h:h + 1], identf),
                       reads=[ctok_r[tb], constf_r], writes=[pa.r])
            augs = bpool.alloc()
            Sx.add(DVE, MEMSET(augs.t[:], 0.0), writes=[augs.r])
            Sx.add(ACT, ACTF(augs.t[0:1, :], pa.t[0:1, :], AF.Identity, scale=-1.0), reads=[pa.r, augs.r], writes=[augs.r])
            ppool.free(pa)
            yield
            ps_o = ppool.alloc()
            ps_n = ppool.alloc()
            njb = tb * 4 + 4

            def scores(jb):
                tau = jb - tb * 4
                c0 = 0 if tau < 0 else tau * 128
                cs = slice(c0, TB)
                ps_s = ppool.alloc()
                Sx.add(PE, MM(ps_s.t[:, cs], kcache[:, h, jb * 128:(jb + 1) * 128], qh.t[:, cs], start=True, stop=False),
                       reads=[kc_r[h][jb // 4], qh.r], writes=[ps_s.r])
                Sx.add(PE, MM(ps_s.t[:, cs], onesrow0_bf[:], augs.t[:, cs], start=False, stop=(tau < 0)),
                       reads=[cbf_r, augs.r], writes=[ps_s.r])
                if tau >= 0:
                    Sx.add(PE, MM(ps_s.t[:, c0:c0 + 128], ident_bf[:], maskT_bf[:], start=False, stop=True),
                           reads=[cbf_r], writes=[ps_s.r])
                pt = bpool.alloc()
                Sx.add(ACT, ACTF(pt.t[:, cs], ps_s.t[:, cs], AF.Exp, bias=ctok[:, jb, h:h + 1]),
                       reads=[ps_s.r, ctok_r[jb // 4]], writes=[pt.r])
                ppool.free(ps_s)
                return pt, cs

            LOOK = int(_os.environ.get("MK_LOOK", "3"))
            pend = []
            nissued = 0
            while nissued < min(LOOK, njb):
                pend.append(scores(nissued))
                nissued += 1
            for jb in range(njb):
                pt, cs = pend.pop(0)
                if nissued < njb:
                    pend.append(scores(nissued))
                    nissued += 1
                Sx.add(PE, MM(ps_o.t[:, cs], vcache[:, jb, h * 128:(h + 1) * 128], pt.t[:, cs], start=(jb == 0), stop=(jb == njb - 1)),
                       reads=[vc_r[h][jb // 4], pt.r], writes=[ps_o.r])
                Sx.add(PE, MM(ps_n.t[:, cs], ones_bf[:], pt.t[:, cs], start=(jb == 0), stop=(jb == njb - 1)),
                       reads=[cbf_r, pt.r], writes=[ps_n.r])
                bpool.free(pt)
                if jb % 3 == 2 and jb + 1 < njb:
                    yield
            bpool.free(qh)
            bpool.free(augs)
            rs = fpool.alloc()
            Sx.add(DVE, (lambda e, o_=rs.t[:], i_=ps_n.t[:]: e.reciprocal(o_, i_)), reads=[ps_n.r], writes=[rs.r])
            ppool.free(ps_n)
            Sx.add(DVE, TT(region[:, 16 + h, :], ps_o.t[:], rs.t[:], ALU.mult), reads=[ps_o.r, rs.r], writes=[reg_r[16 + h]])
            ppool.free(ps_o)
            fpool.free(rs)

        def run_interleaved(items, width, background=None):
            items = list(items)
            live = []
            while items or live or background is not None:
                if background is not None:
                    try:
                        next(background)
                    except StopIteration:
                        background = None
                while items and len(live) < width:
                    k = items[0][0]
                    if k in ("gla", "foxpre", "fox") and any(k == lk for lk, _ in live):
                        break
                    if k == "fox" and any(lk == "foxpre" for lk, _ in live):
                        break
                    live.append(items.pop(0))
                for it in list(live):
                    try:
                        next(it[1])
                    except StopIteration:
                        live.remove(it)

        def emit_mixers(l, tb, background=None):
            gla_pre(l, tb)
            items = [("foxpre", fox_pre(l, tb))]
            for h in range(4):
                items += [("ret", ret_head(l, tb, h)), ("gla", gla_head(l, tb, h)), ("fox", fox_head(l, tb, h))]
            run_interleaved(items, int(_os.environ.get("MK_WIDTH", "3")), background)

        def emit_gate(l, tb):
            smT, smT_r, modT, modT_r, lay, lay_r = smT2[l % 2], smT2_r[l % 2], modT2[l % 2], modT2_r[l % 2], lay2[l % 2], lay2_r[l % 2]
            sm0 = 0
            for n in range(8):
                wm = wload(l, "mg%d" % n)
                wb = wload(l, "br%d" % n)
                mgv = wm.t[:, 0:3072].rearrange("p (k b j) -> p k b j", k=KC, b=3)
                brv = wb.t[:, 0:1536].rearrange("p (b k j) -> p b k j", b=3, k=4)
                prods = []
                for b in range(3):
                    pg = ppool.alloc()
                    for kc in range(KC):
                        Sx.add(PE, MM(pg.t[:], mgv[:, kc, b, :], hT[:, kc, :], start=(kc == 0), stop=(kc == KC - 1)),
                               reads=[wm.r, hT_r[kc]], writes=[pg.r])
                    py = ppool.alloc()
                    for k4 in range(4):
                        ch = 8 + b * 4 + k4
                        Sx.add(PE, MM(py.t[:], brv[:, b, k4, :], region[:, ch, :], start=(k4 == 0), stop=(k4 == 3)),
                               reads=[wb.r, reg_r[ch]], writes=[py.r])
                    sg = fpool.alloc()
                    Sx.add(ACT, ACTF(sg.t[:], pg.t[:], AF.Sigmoid, bias=smT[:, sm0 + SM_BMG + b * 8 + n:sm0 + SM_BMG + b * 8 + n + 1]),
                           reads=[pg.r, smT_r], writes=[sg.r])
                    ppool.free(pg)
                    Sx.add(DVE, TT(sg.t[:], py.t[:], sg.t[:], ALU.mult), reads=[py.r, sg.r], writes=[sg.r])
                    ppool.free(py)
                    prods.append(sg)
                wpool.free(wm)
                wpool.free(wb)
                Sx.add(DVE, TT(prods[0].t[:], prods[0].t[:], prods[1].t[:], ALU.add), reads=[prods[0].r, prods[1].r], writes=[prods[0].r])
                Sx.add(DVE, TT(region[:, n, :], prods[0].t[:], prods[2].t[:], ALU.add), reads=[prods[0].r, prods[2].r], writes=[reg_r[n]])
                for p_ in prods:
                    fpool.free(p_)
            for ch in range(2):
                ws = wload(l, "wo%d" % ch)
                wv = ws.t[:, 0:4096].rearrange("p (k n) -> p k n", k=KC)
                for dl in range(4):
                    dc = ch * 4 + dl
                    pb = ppool.alloc()
                    for n in range(8):
                        Sx.add(PE, MM(pb.t[:], wv[:, n, dl * 128:(dl + 1) * 128], region[:, n, :], start=(n == 0), stop=(n == 7)),
                               reads=[ws.r, reg_r[n]], writes=[pb.r])
                    xs = xT[:, dc, tb * TB:(tb + 1) * TB]
                    Sx.add(DVE, STT(xs, pb.t[:], modT[:, 16 + dc:17 + dc], xs, ALU.mult, ALU.add),
                           reads=[pb.r, modT_r, xT_r[dc][tb]], writes=[xT_r[dc][tb]])
                    ppool.free(pb)
                wpool.free(ws)

        def emit_ffn(l, tb, between=None):
            smT, smT_r, modT, modT_r, lay, lay_r = smT2[l % 2], smT2_r[l % 2], modT2[l % 2], modT2_r[l % 2], lay2[l % 2], lay2_r[l % 2]
            sm0 = 0
            S3 = int(_os.environ.get("MK_SUB3", "99"))

            def c3(k):
                if k > S3:
                    Sx.muted = True
            for i in range(6):
                nf = 4 if i < 5 else 2
                wdt = 128 * nf
                wu = wload(l, "up_u%d" % i)
                wg = wload(l, "up_g%d" % i)
                wuv = wu.t[:, 0:8 * wdt].rearrange("p (k n) -> p k n", k=KC)
                wgv = wg.t[:, 0:8 * wdt].rearrange("p (k n) -> p k n", k=KC)
                for j in range(nf):
                    f = i * 4 + j
                    pu = ppool.alloc()
                    pg = ppool.alloc()
                    for kc in range(KC):
                        Sx.add(PE, MM(pu.t[:], wuv[:, kc, j * 128:(j + 1) * 128], hT[:, kc, :], start=(kc == 0), stop=(kc == KC - 1)),
                               reads=[wu.r, hT_r[kc]], writes=[pu.r])
                    for kc in range(KC):
                        Sx.add(PE, MM(pg.t[:], wgv[:, kc, j * 128:(j + 1) * 128], hT[:, kc, :], start=(kc == 0), stop=(kc == KC - 1)),
                               reads=[wg.r, hT_r[kc]], writes=[pg.r])
                    u_sb = fpool.alloc()
                    Sx.add(ACT, ACP(u_sb.t[:], pu.t[:]), reads=[pu.r], writes=[u_sb.r])
                    ppool.free(pu)
                    c0 = fpool.alloc()
                    wc = sm0 + SM_WCONV
                    w2 = smT[:, wc + 2 * NFC + f:wc + 2 * NFC + f + 1]
                    w1 = smT[:, wc + NFC + f:wc + NFC + f + 1]
                    w0 = smT[:, wc + f:wc + f + 1]
                    Sx.add(DVE, TS(c0.t[:], u_sb.t[:], w2, smT[:, sm0 + SM_BCONV + f:sm0 + SM_BCONV + f + 1], ALU.mult, ALU.add),
                           reads=[u_sb.r, smT_r], writes=[c0.r])
                    Sx.add(DVE, STT(c0.t[:, 1:TB], u_sb.t[:, 0:TB - 1], w1, c0.t[:, 1:TB], ALU.mult, ALU.add),
                           reads=[u_sb.r, smT_r, c0.r], writes=[c0.r])
                    Sx.add(DVE, STT(c0.t[:, 2:TB], u_sb.t[:, 0:TB - 2], w0, c0.t[:, 2:TB], ALU.mult, ALU.add),
                           reads=[u_sb.r, smT_r, c0.r], writes=[c0.r])
                    if tb > 0:
                        Sx.add(DVE, STT(c0.t[:, 0:1], ucarry[:, f, 1:2], w1, c0.t[:, 0:1], ALU.mult, ALU.add),
                               reads=[ucarry_r[f], smT_r, c0.r], writes=[c0.r])
                        Sx.add(DVE, STT(c0.t[:, 0:2], ucarry[:, f, 0:2], w0, c0.t[:, 0:2], ALU.mult, ALU.add),
                               reads=[ucarry_r[f], smT_r, c0.r], writes=[c0.r])
                    Sx.add(DVE, CP(ucarry[:, f, :], u_sb.t[:, TB - 2:TB]), reads=[u_sb.r], writes=[ucarry_r[f]])
                    fpool.free(u_sb)
                    Sx.add(ACT, ACTF(c0.t[:], c0.t[:], AF.Silu), reads=[c0.r], writes=[c0.r])
                    c3(9)
                    Sx.add(DVE, TT(region[:, f, :], pg.t[:], c0.t[:], ALU.mult), reads=[pg.r, c0.r], writes=[reg_r[f]])
                    ppool.free(pg)
                    fpool.free(c0)
                wpool.free(wu)
                wpool.free(wg)
            for ch in range(2):
                if ch == 1 and between is not None:
                    between()
                accs = [ppool.alloc() for _ in range(4)]
                fc0 = 0
                for g, nfc in enumerate((8, 8, 6)):
                    ws = wload(l, "dn%d_%d" % (ch, g))
                    wv = ws.t[:, 0:nfc * 512].rearrange("p (k n) -> p k n", k=nfc)
                    for k in range(nfc):
                        fc = fc0 + k
                        for dl in range(4):
                            Sx.add(PE, MM(accs[dl].t[:], wv[:, k, dl * 128:(dl + 1) * 128], region[:, fc, :],
                                          start=(fc == 0), stop=(fc == NFC - 1)),
                                   reads=[ws.r, reg_r[fc]], writes=[accs[dl].r])
                    wpool.free(ws)
                    fc0 += nfc
                for dl in range(4):
                    dc = ch * 4 + dl
                    xs = xT[:, dc, tb * TB:(tb + 1) * TB]
                    Sx.add(DVE, STT(xs, accs[dl].t[:], modT[:, 40 + dc:41 + dc], xs, ALU.mult, ALU.add),
                           reads=[accs[dl].r, modT_r, xT_r[dc][tb]], writes=[xT_r[dc][tb]])
                    ppool.free(accs[dl])
            Sx.muted = False

        def emit_out(tb):
            for t in range(tb * 4, tb * 4 + 4):
                for half in range(2):
                    pb = ppool.alloc()
                    for j in range(4):
                        c = half * 4 + j
                        Sx.add(PE, TR(pb.t[:, j * 128:(j + 1) * 128], xT[:, c, t * 128:(t + 1) * 128], identf),
                               reads=[xT_r[c][t // 4], constf_r], writes=[pb.r])
                    so = fpool.alloc()
                    eng = ACT if half == 0 else DVE
                    fn_ = ACP(so.t[:], pb.t[:]) if eng == ACT else CP(so.t[:], pb.t[:])
                    Sx.add(eng, fn_, reads=[pb.r], writes=[so.r])
                    ppool.free(pb)
                    op = Sx.dma(SP, "f%d" % so.idx, DMA(out_d[t * 128:(t + 1) * 128, half * 512:(half + 1) * 512], so.t[:]), reads=[so.r])
                    final_ops.append(op)
                    fpool.free(so)

        import os
        STOP = int(os.environ.get("MK_STOP", "99"))
        NTB_RUN = int(os.environ.get("MK_NTB", str(NTB)))
        emit_mod(0)
        emit_norm(0, 0, 1)
        for l in range(n_layers):
            for tb in range(NTB):
                last = (tb == NTB - 1)
                nxt_mod = mod_gen(l + 1) if (last and l + 1 < n_layers) else None
                emit_mixers(l, tb, background=nxt_mod)
                if l == n_layers - 1 and tb > 0:
                    emit_out(tb - 1)
                if last and l + 1 < n_layers:
                    emit_layer_small_loads(l + 1)
                emit_gate(l, tb)
                emit_norm(l, tb, 2)
                if not last:
                    nxt = (lambda l_=l, t_=tb + 1: emit_norm(l_, t_, 1))
                elif l + 1 < n_layers:
                    nxt = (lambda l_=l + 1: emit_norm(l_, 0, 1))
                else:
                    nxt = None
                emit_ffn(l, tb, between=nxt)

        emit_out(NTB - 1)
        print("sbuf bytes remaining:", nc.sbuf_bytes_remaining, "ops:", {e: len(Sx.q[e]) for e in ENGS})
        Sx.emit(final_wait_ops=final_ops)
    return nc


_CACHE = {}


def _small_params(norm1_g, norm2_g, b_ada, b_mg, b_conv, w_conv, ret_norm_g, gla_norm_g, q_norm_g, k_norm_g, layers):
    L = len(layers)
    sm = np.zeros((128, L * NSM), np.float32)
    for i, l in enumerate(layers):
        o = i * NSM
        sm[:, o + SM_N1:o + SM_N1 + 8] = norm1_g[l].reshape(8, 128).T
        sm[:, o + SM_N2:o + SM_N2 + 8] = norm2_g[l].reshape(8, 128).T
        sm[:, o + SM_BADA:o + SM_BADA + 48] = b_ada[l].reshape(48, 128).T
        sm[:, o + SM_BMG:o + SM_BMG + 24] = b_mg[l].reshape(24, 128).T
        sm[:, o + SM_BCONV:o + SM_BCONV + 22] = b_conv[l].reshape(22, 128).T
        sm[:, o + SM_WCONV:o + SM_WCONV + 66] = w_conv[l].reshape(66, 128).T
        sm[:, o + SM_RETG:o + SM_RETG + 4] = ret_norm_g[l].reshape(4, 128).T
        sm[:, o + SM_GLAG] = gla_norm_g[l]
        sm[:, o + SM_QG] = q_norm_g[l]
        sm[:, o + SM_KG] = k_norm_g[l]
    return sm


def _prep_common(inputs, layers):
    L = len(layers)
    idx = list(layers)
    wall, offs, sizes, total = host_pack_weights(inputs["w_ada"][idx], inputs["w_in"][idx], inputs["w_br"][idx],
                                                 inputs["w_mg"][idx], inputs["w_o"][idx], inputs["w_up"][idx], inputs["w_down"][idx])
    wsm = np.zeros((L, 128, KC, 32), np.float32)
    for i, l in enumerate(layers):
        wi = inputs["w_in"][l]
        wsm[i, :, :, 0:16] = wi[:, 3072:3088].reshape(8, 128, 16).transpose(1, 0, 2)
        wsm[i, :, :, 16:20] = wi[:, 5136:5140].reshape(8, 128, 4).transpose(1, 0, 2)
    wsm = wsm.reshape(L, 128, KC * 32)
    wa2 = np.ascontiguousarray(inputs["w_gla_a2"][idx])
    sm = _small_params(inputs["norm1_g"], inputs["norm2_g"], inputs["b_ada"], inputs["b_mg"], inputs["b_conv"], inputs["w_conv"],
                       inputs["ret_norm_g"], inputs["gla_norm_g"], inputs["q_norm_g"], inputs["k_norm_g"], layers)
    bc = np.zeros((128, L * NBC), np.float32)
    for i, l in enumerate(layers):
        bc[:, i * NBC:i * NBC + 256] = inputs["b_gla_a"][l][None, :]
        bc[:, i * NBC + 256:i * NBC + 260] = inputs["b_fox_f"][l][None, :]
    c = host_consts()
    constf = np.concatenate([c[n].reshape(128, -1) for n, _ in CONST_F32], axis=1).astype(np.float32)
    rope = np.stack([c["rope_cos"], c["rope_sin"]], axis=1).astype(np.float32)
    common = {"wall": wall, "wsm": wsm, "wa2": wa2, "smT": sm, "bcp": bc, "constf": constf, "rope": rope}
    return common, offs, sizes, total


def run_layers(x, c, inputs, layers, dbg=None, ncores=NB, trace=False):
    common, offs, sizes, total = _prep_common(inputs, layers)
    import os
    key = (len(layers), total, str(dbg), os.environ.get("MK_STOP"), os.environ.get("MK_NTB"))
    if key not in _CACHE:
        _CACHE[key] = build_program(len(layers), total, offs, sizes, dbg=dbg)
    nc = _CACHE[key]
    in_maps = []
    for b in range(ncores):
        m = dict(common)
        m["x"] = np.ascontiguousarray(x[b])
        m["cT"] = np.ascontiguousarray(c[b].reshape(8, 128).T)
        in_maps.append(m)
    res = run_bass_kernel_spmd(nc, in_maps, core_ids=list(range(ncores)), **({"trace": True} if trace else {}))
    return res


def kernel(**inputs):
    inputs = {k: np.asarray(v) for k, v in inputs.items()}
    x = np.ascontiguousarray(inputs["x"], dtype=np.float32)
    c = np.ascontiguousarray(inputs["c"], dtype=np.float32)
    res = run_layers(x, c, inputs, list(range(DEPTH)))
    out = np.stack([np.asarray(r["out"]) for r in res.results], axis=0)
    return out.astype(np.float32)
```
